# Optimizing a Trainium2 kernel written in Bass

```python
import math
import numpy as np
import jax
import jax.numpy as jnp
from jax import lax

D_MODEL = 1024
BATCH = 16
SEQ = 2048
DEPTH = 2

MIX_WIDTH = D_MODEL
SSM_WIDTH = D_MODEL // 4
NSA_WIDTH = D_MODEL // 2
LRU_WIDTH = D_MODEL // 4

S5_GROUP = 16
S5_GROUPS = SSM_WIDTH // S5_GROUP
S5_STATE = 64

HEAD_DIM = 64
NSA_Q_HEADS = NSA_WIDTH // HEAD_DIM
NSA_KV_HEADS = 2
NSA_GQA = NSA_Q_HEADS // NSA_KV_HEADS
NSA_KV_WIDTH = NSA_KV_HEADS * HEAD_DIM
CMP_LEN = 32
CMP_STRIDE = 16
SEL_LEN = 64
SEL_TOPK = 8
WINDOW = 256
Q_BLOCK = 64

LRU_HEADS = 4
LRU_HEAD_DIM = LRU_WIDTH // LRU_HEADS
CONV_WIDTH = 4
LRU_C = 8.0

REL_BUCKETS = 32
REL_MAX_DIST = 128

D_FF = ((8 * D_MODEL // 3 + 255) // 256) * 256

IN_WIDTH = SSM_WIDTH + NSA_WIDTH + 6 * NSA_KV_WIDTH + 3 * NSA_Q_HEADS + 2 * LRU_WIDTH

NEG_INF = -1e9
FORCE_BONUS = 1e4
RMS_EPS = 1e-6

kernel_name = 'hymba_s5_nsa_rglru_trunk'


def in_proj_splits():
    o1 = SSM_WIDTH
    o2 = o1 + NSA_WIDTH
    o3 = o2 + 6 * NSA_KV_WIDTH
    o4 = o3 + 3 * NSA_Q_HEADS
    o5 = o4 + LRU_WIDTH
    return [o1, o2, o3, o4, o5]


def rmsnorm(x, g):
    xf = x.astype(jnp.float32)
    y = xf * lax.rsqrt(jnp.mean(xf * xf, axis=-1, keepdims=True) + RMS_EPS)
    return (y * g.astype(jnp.float32)).astype(x.dtype)


def masked_softmax(s, mask):
    p = jax.nn.softmax(jnp.where(mask, s, NEG_INF), axis=-1)
    return jnp.where(mask, p, 0.0)


def t5_bucket(dist):
    n = jnp.maximum(dist, 0)
    max_exact = REL_BUCKETS // 2
    nf = jnp.maximum(n, 1).astype(jnp.float32)
    large = max_exact + (jnp.log(nf / max_exact) / math.log(REL_MAX_DIST / max_exact)
                         * (REL_BUCKETS - max_exact)).astype(jnp.int32)
    large = jnp.minimum(large, REL_BUCKETS - 1)
    return jnp.where(n < max_exact, n, large)


def rel_bias_dense(table, dist):
    b = table.astype(jnp.float32)[t5_bucket(dist)]
    b = jnp.moveaxis(b, -1, 0)
    return b.reshape(NSA_KV_HEADS, NSA_GQA, *dist.shape)


def rel_bias_selected(table, dist):
    tbl = table.astype(jnp.float32).reshape(REL_BUCKETS, NSA_KV_HEADS, NSA_GQA).transpose(1, 0, 2)
    b = jax.vmap(lambda tb, bk: tb[bk], in_axes=(0, 1), out_axes=1)(tbl, t5_bucket(dist))
    return jnp.moveaxis(b, -1, 2)


def s5_mixer(u, lam_re, lam_im, log_dt, b_re, b_im, c_re, c_im, d_skip, w_glu):
    bsz, seq, _ = u.shape
    uf = u.astype(jnp.float32).reshape(bsz, seq, S5_GROUPS, S5_GROUP)
    dt = jnp.exp(log_dt.astype(jnp.float32))[:, None]
    lr = lam_re.astype(jnp.float32)
    li = lam_im.astype(jnp.float32)
    mag = jnp.exp(lr * dt)
    ang = li * dt
    ab_re = mag * jnp.cos(ang)
    ab_im = mag * jnp.sin(ang)
    den = lr * lr + li * li
    f_re = ((ab_re - 1.0) * lr + ab_im * li) / den
    f_im = (ab_im * lr - (ab_re - 1.0) * li) / den
    br = b_re.astype(jnp.float32)
    bi = b_im.astype(jnp.float32)
    bb_re = f_re[..., None] * br - f_im[..., None] * bi
    bb_im = f_re[..., None] * bi + f_im[..., None] * br
    bu_re = jnp.einsum('bsgc,gpc->bsgp', uf, bb_re)
    bu_im = jnp.einsum('bsgc,gpc->bsgp', uf, bb_im)
    a_re = jnp.broadcast_to(ab_re, bu_re.shape)
    a_im = jnp.broadcast_to(ab_im, bu_im.shape)

    def combine(e1, e2):
        a1r, a1i, b1r, b1i = e1
        a2r, a2i, b2r, b2i = e2
        return (a1r * a2r - a1i * a2i,
                a1r * a2i + a1i * a2r,
                a2r * b1r - a2i * b1i + b2r,
                a2r * b1i + a2i * b1r + b2i)

    _, _, h_re, h_im = lax.associative_scan(combine, (a_re, a_im, bu_re, bu_im), axis=1)
    y = (jnp.einsum('bsgp,gcp->bsgc', h_re, c_re.astype(jnp.float32))
         - jnp.einsum('bsgp,gcp->bsgc', h_im, c_im.astype(jnp.float32)))
    y = y.reshape(bsz, seq, SSM_WIDTH) + d_skip.astype(jnp.float32) * u.astype(jnp.float32)
    y = jax.nn.gelu(y)
    gl = y @ w_glu.astype(jnp.float32)
    out = gl[..., :SSM_WIDTH] * jax.nn.sigmoid(gl[..., SSM_WIDTH:])
    return out.astype(u.dtype)


def nsa_compress(kv, pe, w1, w2):
    bsz, seq = kv.shape[:2]
    n_cmp = (seq - CMP_LEN) // CMP_STRIDE + 1
    idx = np.arange(n_cmp)[:, None] * CMP_STRIDE + np.arange(CMP_LEN)[None, :]
    blk = kv[:, idx] + pe.astype(jnp.float32)[:, None, :]
    flat = blk.transpose(0, 1, 3, 2, 4).reshape(bsz, n_cmp, NSA_KV_HEADS, CMP_LEN * HEAD_DIM)
    return jax.nn.gelu(flat @ w1.astype(jnp.float32)) @ w2.astype(jnp.float32)


def nsa_mixer(q, kv, gate_logits, rel_table, pe_k, w1_k, w2_k, pe_v, w1_v, w2_v):
    bsz, seq = q.shape[:2]
    n_cmp = (seq - CMP_LEN) // CMP_STRIDE + 1
    n_sel = seq // SEL_LEN
    n_top = min(SEL_TOPK, n_sel)
    n_qb = seq // Q_BLOCK
    scale = HEAD_DIM ** -0.5
    kv = kv.astype(jnp.float32).reshape(bsz, seq, 6, NSA_KV_HEADS, HEAD_DIM)
    k_c, v_c, k_s, v_s, k_w, v_w = (kv[:, :, i] for i in range(6))

    kc = nsa_compress(k_c, pe_k, w1_k, w2_k)
    vc = nsa_compress(v_c, pe_v, w1_v, w2_v)
    cmp_end = jnp.arange(n_cmp) * CMP_STRIDE + CMP_LEN - 1
    cs = np.arange(n_cmp) * CMP_STRIDE
    ss = np.arange(n_sel) * SEL_LEN
    overlap = jnp.asarray(((cs[:, None] < ss[None, :] + SEL_LEN)
                           & (ss[None, :] < cs[:, None] + CMP_LEN)).astype(np.float32))

    ks_blk = k_s.reshape(bsz, n_sel, SEL_LEN, NSA_KV_HEADS, HEAD_DIM).transpose(0, 3, 1, 2, 4)
    vs_blk = v_s.reshape(bsz, n_sel, SEL_LEN, NSA_KV_HEADS, HEAD_DIM).transpose(0, 3, 1, 2, 4)

    pad = ((0, 0), (WINDOW, 0), (0, 0), (0, 0))
    kw_pad = jnp.pad(k_w, pad)
    vw_pad = jnp.pad(v_w, pad)

    q_blocks = (q.astype(jnp.float32) * scale).reshape(
        bsz, n_qb, Q_BLOCK, NSA_KV_HEADS, NSA_GQA, HEAD_DIM).transpose(1, 0, 2, 3, 4, 5)
    g_blocks = jax.nn.sigmoid(gate_logits.astype(jnp.float32)).reshape(
        bsz, n_qb, Q_BLOCK, NSA_KV_HEADS, NSA_GQA, 3).transpose(1, 0, 2, 3, 4, 5)
    sel_off = jnp.arange(SEL_LEN)
    sel_ids = jnp.arange(n_sel)
    win_off = jnp.arange(WINDOW + Q_BLOCK)
    gather = jax.vmap(jax.vmap(lambda tbl, ix: tbl[ix]))

    def block(args):
        qi, qb, gb = args
        t = qi * Q_BLOCK + jnp.arange(Q_BLOCK)
        dist_c = t[:, None] - cmp_end[None, :]
        s_c = jnp.einsum('bqhgd,bnhd->bhgqn', qb, kc) + rel_bias_dense(rel_table, dist_c)
        p_c = masked_softmax(s_c, dist_c >= 0)
        o_c = jnp.einsum('bhgqn,bnhd->bqhgd', p_c, vc)
        imp = jnp.einsum('bhgqn,nj->bhqj', p_c, overlap)
        cur = (t // SEL_LEN)[:, None]
        forced = (sel_ids == 0) | (sel_ids == cur) | (sel_ids == cur - 1)
        avail = sel_ids * SEL_LEN <= t[:, None]
        score = jnp.where(avail, imp + FORCE_BONUS * forced, NEG_INF)
        top_val, top_idx = lax.top_k(score, n_top)
        top_ok = top_val > 0.5 * NEG_INF
        ksel = gather(ks_blk, top_idx).reshape(bsz, NSA_KV_HEADS, Q_BLOCK, n_top * SEL_LEN, HEAD_DIM)
        vsel = gather(vs_blk, top_idx).reshape(bsz, NSA_KV_HEADS, Q_BLOCK, n_top * SEL_LEN, HEAD_DIM)
        pos = top_idx[..., None] * SEL_LEN + sel_off
        dist_s = t[:, None, None] - pos
        mask_s = ((dist_s >= 0) & top_ok[..., None]).reshape(bsz, NSA_KV_HEADS, 1, Q_BLOCK, -1)
        bias_s = rel_bias_selected(rel_table, dist_s).reshape(
            bsz, NSA_KV_HEADS, NSA_GQA, Q_BLOCK, n_top * SEL_LEN)
        s_s = jnp.einsum('bqhgd,bhqmd->bhgqm', qb, ksel) + bias_s
        p_s = masked_softmax(s_s, mask_s)
        o_s = jnp.einsum('bhgqm,bhqmd->bqhgd', p_s, vsel)
        start = qi * Q_BLOCK
        kwin = lax.dynamic_slice_in_dim(kw_pad, start, WINDOW + Q_BLOCK, axis=1)
        vwin = lax.dynamic_slice_in_dim(vw_pad, start, WINDOW + Q_BLOCK, axis=1)
        kpos = start - WINDOW + win_off
        dist_w = t[:, None] - kpos[None, :]
        mask_w = (dist_w >= 0) & (dist_w < WINDOW) & (kpos[None, :] >= 0)
        s_w = jnp.einsum('bqhgd,bkhd->bhgqk', qb, kwin) + rel_bias_dense(rel_table, dist_w)
        p_w = masked_softmax(s_w, mask_w)
        o_w = jnp.einsum('bhgqk,bkhd->bqhgd', p_w, vwin)
        o = gb[..., 0:1] * o_c + gb[..., 1:2] * o_s + gb[..., 2:3] * o_w
        return o.reshape(bsz, Q_BLOCK, NSA_WIDTH)

    out = lax.map(block, (jnp.arange(n_qb), q_blocks, g_blocks))
    return out.transpose(1, 0, 2, 3).reshape(bsz, seq, NSA_WIDTH).astype(q.dtype)


def rglru_mixer(xb, gb, conv_w, conv_b, w_a, b_a, w_x, b_x, lam):
    bsz, seq, width = xb.shape
    xc = lax.conv_general_dilated(xb, conv_w[:, None, :], window_strides=(1,),
                                  padding=((CONV_WIDTH - 1, 0),),
                                  dimension_numbers=('NWC', 'WIO', 'NWC'),
                                  feature_group_count=width) + conv_b
    xf = xc.astype(jnp.float32)
    xh = xf.reshape(bsz, seq, LRU_HEADS, LRU_HEAD_DIM)
    gate_r = jax.nn.sigmoid(jnp.einsum('bshi,hij->bshj', xh, w_a.astype(jnp.float32)).reshape(
        bsz, seq, width) + b_a.astype(jnp.float32))
    gate_i = jax.nn.sigmoid(jnp.einsum('bshi,hij->bshj', xh, w_x.astype(jnp.float32)).reshape(
        bsz, seq, width) + b_x.astype(jnp.float32))
    log_a = -LRU_C * gate_r * jax.nn.softplus(-lam.astype(jnp.float32))
    a = jnp.exp(log_a)
    mult = jnp.sqrt(-jnp.expm1(2.0 * log_a))
    bt = mult * gate_i * xf

    def combine(e1, e2):
        a1, b1 = e1
        a2, b2 = e2
        return (a1 * a2, a2 * b1 + b2)

    _, h = lax.associative_scan(combine, (a, bt), axis=1)
    y = h * jax.nn.gelu(gb.astype(jnp.float32))
    return y.astype(xb.dtype)


def swiglu(h, w_gate, w_up, w_down):
    return (jax.nn.silu(h @ w_gate) * (h @ w_up)) @ w_down


def setup_inputs(seed: int = 0) -> dict:
    key = jax.random.key(seed)
    ks = iter(jax.random.split(key, 48))
    f32 = jnp.float32
    L = DEPTH

    def nrm(shape, scale):
        return jax.random.normal(next(ks), shape, f32) * scale

    x = nrm((BATCH, SEQ, D_MODEL), 1.0)
    rel_bias_table = nrm((REL_BUCKETS, NSA_Q_HEADS), 0.1)
    norm_mix = 1.0 + nrm((L, D_MODEL), 0.02)
    w_in = nrm((L, D_MODEL, IN_WIDTH), D_MODEL ** -0.5)
    w_out = nrm((L, MIX_WIDTH, D_MODEL), MIX_WIDTH ** -0.5)
    s5_lam_re = -0.5 + nrm((L, S5_GROUPS, S5_STATE), 0.01)
    s5_lam_im = math.pi * jnp.arange(S5_STATE, dtype=f32) + nrm((L, S5_GROUPS, S5_STATE), 0.01)
    s5_log_dt = jax.random.uniform(next(ks), (L, S5_GROUPS), f32,
                                   minval=math.log(1e-3), maxval=math.log(1e-1))
    s5_b_re = nrm((L, S5_GROUPS, S5_STATE, S5_GROUP), (2 * S5_GROUP) ** -0.5)
    s5_b_im = nrm((L, S5_GROUPS, S5_STATE, S5_GROUP), (2 * S5_GROUP) ** -0.5)
    s5_c_re = nrm((L, S5_GROUPS, S5_GROUP, S5_STATE), S5_STATE ** -0.5)
    s5_c_im = nrm((L, S5_GROUPS, S5_GROUP, S5_STATE), S5_STATE ** -0.5)
    s5_d = nrm((L, SSM_WIDTH), 1.0)
    s5_w_glu = nrm((L, SSM_WIDTH, 2 * SSM_WIDTH), SSM_WIDTH ** -0.5)
    nsa_pe_k = nrm((L, CMP_LEN, HEAD_DIM), 0.02)
    nsa_w1_k = nrm((L, CMP_LEN * HEAD_DIM, HEAD_DIM), (CMP_LEN * HEAD_DIM) ** -0.5)
    nsa_w2_k = nrm((L, HEAD_DIM, HEAD_DIM), HEAD_DIM ** -0.5)
    nsa_pe_v = nrm((L, CMP_LEN, HEAD_DIM), 0.02)
    nsa_w1_v = nrm((L, CMP_LEN * HEAD_DIM, HEAD_DIM), (CMP_LEN * HEAD_DIM) ** -0.5)
    nsa_w2_v = nrm((L, HEAD_DIM, HEAD_DIM), HEAD_DIM ** -0.5)
    lru_conv_w = nrm((L, CONV_WIDTH, LRU_WIDTH), CONV_WIDTH ** -0.5)
    lru_conv_b = nrm((L, LRU_WIDTH), 0.01)
    lru_w_a = nrm((L, LRU_HEADS, LRU_HEAD_DIM, LRU_HEAD_DIM), LRU_HEAD_DIM ** -0.5)
    lru_b_a = nrm((L, LRU_WIDTH), 0.01)
    lru_w_x = nrm((L, LRU_HEADS, LRU_HEAD_DIM, LRU_HEAD_DIM), LRU_HEAD_DIM ** -0.5)
    lru_b_x = nrm((L, LRU_WIDTH), 0.01)
    a_pow = jax.random.uniform(next(ks), (L, LRU_WIDTH), f32, minval=0.9, maxval=0.999)
    a0 = a_pow ** (1.0 / LRU_C)
    lru_lam = jnp.log(a0) - jnp.log1p(-a0)
    norm_ffn = 1.0 + nrm((L, D_MODEL), 0.02)
    w_gate = nrm((L, D_MODEL, D_FF), D_MODEL ** -0.5)
    w_up = nrm((L, D_MODEL, D_FF), D_MODEL ** -0.5)
    w_down = nrm((L, D_FF, D_MODEL), D_FF ** -0.5)
    norm_final = 1.0 + nrm((D_MODEL,), 0.02)
    return {'x': x, 'rel_bias_table': rel_bias_table, 'norm_mix': norm_mix, 'w_in': w_in,
            'w_out': w_out, 's5_lam_re': s5_lam_re, 's5_lam_im': s5_lam_im,
            's5_log_dt': s5_log_dt, 's5_b_re': s5_b_re, 's5_b_im': s5_b_im,
            's5_c_re': s5_c_re, 's5_c_im': s5_c_im, 's5_d': s5_d, 's5_w_glu': s5_w_glu,
            'nsa_pe_k': nsa_pe_k, 'nsa_w1_k': nsa_w1_k, 'nsa_w2_k': nsa_w2_k,
            'nsa_pe_v': nsa_pe_v, 'nsa_w1_v': nsa_w1_v, 'nsa_w2_v': nsa_w2_v,
            'lru_conv_w': lru_conv_w, 'lru_conv_b': lru_conv_b, 'lru_w_a': lru_w_a,
            'lru_b_a': lru_b_a, 'lru_w_x': lru_w_x, 'lru_b_x': lru_b_x, 'lru_lam': lru_lam,
            'norm_ffn': norm_ffn, 'w_gate': w_gate, 'w_up': w_up, 'w_down': w_down,
            'norm_final': norm_final}


def reference(x, rel_bias_table, norm_mix, w_in, w_out, s5_lam_re, s5_lam_im, s5_log_dt,
              s5_b_re, s5_b_im, s5_c_re, s5_c_im, s5_d, s5_w_glu, nsa_pe_k, nsa_w1_k,
              nsa_w2_k, nsa_pe_v, nsa_w1_v, nsa_w2_v, lru_conv_w, lru_conv_b, lru_w_a,
              lru_b_a, lru_w_x, lru_b_x, lru_lam, norm_ffn, w_gate, w_up, w_down, norm_final):
    h = x
    splits = in_proj_splits()
    for l in range(DEPTH):
        hn = rmsnorm(h, norm_mix[l])
        z = hn @ w_in[l]
        u_ssm, q, kv, gate_logits, lru_x, lru_g = jnp.split(z, splits, axis=-1)
        y_ssm = s5_mixer(u_ssm, s5_lam_re[l], s5_lam_im[l], s5_log_dt[l], s5_b_re[l],
                         s5_b_im[l], s5_c_re[l], s5_c_im[l], s5_d[l], s5_w_glu[l])
        y_nsa = nsa_mixer(q, kv, gate_logits, rel_bias_table, nsa_pe_k[l], nsa_w1_k[l],
                          nsa_w2_k[l], nsa_pe_v[l], nsa_w1_v[l], nsa_w2_v[l])
        y_lru = rglru_mixer(lru_x, lru_g, lru_conv_w[l], lru_conv_b[l], lru_w_a[l], lru_b_a[l],
                            lru_w_x[l], lru_b_x[l], lru_lam[l])
        mix = jnp.concatenate([y_ssm, y_nsa, y_lru], axis=-1)
        h = h + mix @ w_out[l]
        h = h + swiglu(rmsnorm(h, norm_ffn[l]), w_gate[l], w_up[l], w_down[l])
    return rmsnorm(h, norm_final)
```

```python
import math
import numpy as np
import concourse.bass as bass
import concourse.mybir as mybir
from concourse.bass_utils import run_bass_kernel_spmd
from contextlib import ExitStack

F32 = mybir.dt.float32
BF16 = mybir.dt.bfloat16
AF = mybir.ActivationFunctionType
ALU = mybir.AluOpType
AX = mybir.AxisListType

D = 1024
SEQ = 2048
NT = 16
DFF = 2816
NFT = 22
INW = 2072
NFM = 14
NTM = 280
NSEQ = 2
DEPTH = 2
EPS = 1e-6
NEG = -30000.0

ENGS = ("pe", "act", "dve", "pool", "sp")


class Buf:
    __slots__ = ("name", "last_w", "readers")

    def __init__(self, name=""):
        self.name = name
        self.last_w = None
        self.readers = []


class Op:
    __slots__ = ("eng", "emit", "waits", "sig", "pos", "dma", "sem", "val", "cnt", "prewait", "gsn", "raw")


class Sched:
    NDSEM = 12

    def __init__(self, nc):
        self.nc = nc
        self.ops = {e: [] for e in ENGS}
        self.known = {e: {f: -1 for f in ENGS} for e in ENGS}
        self.dma_waited = {e: set() for e in ENGS}
        self.dma_count = {e: 0 for e in ENGS}
        self.pending_dma = []
        self.out_dmas = []
        self.nbar = 0

    def op(self, eng, emit, reads=(), writes=(), dma=False, is_out=False):
        o = Op()
        o.eng = eng
        o.emit = emit
        o.sig = False
        o.dma = dma
        o.raw = False
        o.pos = len(self.ops[eng])
        o.waits = []
        o.prewait = None
        o.cnt = None
        deps = []
        for b in reads:
            if b.last_w is not None:
                deps.append(b.last_w)
        for b in writes:
            if b.last_w is not None:
                deps.append(b.last_w)
            deps.extend(b.readers)
        seen = set()
        for y in deps:
            if id(y) in seen:
                continue
            seen.add(id(y))
            self._add_wait(o, y)
        if dma:
            i = self.dma_count[eng]
            self.dma_count[eng] += 1
            o.sem = (eng, i % self.NDSEM)
            o.val = 16 * (i // self.NDSEM + 1)
            if i >= self.NDSEM:
                o.prewait = (o.sem, 16 * (i // self.NDSEM))
            self.pending_dma.append(o)
            if is_out:
                self.out_dmas.append(o)
        self.ops[eng].append(o)
        for b in writes:
            b.last_w = o
            b.readers = []
        for b in reads:
            if not dma:
                b.readers = [r for r in b.readers if r.dma or r.eng != eng]
            b.readers.append(o)
        return o

    def _add_wait(self, o, y):
        e = o.eng
        if y.dma:
            if id(y) in self.dma_waited[e]:
                return
            self.dma_waited[e].add(id(y))
            o.waits.append(y)
        else:
            f = y.eng
            if f == e and e == "pe":
                return
            if y.pos <= self.known[e][f]:
                return
            self.known[e][f] = y.pos
            y.sig = True
            o.waits.append(y)

    def barrier(self):
        lasts = []
        for f in ENGS:
            for y in reversed(self.ops[f]):
                if not y.dma and not y.raw:
                    lasts.append(y)
                    break
        o = Op()
        o.eng = "sp"; o.emit = None; o.sig = False; o.dma = False; o.raw = True
        o.pos = len(self.ops["sp"]); o.waits = []; o.prewait = None; o.cnt = None
        for y in lasts:
            if y.pos > self.known["sp"][y.eng]:
                self.known["sp"][y.eng] = y.pos
                y.sig = True
                o.waits.append(y)
        for y in self.pending_dma:
            if id(y) not in self.dma_waited["sp"]:
                o.waits.append(y)
        self.pending_dma = []
        self.nbar += 1
        o.val = self.nbar
        o.emit = "bar_sig"
        self.ops["sp"].append(o)
        for e in ENGS:
            if e == "sp":
                continue
            w = Op()
            w.eng = e; w.emit = "bar_wait"; w.sig = False; w.dma = False; w.raw = True
            w.pos = len(self.ops[e]); w.waits = []; w.prewait = None; w.cnt = None
            w.val = self.nbar
            self.ops[e].append(w)
        for e in ENGS:
            for f in ENGS:
                self.known[e][f] = len(self.ops[f]) - 1
            self.dma_waited[e] = set()

    def finish(self):
        o = Op()
        o.eng = "sp"; o.emit = "nop"; o.sig = False; o.dma = False; o.raw = True
        o.pos = len(self.ops["sp"]); o.waits = list(self.out_dmas) + [y for y in self.pending_dma]
        o.prewait = None; o.cnt = None; o.val = 0
        self.ops["sp"].append(o)

    def emit_all(self, es):
        nc = self.nc
        esem = {e: es.enter_context(nc.semaphore("es_" + e)) for e in ENGS}
        dsem = {}
        for e in ENGS:
            if self.dma_count[e] > 0:
                for i in range(min(self.NDSEM, self.dma_count[e])):
                    dsem[(e, i)] = es.enter_context(nc.semaphore("ds_%s_%d" % (e, i)))
        bsem = es.enter_context(nc.semaphore("barrier"))
        for e in ENGS:
            c = 0
            for o in self.ops[e]:
                if o.sig:
                    c += 1
                    o.cnt = c
        block = es.enter_context(nc.Block())
        decos = {"pe": block.tensor, "act": block.scalar, "dve": block.vector,
                 "pool": block.gpsimd, "sp": block.sync}
        stats = {}
        for e in ENGS:
            ops = self.ops[e]
            stats[e] = len(ops)
            if not ops:
                continue

            def body(eng, ops=ops):
                for o in ops:
                    for y in o.waits:
                        if y.dma:
                            eng.wait_ge(dsem[y.sem], y.val)
                        else:
                            eng.wait_ge(esem[y.eng], y.cnt)
                    if o.prewait is not None:
                        eng.wait_ge(dsem[o.prewait[0]], o.prewait[1])
                    if o.raw:
                        if o.emit == "bar_sig":
                            eng.nop().then_inc(bsem, 1)
                        elif o.emit == "bar_wait":
                            eng.wait_ge(bsem, o.val)
                        continue
                    ins = o.emit(eng)
                    if o.dma:
                        ins.then_inc(dsem[o.sem], 16)
                    elif o.sig:
                        ins.then_inc(esem[o.eng], 1)

            decos[e](body)
        return stats


def in_perm():
    o_u, o_q, o_kv, o_g, o_lx, o_lg = 0, 256, 768, 1536, 1560, 1816
    p = list(range(o_u, o_u + 256))
    for j in range(4):
        p += list(range(o_q + 64 * j, o_q + 64 * j + 64))
        p += list(range(o_q + 64 * (4 + j), o_q + 64 * (4 + j) + 64))
    for i in (0, 1, 2, 4):
        p += list(range(o_kv + 128 * i, o_kv + 128 * i + 128))
    p += list(range(o_lx, o_lx + 256))
    p += list(range(o_lg, o_lg + 256))
    for i in (3, 5):
        p += list(range(o_kv + 128 * i, o_kv + 128 * i + 128))
    p += list(range(o_g, o_g + 24))
    assert len(p) == INW
    return np.array(p)


class Ctx:
    pass


class _NC:
    def __init__(self, nc):
        self._nc = nc
        self._n = 0

    def __getattr__(self, k):
        return getattr(self._nc, k)

    def sbuf_tensor(self, name, *a, **kw):
        self._n += 1
        return self._nc.sbuf_tensor("%s_%d" % (name, self._n), *a, **kw)

    def psum_tensor(self, name, *a, **kw):
        self._n += 1
        return self._nc.psum_tensor("%s_%d" % (name, self._n), *a, **kw)


def bview(t, bufs):
    return t, bufs


def build_program(dbg=None):
    dbg = dbg or {}
    nc = _NC(bass.Bass("TRN2", target_bir_lowering=False))
    S = Sched(nc)
    C = Ctx()
    C.nc = nc
    C.S = S
    C.dbg = dbg
    C.bcast = lambda ap, n: ap.to_broadcast([128, n])

    def dump(name, ap, shape, dt, reads):
        if name not in dbg.get("dump", ()):
            return
        d = nc.dram_tensor("dump_" + name, list(shape), dt, kind="ExternalOutput").ap()
        S.op("sp", lambda q: q.dma_start(out=d, in_=ap), reads=reads, writes=[Buf()], dma=True, is_out=True)
    C.dump = dump

    def din(name, shape, dt=F32):
        return nc.dram_tensor(name, list(shape), dt, kind="ExternalInput").ap()

    def dscr(name, shape, dt=F32):
        kind = "ExternalOutput" if name in dbg.get("expose", ()) else "Internal"
        return nc.dram_tensor(name, list(shape), dt, kind=kind).ap()

    C.x = din("x", [NSEQ, SEQ, D])
    C.w_in = din("w_in", [DEPTH, D, INW])
    C.w_out = din("w_out", [DEPTH, D, D])
    C.w_gate = din("w_gate", [DEPTH, D, DFF])
    C.w_up = din("w_up", [DEPTH, D, DFF])
    C.w_down = din("w_down", [DEPTH, DFF, D])
    C.gains = din("gains", [128, 5, 8])
    C.gfin = din("gfin", [128, D])
    C.grep_in = din("grep", [128, 4, D])
    C.ident = din("ident", [128, 128])
    C.s5p = din("s5p", [DEPTH, 128, 8, 3])
    C.s5_bre = din("s5_bre", [DEPTH, 128, 8, 128])
    C.s5_bim = din("s5_bim", [DEPTH, 128, 8, 128])
    C.s5_cre = din("s5_cre", [DEPTH, 128, 8, 128])
    C.s5_cim = din("s5_cim", [DEPTH, 128, 8, 128])
    C.s5_dsk = din("s5_dsk", [DEPTH, 128, 2])
    C.s5_wglu = din("s5_wglu", [DEPTH, 256, 512])
    C.btab_raw = din("btab_raw", [128, 8, NBT])
    C.cvec = din("cvec", [128, 8])
    C.expand = din("expand", [32, NT, 128], BF16)
    C.vca_const = din("vca_const", [128, 33], BF16)
    C.cmask = din("cmask", [128, NT, 32])
    C.nsa_w1 = din("nsa_w1", [DEPTH, 2, 2, 128, 32, 64])
    C.nsa_peT = din("nsa_peT", [DEPTH, 128, 2, 32])
    C.nsa_w2pad = din("nsa_w2pad", [DEPTH, 64, 2, 128])
    C.nsa_w2v = din("nsa_w2v", [DEPTH, 64, 64])
    C.btab = dscr("btab", [128, 8, NBT], BF16)
    C.b_btab = Buf()
    C.lruv = din("lruv", [DEPTH, 128, 2, 8])
    C.lru_wa = din("lru_wa", [DEPTH, 128, 2, 128])
    C.lru_wx = din("lru_wx", [DEPTH, 128, 2, 128])
    C.out = nc.dram_tensor("out", [NSEQ, SEQ, D], F32, kind="ExternalOutput").ap()

    C.hres = dscr("hres", [NSEQ, SEQ, D])
    C.zu = dscr("zu", [256, SEQ])
    C.zlru = dscr("zlru", [512, SEQ])
    C.zqk = dscr("zqk", [8 * 128, SEQ], BF16)
    C.zv = dscr("zv", [SEQ, 2, 2, 64], BF16)
    C.zg = dscr("zg", [128, NT * 24])
    if "mixT_in" in dbg:
        C.mixT = din("mixT", [D, SEQ], BF16)
    else:
        C.mixT = dscr("mixT", [D, SEQ], BF16)
    C.hn2T = dscr("hn2T", [D, SEQ], BF16)
    C.b_hn2T = [Buf() for _ in range(NT)]
    C.b_hres = [[Buf("hres%d_%d" % (s, t)) for t in range(NT)] for s in range(NSEQ)]
    C.b_mixT = [Buf("mixT%d" % i) for i in range(8)]
    C.b_zu = [Buf() for _ in range(2)]
    C.b_zlru = [Buf() for _ in range(4)]
    C.b_zqk = [Buf() for _ in range(8)]
    C.b_zv = Buf()
    C.b_zg = Buf()
    C.b_out = Buf()

    with ExitStack() as es:
        C.identb = es.enter_context(nc.sbuf_tensor("identb", [128, 128], BF16))
        C.identf = es.enter_context(nc.sbuf_tensor("identf", [128, 128], F32))
        C.gn = es.enter_context(nc.sbuf_tensor("gn", [128, 5, 8], F32))
        C.gfin_sb = es.enter_context(nc.sbuf_tensor("gfin_sb", [128, D], F32))
        C.epsc = es.enter_context(nc.sbuf_tensor("epsc", [128, 1], F32))
        C.b_const = Buf("const")
        S.op("sp", lambda q: q.dma_start(out=C.identf[:], in_=C.ident[:, :]), writes=[C.b_const], dma=True)
        S.op("sp", lambda q: q.dma_start(out=C.gn[:], in_=C.gains[:, :, :]), writes=[C.b_const], dma=True)
        S.op("sp", lambda q: q.dma_start(out=C.gfin_sb[:], in_=C.gfin[:, :]), writes=[C.b_const], dma=True)
        S.op("dve", lambda v: v.tensor_copy(out=C.identb[:], in_=C.identf[:]), reads=[C.b_const], writes=[C.b_const])
        S.op("dve", lambda v: v.memset(C.epsc[:], EPS), writes=[C.b_const])
        S.barrier()
        if "NSA" in dbg.get("phases", ("NSA",)):
            phase_btab(C)
            S.barrier()

        phases = dbg.get("phases", ("A", "S5", "LRU", "NSA", "O1", "O2"))
        fns = {"A": lambda s, l, src: phase_A(C, s, l, src),
               "S5": lambda s, l, src: phase_S5(C, s, l),
               "LRU": lambda s, l, src: phase_LRU(C, s, l),
               "NSA": lambda s, l, src: phase_NSA(C, s, l),
               "O1": lambda s, l, src: phase_O1(C, s, l, src),
               "O2": lambda s, l, src: phase_O2(C, s, l, last=(l == DEPTH - 1))}
        for s in range(dbg.get("nseq", NSEQ)):
            for l in range(dbg.get("nlayer", DEPTH)):
                src = C.x if l == 0 else C.hres
                for ph in phases:
                    fns[ph](s, l, src)
                    S.barrier()
        S.finish()
        stats = S.emit_all(es)
    C.stats = stats
    return nc._nc, C


def rms_tile(C, S, grep, htile, b_h, sqjunk, ss, rstd, xn, b_tmp, b_xn):
    S.op("act", lambda a: a.activation(out=sqjunk[:], in_=htile, func=AF.Square, accum_out=ss[:]),
         reads=[b_h], writes=[b_tmp])
    S.op("act", lambda a: a.activation(out=rstd[:], in_=ss[:], func=AF.Sqrt, scale=1.0 / D, bias=C.epsc[:]),
         reads=[b_tmp, C.b_const], writes=[b_tmp])
    S.op("dve", lambda v: v.reciprocal(out=rstd[:], in_=rstd[:]), reads=[b_tmp], writes=[b_tmp])
    S.op("dve", lambda v: v.scalar_tensor_tensor(out=xn[:], in0=htile, scalar=rstd[:], in1=grep, op0=ALU.mult, op1=ALU.mult),
         reads=[b_h, b_tmp, C.b_gain], writes=[b_xn])


def norm_transpose_phase(C, S, es, nc, src_ap, s, b_src, hnT, b_hnT, ps_tr, b_ps_tr, grep):
    ht = [es.enter_context(nc.sbuf_tensor("nt_h%d" % i, [128, D], F32)) for i in range(2)]
    b_ht = [Buf() for _ in range(2)]
    sq = es.enter_context(nc.sbuf_tensor("nt_sq", [128, D], BF16))
    ss = [es.enter_context(nc.sbuf_tensor("nt_ss%d" % i, [128, 1], F32)) for i in range(2)]
    rs = [es.enter_context(nc.sbuf_tensor("nt_rs%d" % i, [128, 1], F32)) for i in range(2)]
    xn = [es.enter_context(nc.sbuf_tensor("nt_xn%d" % i, [128, D], BF16)) for i in range(2)]
    b_tmp = [Buf() for _ in range(2)]
    b_xn = [Buf() for _ in range(2)]
    for tt in range(NT):
        i = tt % 2
        S.op("sp", lambda q, tt=tt, i=i: q.dma_start(out=ht[i][:], in_=src_ap[s, tt * 128:(tt + 1) * 128, :]),
             reads=[b_src[tt]], writes=[b_ht[i]], dma=True)
        rms_tile(C, S, grep, ht[i][:], b_ht[i], sq, ss[i], rs[i], xn[i], b_tmp[i], b_xn[i])
        for ct in range(8):
            S.op("pe", lambda pe, ct=ct, i=i: pe.transpose(ps_tr[i][:, ct * 128:(ct + 1) * 128],
                                                             xn[i][:, ct * 128:(ct + 1) * 128], C.identb[:]),
                 reads=[b_xn[i], C.b_const], writes=[b_ps_tr[i]])
        eng = "act" if tt % 2 == 0 else "dve"
        if eng == "act":
            S.op("act", lambda a, tt=tt, i=i: a.copy(out=hnT[:, :, tt * 128:(tt + 1) * 128],
                                                     in_=ps_tr[i][:].rearrange("p (c t) -> p c t", c=8)),
                 reads=[b_ps_tr[i]], writes=[b_hnT[tt]])
        else:
            S.op("dve", lambda v, tt=tt, i=i: v.tensor_copy(out=hnT[:, :, tt * 128:(tt + 1) * 128],
                                                            in_=ps_tr[i][:].rearrange("p (c t) -> p c t", c=8)),
                 reads=[b_ps_tr[i]], writes=[b_hnT[tt]])


def load_cast_weight(C, S, nc, stage, b_stage, dst_ap_fn, src_ap_fn, n, gain_fn, b_dst, eng="act"):
    for k in range(n):
        i = k % len(stage)
        S.op("sp", lambda q, k=k, i=i: q.dma_start(out=stage[i][:], in_=src_ap_fn(k)),
             writes=[b_stage[i]], dma=True)
        g = gain_fn(k) if gain_fn is not None else None
        e = eng if isinstance(eng, str) else eng[k % len(eng)]
        if e == "act":
            if g is not None:
                S.op("act", lambda a, k=k, i=i, g=g: a.activation(out=dst_ap_fn(k), in_=stage[i][:], func=AF.Copy, scale=g),
                     reads=[b_stage[i], C.b_const], writes=[b_dst[k] if isinstance(b_dst, list) else b_dst])
            else:
                S.op("act", lambda a, k=k, i=i: a.copy(out=dst_ap_fn(k), in_=stage[i][:]),
                     reads=[b_stage[i]], writes=[b_dst[k] if isinstance(b_dst, list) else b_dst])
        else:
            if g is not None:
                S.op(e, lambda v, k=k, i=i, g=g: v.tensor_scalar(out=dst_ap_fn(k), in0=stage[i][:], scalar1=g, scalar2=None, op0=ALU.mult),
                     reads=[b_stage[i], C.b_const], writes=[b_dst[k] if isinstance(b_dst, list) else b_dst])
            else:
                S.op(e, lambda v, k=k, i=i: v.tensor_copy(out=dst_ap_fn(k), in_=stage[i][:]),
                     reads=[b_stage[i]], writes=[b_dst[k] if isinstance(b_dst, list) else b_dst])


def phase_A(C, s, l, src):
    nc, S = C.nc, C.S
    with ExitStack() as es:
        winb = es.enter_context(nc.sbuf_tensor("A_winb", [128, 8, INW], BF16))
        b_winb = [Buf() for _ in range(8)]
        stage = [es.enter_context(nc.sbuf_tensor("A_wst%d" % i, [128, INW], F32)) for i in range(2)]
        b_stage = [Buf() for _ in range(2)]
        hnT = es.enter_context(nc.sbuf_tensor("A_hnT", [128, 8, SEQ], BF16))
        b_hnT = [Buf() for _ in range(NT)]
        ps_tr = [es.enter_context(nc.psum_tensor("A_pstr%d" % i, [128, D], BF16)) for i in range(2)]
        b_ps_tr = [Buf() for _ in range(2)]
        ps_mm = [es.enter_context(nc.psum_tensor("A_psmm%d" % i, [128, 512], F32)) for i in range(4)]
        b_ps_mm = [Buf() for _ in range(4)]
        ev32 = [es.enter_context(nc.sbuf_tensor("A_ev32_%d" % i, [128, 512], F32)) for i in range(2)]
        ev16 = [es.enter_context(nc.sbuf_tensor("A_ev16_%d" % i, [128, 512], BF16)) for i in range(2)]
        b_ev32 = [Buf() for _ in range(2)]
        b_ev16 = [Buf() for _ in range(2)]
        vt = [es.enter_context(nc.sbuf_tensor("A_vt%d" % i, [128, 256], BF16)) for i in range(2)]
        gt_all = es.enter_context(nc.sbuf_tensor("A_gtall", [128, NT, 24], F32))
        b_vt = [Buf() for _ in range(2)]
        b_gt = [Buf() for _ in range(2)]

        w_src = C.w_in[l].rearrange("(c p) f -> p c f", p=128)
        load_cast_weight(C, S, nc, stage, b_stage,
                         lambda k: winb[:, k, :], lambda k: w_src[:, k, :], 8,
                         None, b_winb, eng=("act", "dve"))
        if C.dbg.get("A_steps", 9) < 2:
            return
        b_src = C.b_hres[s]
        gt_ = es.enter_context(nc.sbuf_tensor("A_gain", [128, D], F32))
        C.b_gain = Buf()
        S.op("sp", lambda q: q.dma_start(out=gt_[:], in_=C.grep_in[:, l, :]), writes=[C.b_gain], dma=True)
        norm_transpose_phase(C, S, es, nc, src, s, b_src, hnT, b_hnT, ps_tr, b_ps_tr, gt_[:])
        if C.dbg.get("A_steps", 9) < 3:
            return

        k32 = 0
        k16 = 0
        mmi = 0
        for ft in range(NFM):
            for nb in range(4):
                pi = mmi % 4
                mmi += 1
                for ct in range(8):
                    S.op("pe", lambda pe, ft=ft, nb=nb, ct=ct, pi=pi: pe.matmul(
                        ps_mm[pi][:], lhsT=winb[:, ct, ft * 128:(ft + 1) * 128],
                        rhs=hnT[:, ct, nb * 512:(nb + 1) * 512], start=(ct == 0), stop=(ct == 7)),
                        reads=[b_winb[ct]] + b_hnT[nb * 4:(nb + 1) * 4], writes=[b_ps_mm[pi]])
                cs = slice(nb * 512, (nb + 1) * 512)
                if ft < 2 or ft >= 10:
                    j = k32 % 2
                    k32 += 1
                    if ft < 2:
                        dst, bd = C.zu[ft * 128:(ft + 1) * 128, cs], C.b_zu[ft]
                    else:
                        dst, bd = C.zlru[(ft - 10) * 128:(ft - 9) * 128, cs], C.b_zlru[ft - 10]
                    S.op("dve", lambda v, j=j, pi=pi: v.tensor_copy(out=ev32[j][:], in_=ps_mm[pi][:]),
                         reads=[b_ps_mm[pi]], writes=[b_ev32[j]])
                    S.op("sp", lambda q, j=j, dst=dst: q.dma_start(out=dst, in_=ev32[j][:]),
                         reads=[b_ev32[j]], writes=[bd], dma=True)
                else:
                    j = k16 % 2
                    k16 += 1
                    sc = 0.125 if ft < 6 else 1.0
                    dst, bd = C.zqk[(ft - 2) * 128:(ft - 1) * 128, cs], C.b_zqk[ft - 2]
                    S.op("act", lambda a, j=j, pi=pi, sc=sc: a.mul(out=ev16[j][:], in_=ps_mm[pi][:], mul=sc),
                         reads=[b_ps_mm[pi]], writes=[b_ev16[j]])
                    S.op("sp", lambda q, j=j, dst=dst: q.dma_start(out=dst, in_=ev16[j][:]),
                         reads=[b_ev16[j]], writes=[bd], dma=True)
        if C.dbg.get("A_steps", 9) < 4:
            return
        for tt in range(NT):
            pi = mmi % 4
            mmi += 1
            j = tt % 2
            for ct in range(8):
                S.op("pe", lambda pe, tt=tt, ct=ct, pi=pi: pe.matmul(
                    ps_mm[pi][:, 0:NTM], lhsT=hnT[:, ct, tt * 128:(tt + 1) * 128],
                    rhs=winb[:, ct, NFM * 128:INW], start=(ct == 0), stop=(ct == 7)),
                    reads=[b_winb[ct], b_hnT[tt]], writes=[b_ps_mm[pi]])
            a4 = C.dbg.get("A4", "vg")
            if "v" in a4:
                S.op("dve", lambda v, j=j, pi=pi: v.tensor_copy(out=vt[j][:], in_=ps_mm[pi][:, 0:256]),
                     reads=[b_ps_mm[pi]], writes=[b_vt[j]])
                S.op("sp", lambda q, j=j, tt=tt: q.dma_start(
                    out=C.zv[tt * 128:(tt + 1) * 128].rearrange("t a h d -> t (a h d)"), in_=vt[j][:]),
                    reads=[b_vt[j]], writes=[C.b_zv], dma=True)
            if "g" in a4:
                S.op("act", lambda a, tt=tt, pi=pi: a.activation(out=gt_all[:, tt, :], in_=ps_mm[pi][:, 256:280], func=AF.Sigmoid),
                     reads=[b_ps_mm[pi]], writes=[b_gt[0]])
        S.op("sp", lambda q: q.dma_start(out=C.zg[:, :], in_=gt_all[:].rearrange("p t g -> p (t g)")),
             reads=[b_gt[0]], writes=[C.b_zg], dma=True)


GELU_C = 1.5957691216057308


def gelu_tanh(S, x, b_x, t, b_t, out, b_out, n=None):
    S.op("act", lambda a: a.activation(out=t, in_=x, func=AF.Square), reads=[b_x], writes=[b_t])
    S.op("dve", lambda v: v.tensor_scalar(out=t, in0=t, scalar1=0.044715, scalar2=1.0, op0=ALU.mult, op1=ALU.add),
         reads=[b_t], writes=[b_t])
    S.op("dve", lambda v: v.tensor_tensor(out=t, in0=t, in1=x, op=ALU.mult), reads=[b_t, b_x], writes=[b_t])
    S.op("act", lambda a: a.activation(out=t, in_=t, func=AF.Sigmoid, scale=GELU_C), reads=[b_t], writes=[b_t])
    S.op("dve", lambda v: v.tensor_tensor(out=out, in0=t, in1=x, op=ALU.mult), reads=[b_t, b_x], writes=[b_out])


def phase_LRU(C, s, l):
    nc, S = C.nc, C.S
    N = SEQ
    with ExitStack() as es:
        def T(name, shape, dt=F32):
            return es.enter_context(nc.sbuf_tensor("L_" + name, shape, dt))
        lv = T("lv", [128, 2, 8])
        wa = T("wa", [128, 2, 128]); wx = T("wx", [128, 2, 128])
        b_par = Buf()
        xpad = T("xpad", [128, N + 3]); xc = T("xc", [128, N]); r = T("r", [128, N]); gi = T("gi", [128, N])
        a = T("a", [128, N]); a2 = T("a2", [128, N]); bt = T("bt", [128, N]); h = T("h", [128, N])
        g = T("g", [128, N]); tg = T("tg", [128, N]); y = T("y", [128, N], BF16)
        sp = T("sp", [128, 4]); ones = T("ones", [128, 1])
        b = {k: Buf(k) for k in ("xpad", "xc", "r", "gi", "a", "a2", "bt", "h", "g", "tg", "y", "sp")}
        ps = [es.enter_context(nc.psum_tensor("L_ps%d" % i, [128, 512], F32)) for i in range(4)]
        b_ps = [Buf() for _ in range(4)]
        S.op("sp", lambda q: q.dma_start(out=lv[:], in_=C.lruv[l]), writes=[b_par], dma=True)
        S.op("sp", lambda q: q.dma_start(out=wa[:], in_=C.lru_wa[l]), writes=[b_par], dma=True)
        S.op("sp", lambda q: q.dma_start(out=wx[:], in_=C.lru_wx[l]), writes=[b_par], dma=True)
        S.op("dve", lambda v: v.memset(ones[:], 1.0), writes=[b_par])
        S.op("dve", lambda v: v.memset(xpad[:, 0:3], 0.0), writes=[b["xpad"]])
        mmi = 0
        for ct in range(2):
            S.op("sp", lambda q, ct=ct: q.dma_start(out=xpad[:, 3:3 + N], in_=C.zlru[ct * 128:(ct + 1) * 128, :]),
                 reads=[C.b_zlru[ct]], writes=[b["xpad"]], dma=True)
            S.op("sp", lambda q, ct=ct: q.dma_start(out=g[:], in_=C.zlru[256 + ct * 128:256 + (ct + 1) * 128, :]),
                 reads=[C.b_zlru[2 + ct]], writes=[b["g"]], dma=True)
            S.op("act", lambda e, ct=ct: e.activation(out=sp[:, 0:1], in_=lv[:, ct, 7:8], func=AF.Exp, scale=-1.0),
                 reads=[b_par], writes=[b["sp"]])
            S.op("act", lambda e: e.activation(out=sp[:, 1:2], in_=sp[:, 0:1], func=AF.Ln, scale=1.0, bias=ones[:]),
                 reads=[b["sp"], b_par], writes=[b["sp"]])
            S.op("dve", lambda v: v.tensor_scalar(out=sp[:, 2:3], in0=sp[:, 1:2], scalar1=-8.0, scalar2=None, op0=ALU.mult),
                 reads=[b["sp"]], writes=[b["sp"]])
            S.op("dve", lambda v: v.tensor_scalar(out=sp[:, 3:4], in0=sp[:, 1:2], scalar1=-16.0, scalar2=None, op0=ALU.mult),
                 reads=[b["sp"]], writes=[b["sp"]])
            S.op("dve", lambda v, ct=ct: v.tensor_scalar(out=xc[:], in0=xpad[:, 3:3 + N], scalar1=lv[:, ct, 3:4],
                                                         scalar2=lv[:, ct, 4:5], op0=ALU.mult, op1=ALU.add),
                 reads=[b["xpad"], b_par], writes=[b["xc"]])
            for i in range(3):
                S.op("dve", lambda v, ct=ct, i=i: v.scalar_tensor_tensor(out=xc[:], in0=xpad[:, i:i + N], scalar=lv[:, ct, i:i + 1],
                                                                         in1=xc[:], op0=ALU.mult, op1=ALU.add),
                     reads=[b["xpad"], b["xc"], b_par], writes=[b["xc"]])
            for (w, dst, bi, nm) in ((wa, r, 5, "r"), (wx, gi, 6, "gi")):
                for nb in range(4):
                    pi = mmi % 4
                    mmi += 1
                    ns = slice(nb * 512, (nb + 1) * 512)
                    S.op("pe", lambda pe, w=w, ct=ct, pi=pi, ns=ns: pe.matmul(ps[pi][:], lhsT=w[:, ct, :], rhs=xc[:, ns],
                                                                             start=True, stop=True),
                         reads=[b_par, b["xc"]], writes=[b_ps[pi]])
                    S.op("act", lambda e, dst=dst, pi=pi, ns=ns, ct=ct, bi=bi: e.activation(
                        out=dst[:, ns], in_=ps[pi][:], func=AF.Sigmoid, bias=lv[:, ct, bi:bi + 1]),
                        reads=[b_ps[pi], b_par], writes=[b[nm]])
            S.op("act", lambda e: e.activation(out=a[:], in_=r[:], func=AF.Exp, scale=sp[:, 2:3]),
                 reads=[b["r"], b["sp"]], writes=[b["a"]])
            S.op("act", lambda e: e.activation(out=a2[:], in_=r[:], func=AF.Exp, scale=sp[:, 3:4]),
                 reads=[b["r"], b["sp"]], writes=[b["a2"]])
            S.op("act", lambda e: e.activation(out=a2[:], in_=a2[:], func=AF.Sqrt, scale=-1.0, bias=ones[:]),
                 reads=[b["a2"], b_par], writes=[b["a2"]])
            S.op("dve", lambda v: v.tensor_tensor(out=bt[:], in0=gi[:], in1=xc[:], op=ALU.mult),
                 reads=[b["gi"], b["xc"]], writes=[b["bt"]])
            S.op("dve", lambda v: v.tensor_tensor(out=bt[:], in0=bt[:], in1=a2[:], op=ALU.mult),
                 reads=[b["bt"], b["a2"]], writes=[b["bt"]])
            S.op("dve", lambda v: v.tensor_tensor_scan(out=h[:], data0=a[:], data1=bt[:], initial=0.0,
                                                       op0=ALU.mult, op1=ALU.add),
                 reads=[b["a"], b["bt"]], writes=[b["h"]])
            gelu_tanh(S, g[:], b["g"], tg[:], b["tg"], tg[:], b["tg"])
            S.op("dve", lambda v: v.tensor_tensor(out=y[:], in0=h[:], in1=tg[:], op=ALU.mult),
                 reads=[b["h"], b["tg"]], writes=[b["y"]])
            S.op("sp", lambda q, ct=ct: q.dma_start(out=C.mixT[768 + ct * 128:768 + (ct + 1) * 128, :], in_=y[:]),
                 reads=[b["y"]], writes=[C.b_mixT[6 + ct]], dma=True)


TC = 512
NLEV = 9
MAGIC = 12582912.0
TWO_PI = 2.0 * math.pi
CW1 = 6.28125
CW2 = TWO_PI - 6.28125


def phase_S5(C, s, l):
    nc, S = C.nc, C.S
    with ExitStack() as es:
        def T(name, shape, dt=F32):
            return es.enter_context(nc.sbuf_tensor("S_" + name, shape, dt))
        par = T("par", [128, 8, 3]); b_par = Buf()
        Bre = T("Bre", [128, 8, 128]); Bim = T("Bim", [128, 8, 128]); b_B = Buf()
        Cre = T("Cre", [128, 8, 128]); Cim = T("Cim", [128, 8, 128]); b_Cf = Buf()
        Creb = T("Creb", [128, 8, 128], BF16); nCimb = T("nCimb", [128, 8, 128], BF16); b_Cb = Buf()
        dsk = T("dsk", [128, 2]); wgl = T("wgl", [128, 2, 512]); wglb = T("wglb", [128, 2, 512], BF16); b_w = Buf()
        sm = T("sm", [128, 24, 8]); b_sm = Buf()
        wre = T("wre", [128, NLEV + 1, 8]); wim = T("wim", [128, NLEV + 1, 8]); b_wp = Buf()
        hpi = T("hpi", [128, 1])
        Ec = T("Ec", [128, 8, TC]); Es = T("Es", [128, 8, TC]); b_E = [Buf() for _ in range(8)]
        Mre = T("Mre", [128, 8, 128], BF16); Mim = T("Mim", [128, 8, 128], BF16); Mt = T("Mt", [128, 128]); b_M = Buf(); b_Mt = Buf()
        WB = T("WB", [128, 2, 8, 128], BF16); b_WB = Buf()
        u32 = T("u32", [128, 2, SEQ]); ub = T("ub", [128, 2, SEQ], BF16); b_u = [Buf() for _ in range(2)]; b_ub = [Buf() for _ in range(2)]
        gr = T("gr", [128, 8, TC]); gim = T("gim", [128, 8, TC]); b_g = [Buf() for _ in range(8)]
        init = T("init", [128, 2, 8]); b_init = Buf()
        t1 = T("t1", [128, TC]); t2 = T("t2", [128, TC]); xr = T("xr", [128, TC]); xi = T("xi", [128, TC])
        b_t1 = Buf(); b_t2 = Buf(); b_xr = Buf(); b_xi = Buf()
        p1 = T("p1", [128, TC]); p2 = T("p2", [128, TC]); b_p1 = Buf(); b_p2 = Buf()
        hre = T("hre", [128, 8, TC], BF16); him = T("him", [128, 8, TC], BF16); b_h = [Buf() for _ in range(8)]
        yv = T("yv", [128, TC]); yt = T("yt", [128, TC]); b_yv = Buf(); b_yt = Buf()
        yg = T("yg", [128, 2, TC], BF16); b_yg = [Buf() for _ in range(2)]
        sgl = T("sgl", [128, TC]); b_sgl = Buf()
        og = [T("og%d" % i, [128, TC], BF16) for i in range(2)]; b_og = [Buf() for _ in range(2)]
        psX = [es.enter_context(nc.psum_tensor("S_psX%d" % i, [128, 512], F32)) for i in range(4)]
        b_psX = [Buf() for _ in range(4)]
        psY = [es.enter_context(nc.psum_tensor("S_psY%d" % i, [128, 512], F32)) for i in range(2)]
        b_psY = [Buf() for _ in range(2)]
        psT = es.enter_context(nc.psum_tensor("S_psT", [128, 8, 128], BF16)); b_psT = Buf()

        def dve(fn, reads, writes):
            S.op("dve", fn, reads=reads, writes=writes)

        def act(fn, reads, writes):
            S.op("act", fn, reads=reads, writes=writes)

        S.op("sp", lambda q: q.dma_start(out=par[:], in_=C.s5p[l]), writes=[b_par], dma=True)
        S.op("sp", lambda q: q.dma_start(out=Bre[:], in_=C.s5_bre[l]), writes=[b_B], dma=True)
        S.op("sp", lambda q: q.dma_start(out=Bim[:], in_=C.s5_bim[l]), writes=[b_B], dma=True)
        S.op("sp", lambda q: q.dma_start(out=Cre[:], in_=C.s5_cre[l]), writes=[b_Cf], dma=True)
        S.op("sp", lambda q: q.dma_start(out=Cim[:], in_=C.s5_cim[l]), writes=[b_Cf], dma=True)
        S.op("sp", lambda q: q.dma_start(out=dsk[:], in_=C.s5_dsk[l]), writes=[b_w], dma=True)
        S.op("sp", lambda q: q.dma_start(out=wgl[:], in_=C.s5_wglu[l].rearrange("(c p) f -> p c f", p=128)), writes=[b_w], dma=True)
        for ct in range(2):
            S.op("sp", lambda q, ct=ct: q.dma_start(out=u32[:, ct, :], in_=C.zu[ct * 128:(ct + 1) * 128, :]),
                 reads=[C.b_zu[ct]], writes=[b_u[ct]], dma=True)
            S.op("act", lambda v, ct=ct: v.copy(out=ub[:, ct, :], in_=u32[:, ct, :]), reads=[b_u[ct]], writes=[b_ub[ct]])
        act(lambda a: a.copy(out=wglb[:], in_=wgl[:]), [b_w], [b_w])
        act(lambda a: a.copy(out=Creb[:], in_=Cre[:]), [b_Cf], [b_Cb])
        act(lambda a: a.mul(out=nCimb[:], in_=Cim[:], mul=-1.0), [b_Cf], [b_Cb])
        dve(lambda v: v.memset(hpi[:], math.pi / 2.0), [], [b_sm])

        LR, LI, LDT = par[:, :, 0], par[:, :, 1], par[:, :, 2]
        sl = lambda i: sm[:, i, :]
        DT, MAG, ANG, K_, R_, AR, SN, CS, ABR, ABI, DEN, T1_, T2_, FRE, FIM = range(15)
        act(lambda a: a.activation(out=sl(DT), in_=LDT, func=AF.Exp), [b_par], [b_sm])
        dve(lambda v: v.tensor_tensor(out=sl(MAG), in0=LR, in1=sl(DT), op=ALU.mult), [b_par, b_sm], [b_sm])
        act(lambda a: a.activation(out=sl(MAG), in_=sl(MAG), func=AF.Exp), [b_sm], [b_sm])
        dve(lambda v: v.tensor_tensor(out=sl(ANG), in0=LI, in1=sl(DT), op=ALU.mult), [b_par, b_sm], [b_sm])
        dve(lambda v: v.tensor_scalar(out=sl(K_), in0=sl(ANG), scalar1=1.0 / TWO_PI, scalar2=MAGIC, op0=ALU.mult, op1=ALU.add), [b_sm], [b_sm])
        dve(lambda v: v.tensor_scalar(out=sl(K_), in0=sl(K_), scalar1=-MAGIC, scalar2=None, op0=ALU.add), [b_sm], [b_sm])
        dve(lambda v: v.scalar_tensor_tensor(out=sl(R_), in0=sl(K_), scalar=-CW1, in1=sl(ANG), op0=ALU.mult, op1=ALU.add), [b_sm], [b_sm])
        dve(lambda v: v.scalar_tensor_tensor(out=sl(R_), in0=sl(K_), scalar=-CW2, in1=sl(R_), op0=ALU.mult, op1=ALU.add), [b_sm], [b_sm])
        dve(lambda v: v.tensor_scalar(out=sl(R_), in0=sl(R_), scalar1=math.pi, scalar2=-math.pi, op0=ALU.min, op1=ALU.max), [b_sm], [b_sm])
        act(lambda a: a.activation(out=sl(AR), in_=sl(R_), func=AF.Abs), [b_sm], [b_sm])
        act(lambda a: a.activation(out=sl(SN), in_=sl(R_), func=AF.Sin), [b_sm], [b_sm])
        act(lambda a: a.activation(out=sl(CS), in_=sl(AR), func=AF.Sin, scale=-1.0, bias=hpi[:]), [b_sm], [b_sm])
        dve(lambda v: v.tensor_tensor(out=sl(ABR), in0=sl(MAG), in1=sl(CS), op=ALU.mult), [b_sm], [b_sm])
        dve(lambda v: v.tensor_tensor(out=sl(ABI), in0=sl(MAG), in1=sl(SN), op=ALU.mult), [b_sm], [b_sm])
        dve(lambda v: v.tensor_tensor(out=sl(DEN), in0=LR, in1=LR, op=ALU.mult), [b_par, b_sm], [b_sm])
        dve(lambda v: v.tensor_tensor(out=sl(T1_), in0=LI, in1=LI, op=ALU.mult), [b_par, b_sm], [b_sm])
        dve(lambda v: v.tensor_tensor(out=sl(DEN), in0=sl(DEN), in1=sl(T1_), op=ALU.add), [b_sm], [b_sm])
        dve(lambda v: v.reciprocal(out=sl(DEN), in_=sl(DEN)), [b_sm], [b_sm])
        dve(lambda v: v.tensor_scalar(out=sl(T1_), in0=sl(ABR), scalar1=-1.0, scalar2=None, op0=ALU.add), [b_sm], [b_sm])
        dve(lambda v: v.tensor_tensor(out=sl(FRE), in0=sl(T1_), in1=LR, op=ALU.mult), [b_par, b_sm], [b_sm])
        dve(lambda v: v.tensor_tensor(out=sl(T2_), in0=sl(ABI), in1=LI, op=ALU.mult), [b_par, b_sm], [b_sm])
        dve(lambda v: v.tensor_tensor(out=sl(FRE), in0=sl(FRE), in1=sl(T2_), op=ALU.add), [b_sm], [b_sm])
        dve(lambda v: v.tensor_tensor(out=sl(FRE), in0=sl(FRE), in1=sl(DEN), op=ALU.mult), [b_sm], [b_sm])
        dve(lambda v: v.tensor_tensor(out=sl(FIM), in0=sl(ABI), in1=LR, op=ALU.mult), [b_par, b_sm], [b_sm])
        dve(lambda v: v.tensor_tensor(out=sl(T2_), in0=sl(T1_), in1=LI, op=ALU.mult), [b_par, b_sm], [b_sm])
        dve(lambda v: v.tensor_tensor(out=sl(FIM), in0=sl(FIM), in1=sl(T2_), op=ALU.subtract), [b_sm], [b_sm])
        dve(lambda v: v.tensor_tensor(out=sl(FIM), in0=sl(FIM), in1=sl(DEN), op=ALU.mult), [b_sm], [b_sm])
        dve(lambda v: v.tensor_copy(out=wre[:, 0, :], in_=sl(CS)), [b_sm], [b_wp])
        dve(lambda v: v.tensor_copy(out=wim[:, 0, :], in_=sl(SN)), [b_sm], [b_wp])
        for k in range(NLEV):
            dve(lambda v, k=k: v.tensor_tensor(out=sl(T1_), in0=wre[:, k, :], in1=wre[:, k, :], op=ALU.mult), [b_wp, b_sm], [b_sm])
            dve(lambda v, k=k: v.tensor_tensor(out=sl(T2_), in0=wim[:, k, :], in1=wim[:, k, :], op=ALU.mult), [b_wp, b_sm], [b_sm])
            dve(lambda v, k=k: v.tensor_tensor(out=wre[:, k + 1, :], in0=sl(T1_), in1=sl(T2_), op=ALU.subtract), [b_sm, b_wp], [b_wp])
            dve(lambda v, k=k: v.tensor_tensor(out=sl(T1_), in0=wre[:, k, :], in1=wim[:, k, :], op=ALU.mult), [b_wp, b_sm], [b_sm])
            dve(lambda v, k=k: v.tensor_scalar(out=wim[:, k + 1, :], in0=sl(T1_), scalar1=2.0, scalar2=None, op0=ALU.mult), [b_sm, b_wp], [b_wp])
        tA = T("tA", [128, 8, TC // 2]); tB = T("tB", [128, 8, TC // 2]); b_tA = Buf(); b_tB = Buf()
        b_Eall = Buf()
        dve(lambda v: v.memset(Ec[:, :, 0:1], 1.0), [], [b_Eall])
        dve(lambda v: v.memset(Es[:, :, 0:1], 0.0), [], [b_Eall])
        for k in range(NLEV):
            n = 1 << k
            wrb = wre[:, k, :].unsqueeze(2).to_broadcast([128, 8, n])
            wib = wim[:, k, :].unsqueeze(2).to_broadcast([128, 8, n])
            dve(lambda v, n=n, wib=wib: v.tensor_tensor(out=tA[:, :, 0:n], in0=Es[:, :, 0:n], in1=wib, op=ALU.mult), [b_Eall, b_wp], [b_tA])
            dve(lambda v, n=n, wrb=wrb: v.tensor_tensor(out=tB[:, :, 0:n], in0=Ec[:, :, 0:n], in1=wrb, op=ALU.mult), [b_Eall, b_wp], [b_tB])
            dve(lambda v, n=n: v.tensor_tensor(out=Ec[:, :, n:2 * n], in0=tB[:, :, 0:n], in1=tA[:, :, 0:n], op=ALU.subtract), [b_tA, b_tB], [b_Eall])
            dve(lambda v, n=n, wib=wib: v.tensor_tensor(out=tA[:, :, 0:n], in0=Ec[:, :, 0:n], in1=wib, op=ALU.mult), [b_Eall, b_wp], [b_tA])
            dve(lambda v, n=n, wrb=wrb: v.tensor_tensor(out=tB[:, :, 0:n], in0=Es[:, :, 0:n], in1=wrb, op=ALU.mult), [b_Eall, b_wp], [b_tB])
            dve(lambda v, n=n: v.tensor_tensor(out=Es[:, :, n:2 * n], in0=tB[:, :, 0:n], in1=tA[:, :, 0:n], op=ALU.add), [b_tA, b_tB], [b_Eall])
        for st in range(8):
            dve(lambda v: v.engine_nop() if hasattr(v, "engine_nop") else v.memset(hpi[:], math.pi / 2.0), [b_Eall], [b_E[st]]) if False else None
            b_E[st].last_w = b_Eall.last_w
        frb = sm[:, FRE, :].unsqueeze(2).to_broadcast([128, 8, 128])
        fib = sm[:, FIM, :].unsqueeze(2).to_broadcast([128, 8, 128])
        MtA = T("MtA", [128, 8, 128]); MtB = T("MtB", [128, 8, 128])
        for (dst, A_, B_, sgn) in ((Mre, Bre, Bim, ALU.subtract), (Mim, Bim, Bre, ALU.add)):
            dve(lambda v, B_=B_: v.tensor_tensor(out=MtA[:], in0=B_[:], in1=fib, op=ALU.mult), [b_B, b_sm], [b_Mt])
            dve(lambda v, A_=A_: v.tensor_tensor(out=MtB[:], in0=A_[:], in1=frb, op=ALU.mult), [b_B, b_sm], [b_Mt])
            dve(lambda v, dst=dst, sgn=sgn: v.tensor_tensor(out=dst[:], in0=MtB[:], in1=MtA[:], op=sgn), [b_Mt], [b_M])
        for ri, Msrc in enumerate((Mre, Mim)):
            for st in range(8):
                S.op("pe", lambda pe, st=st, Msrc=Msrc: pe.transpose(psT[:, st, :], Msrc[:, st, :], C.identb[:]),
                     reads=[b_M, C.b_const], writes=[b_psT])
            act(lambda a, ri=ri: a.copy(out=WB[:, ri, :, :], in_=psT[:]), [b_psT], [b_WB])
        dve(lambda v: v.memset(init[:], 0.0), [], [b_init])

        xi_ = 0
        for k in range(SEQ // TC):
            ts = slice(k * TC, (k + 1) * TC)
            for st in range(8):
                ct = st // 4
                pr, pim = xi_ % 4, (xi_ + 1) % 4
                xi_ += 2
                S.op("pe", lambda pe, st=st, ct=ct, pr=pr, ts=ts: pe.matmul(psX[pr][:], lhsT=WB[:, 0, st, :], rhs=ub[:, ct, ts], start=True, stop=True),
                     reads=[b_WB, b_ub[ct]], writes=[b_psX[pr]])
                S.op("pe", lambda pe, st=st, ct=ct, pim=pim, ts=ts: pe.matmul(psX[pim][:], lhsT=WB[:, 1, st, :], rhs=ub[:, ct, ts], start=True, stop=True),
                     reads=[b_WB, b_ub[ct]], writes=[b_psX[pim]])
                dve(lambda v, st=st, pr=pr: v.tensor_tensor(out=t1[:], in0=psX[pr][:], in1=Ec[:, st, :], op=ALU.mult), [b_psX[pr], b_E[st]], [b_t1])
                dve(lambda v, st=st, pim=pim: v.tensor_tensor(out=t2[:], in0=psX[pim][:], in1=Es[:, st, :], op=ALU.mult), [b_psX[pim], b_E[st]], [b_t2])
                dve(lambda v: v.tensor_tensor(out=xr[:], in0=t1[:], in1=t2[:], op=ALU.add), [b_t1, b_t2], [b_xr])
                dve(lambda v, st=st, pim=pim: v.tensor_tensor(out=t1[:], in0=psX[pim][:], in1=Ec[:, st, :], op=ALU.mult), [b_psX[pim], b_E[st]], [b_t1])
                dve(lambda v, st=st, pr=pr: v.tensor_tensor(out=t2[:], in0=psX[pr][:], in1=Es[:, st, :], op=ALU.mult), [b_psX[pr], b_E[st]], [b_t2])
                dve(lambda v: v.tensor_tensor(out=xi[:], in0=t1[:], in1=t2[:], op=ALU.subtract), [b_t1, b_t2], [b_xi])
                dve(lambda v, st=st: v.tensor_tensor_scan(out=gr[:, st, :], data0=C.bcast(sm[:, MAG, st:st + 1], TC), data1=xr[:],
                                                          initial=init[:, 0, st:st + 1], op0=ALU.mult, op1=ALU.add),
                    [b_sm, b_xr, b_init], [b_g[st]])
                dve(lambda v, st=st: v.tensor_tensor_scan(out=gim[:, st, :], data0=C.bcast(sm[:, MAG, st:st + 1], TC), data1=xi[:],
                                                          initial=init[:, 1, st:st + 1], op0=ALU.mult, op1=ALU.add),
                    [b_sm, b_xi, b_init], [b_g[st]])
                pl = lambda fn, reads, writes: S.op("pool", fn, reads=reads, writes=writes)
                pl(lambda v, st=st: v.tensor_tensor(out=p1[:], in0=gr[:, st, :], in1=Ec[:, st, :], op=ALU.mult), [b_g[st], b_E[st]], [b_p1])
                pl(lambda v, st=st: v.tensor_tensor(out=p2[:], in0=gim[:, st, :], in1=Es[:, st, :], op=ALU.mult), [b_g[st], b_E[st]], [b_p2])
                pl(lambda v, st=st: v.tensor_tensor(out=hre[:, st, :], in0=p1[:], in1=p2[:], op=ALU.subtract), [b_p1, b_p2], [b_h[st]])
                pl(lambda v, st=st: v.tensor_tensor(out=p1[:], in0=gim[:, st, :], in1=Ec[:, st, :], op=ALU.mult), [b_g[st], b_E[st]], [b_p1])
                pl(lambda v, st=st: v.tensor_tensor(out=p2[:], in0=gr[:, st, :], in1=Es[:, st, :], op=ALU.mult), [b_g[st], b_E[st]], [b_p2])
                pl(lambda v, st=st: v.tensor_tensor(out=him[:, st, :], in0=p1[:], in1=p2[:], op=ALU.add), [b_p1, b_p2], [b_h[st]])
            if k + 1 < SEQ // TC:
                glr, gli = gr[:, :, TC - 1], gim[:, :, TC - 1]
                wr_, wi_ = wre[:, NLEV, :], wim[:, NLEV, :]
                dve(lambda v: v.tensor_tensor(out=sl(T1_), in0=glr, in1=wr_, op=ALU.mult), b_g + [b_wp, b_sm], [b_sm])
                dve(lambda v: v.tensor_tensor(out=sl(T2_), in0=gli, in1=wi_, op=ALU.mult), b_g + [b_wp, b_sm], [b_sm])
                dve(lambda v: v.tensor_tensor(out=init[:, 0, :], in0=sl(T1_), in1=sl(T2_), op=ALU.subtract), [b_sm, b_init], [b_init])
                dve(lambda v: v.tensor_tensor(out=sl(T1_), in0=gli, in1=wr_, op=ALU.mult), b_g + [b_wp, b_sm], [b_sm])
                dve(lambda v: v.tensor_tensor(out=sl(T2_), in0=glr, in1=wi_, op=ALU.mult), b_g + [b_wp, b_sm], [b_sm])
                dve(lambda v: v.tensor_tensor(out=init[:, 1, :], in0=sl(T1_), in1=sl(T2_), op=ALU.add), [b_sm, b_init], [b_init])
            for ct in range(2):
                py = ct
                for j in range(4):
                    st = ct * 4 + j
                    S.op("pe", lambda pe, st=st, py=py, j=j: pe.matmul(psY[py][:], lhsT=Creb[:, st, :], rhs=hre[:, st, :], start=(j == 0), stop=False),
                         reads=[b_Cb, b_h[st]], writes=[b_psY[py]])
                    S.op("pe", lambda pe, st=st, py=py, j=j: pe.matmul(psY[py][:], lhsT=nCimb[:, st, :], rhs=him[:, st, :], start=False, stop=(j == 3)),
                         reads=[b_Cb, b_h[st]], writes=[b_psY[py]])
                dve(lambda v, ct=ct, py=py, ts=ts: v.scalar_tensor_tensor(out=yv[:], in0=u32[:, ct, ts], scalar=dsk[:, ct:ct + 1], in1=psY[py][:],
                                                                         op0=ALU.mult, op1=ALU.add), [b_u[ct], b_w, b_psY[py]], [b_yv])
                gelu_tanh(S, yv[:], b_yv, yt[:], b_yt, yg[:, ct, :], b_yg[ct])
            for j in range(2):
                S.op("pe", lambda pe, j=j: pe.matmul(psY[0][:], lhsT=wglb[:, 0, j * 128:(j + 1) * 128], rhs=yg[:, 0, :], start=True, stop=False),
                     reads=[b_w] + b_yg, writes=[b_psY[0]])
                S.op("pe", lambda pe, j=j: pe.matmul(psY[0][:], lhsT=wglb[:, 1, j * 128:(j + 1) * 128], rhs=yg[:, 1, :], start=False, stop=True),
                     reads=[b_w] + b_yg, writes=[b_psY[0]])
                S.op("pe", lambda pe, j=j: pe.matmul(psY[1][:], lhsT=wglb[:, 0, 256 + j * 128:256 + (j + 1) * 128], rhs=yg[:, 0, :], start=True, stop=False),
                     reads=[b_w] + b_yg, writes=[b_psY[1]])
                S.op("pe", lambda pe, j=j: pe.matmul(psY[1][:], lhsT=wglb[:, 1, 256 + j * 128:256 + (j + 1) * 128], rhs=yg[:, 1, :], start=False, stop=True),
                     reads=[b_w] + b_yg, writes=[b_psY[1]])
                act(lambda a: a.activation(out=sgl[:], in_=psY[1][:], func=AF.Sigmoid), [b_psY[1]], [b_sgl])
                dve(lambda v, j=j: v.tensor_tensor(out=og[j][:], in0=psY[0][:], in1=sgl[:], op=ALU.mult), [b_psY[0], b_sgl], [b_og[j]])
                S.op("sp", lambda q, j=j, ts=ts: q.dma_start(out=C.mixT[j * 128:(j + 1) * 128, ts], in_=og[j][:]),
                     reads=[b_og[j]], writes=[C.b_mixT[j]], dma=True)


NBT = 2688
OFF_BS, OFF_BW, OFF_BC = 0, 256, 640


def phase_btab(C):
    nc, S = C.nc, C.S
    with ExitStack() as es:
        raw = [es.enter_context(nc.sbuf_tensor("BT_raw%d" % i, [128, NBT], F32)) for i in range(2)]
        cv = es.enter_context(nc.sbuf_tensor("BT_cv", [128, 8], F32))
        ob = [es.enter_context(nc.sbuf_tensor("BT_ob%d" % i, [128, NBT], BF16)) for i in range(2)]
        b_raw = [Buf() for _ in range(2)]; b_ob = [Buf() for _ in range(2)]; b_cv = Buf()
        S.op("sp", lambda q: q.dma_start(out=cv[:], in_=C.cvec[:, :]), writes=[b_cv], dma=True)
        for hq in range(8):
            i = hq % 2
            S.op("sp", lambda q, hq=hq, i=i: q.dma_start(out=raw[i][:], in_=C.btab_raw[:, hq, :]), writes=[b_raw[i]], dma=True)
            S.op("dve", lambda v, hq=hq, i=i: v.tensor_scalar(out=ob[i][:], in0=raw[i][:], scalar1=cv[:, hq:hq + 1], scalar2=None, op0=ALU.subtract),
                 reads=[b_raw[i], b_cv], writes=[b_ob[i]])
            S.op("sp", lambda q, hq=hq, i=i: q.dma_start(out=C.btab[:, hq, :], in_=ob[i][:]), reads=[b_ob[i]], writes=[C.b_btab], dma=True)


def phase_NSA(C, s, l):
    nc, S = C.nc, C.S
    with ExitStack() as es:
        def T(name, shape, dt=F32):
            return es.enter_context(nc.sbuf_tensor("N_" + name, shape, dt))
        qT = T("qT", [128, 4, SEQ], BF16); b_q = [Buf() for _ in range(4)]
        kcT = T("kcT", [128, SEQ + 32], BF16); vcT = T("vcT", [128, SEQ + 32], BF16); b_kc = Buf(); b_vc = Buf()
        ksT = T("ksT", [128, SEQ], BF16); kwT = T("kwT", [128, SEQ], BF16); b_ks = Buf(); b_kw = Buf()
        VS = T("VS", [128, NT, 2, 65], BF16); VW = T("VW", [128, NT, 2, 65], BF16); b_VS = Buf(); b_VW = Buf()
        btab = T("btab", [128, 8, NBT], BF16); b_bt = Buf()
        zg = T("zg", [128, NT, 24]); cmask = T("cmask", [128, NT, 32]); b_zg = Buf(); b_cm = Buf()
        expb = T("expb", [32, NT, 128], BF16); vcc = T("vcc", [128, 33], BF16); b_cst = Buf()
        acc = T("acc", [128, NT, 8, 64]); b_acc = [[Buf() for _ in range(8)] for _ in range(4)]
        imp = T("imp", [128, NT, 2, 32]); b_imp = [Buf() for _ in range(4)]
        negT = T("negT", [32, 2, SEQ], BF16); b_neg = [[Buf() for _ in range(2)] for _ in range(2)]
        PT = [T("PT%d" % i, [128, 512], BF16) for i in range(4)]; b_PT = [Buf() for _ in range(4)]
        w1f = T("w1f", [128, 32, 64]); w1b = [[T("w1b%d_%d" % (i, j), [128, 32, 64], BF16) for j in range(2)] for i in range(2)]; b_w1f = Buf(); b_w1 = Buf()
        pef = T("pef", [128, 2, 32]); peb = T("peb", [128, 2, 32], BF16)
        w2pf = T("w2pf", [64, 2, 128]); w2pb = T("w2pb", [64, 2, 128], BF16); w2vf = T("w2vf", [64, 64]); w2vb = T("w2vb", [64, 64], BF16)
        cst = T("cst", [64, 2]); xm = T("xm", [64, 256]); xt = T("xt", [64, 256]); hmid = [T("hmid%d" % i, [64, 256], BF16) for i in range(2)]
        b_cstv = Buf(); b_xm = Buf(); b_xt = Buf(); b_hm = [Buf() for _ in range(2)]
        kc2 = T("kc2", [128, 128], BF16); VC = T("VC", [128, 2, 97], BF16); b_kc2 = Buf(); b_VC = Buf()
        rd = T("rd", [128, 4]); wg = T("wg", [128, 4]); tmpo = T("tmpo", [128, 4, 64]); tmpi = T("tmpi", [128, 4, 32])
        b_rd = Buf(); b_wg = Buf(); b_tmpo = Buf(); b_tmpi = Buf()
        sc = T("sc", [128, 32]); top8 = T("top8", [128, 8]); nsb = T("nsb", [128, 32], BF16); b_sc = Buf(); b_top = Buf(); b_nsb = Buf()
        accb = T("accb", [128, 512], BF16); b_accb = Buf()
        ost = T("ost", [128, 4, SEQ], BF16); b_ost = [Buf() for _ in range(NT)]
        psS = [es.enter_context(nc.psum_tensor("N_psS%d" % i, [128, 512], F32)) for i in range(4)]; b_psS = [Buf() for _ in range(4)]
        psO = [es.enter_context(nc.psum_tensor("N_psO%d" % i, [128, 512], F32)) for i in range(2)]; b_psO = [Buf() for _ in range(2)]
        psM = es.enter_context(nc.psum_tensor("N_psM", [128, 512], F32)); b_psM = Buf()
        psT = es.enter_context(nc.psum_tensor("N_psT", [128, 1024], BF16)); b_psT = Buf()

        def dve(fn, reads, writes):
            S.op("dve", fn, reads=reads, writes=writes)

        def act(fn, reads, writes):
            S.op("act", fn, reads=reads, writes=writes)

        def pe(fn, reads, writes):
            S.op("pe", fn, reads=reads, writes=writes)

        def dma(fn, reads, writes):
            S.op("sp", fn, reads=reads, writes=writes, dma=True)

        zq = C.zqk.rearrange("(c p) t -> p c t", p=128)
        dve(lambda v: v.memset(kcT[:, SEQ:SEQ + 32], 0.0), [], [b_kc])
        dve(lambda v: v.memset(vcT[:, SEQ:SEQ + 32], 0.0), [], [b_vc])
        dma(lambda q: q.dma_start(out=kcT[:, 0:SEQ], in_=zq[:, 4, :]), [C.b_zqk[4]], [b_kc])
        dma(lambda q: q.dma_start(out=vcT[:, 0:SEQ], in_=zq[:, 5, :]), [C.b_zqk[5]], [b_vc])
        for wi in range(2):
            for hh in range(2):
                dma(lambda q, wi=wi, hh=hh: q.dma_start(out=w1f[:], in_=C.nsa_w1[l, wi, hh]), [], [b_w1f])
                act(lambda a, wi=wi, hh=hh: a.copy(out=w1b[wi][hh][:], in_=w1f[:]), [b_w1f], [b_w1])
        dma(lambda q: q.dma_start(out=pef[:], in_=C.nsa_peT[l]), [], [b_w1f])
        dma(lambda q: q.dma_start(out=w2pf[:], in_=C.nsa_w2pad[l]), [], [b_w1f])
        dma(lambda q: q.dma_start(out=w2vf[:], in_=C.nsa_w2v[l]), [], [b_w1f])
        dve(lambda v: v.tensor_copy(out=peb[:], in_=pef[:]), [b_w1f], [b_w1])
        dve(lambda v: v.tensor_copy(out=w2pb[:], in_=w2pf[:]), [b_w1f], [b_w1])
        dve(lambda v: v.tensor_copy(out=w2vb[:], in_=w2vf[:]), [b_w1f], [b_w1])
        dma(lambda q: q.dma_start(out=vcc[:], in_=C.vca_const[:, :]), [], [b_cst])
        dma(lambda q: q.dma_start(out=expb[:], in_=C.expand[:, :, :]), [], [b_cst])
        dma(lambda q: q.dma_start(out=cmask[:], in_=C.cmask[:, :, :]), [], [b_cm])
        dma(lambda q: q.dma_start(out=zg[:].rearrange("p t g -> p (t g)"), in_=C.zg[:, :]), [C.b_zg], [b_zg])
        for j in range(4):
            dma(lambda q, j=j: q.dma_start(out=qT[:, j, :], in_=zq[:, j, :]), [C.b_zqk[j]], [b_q[j]])
        dma(lambda q: q.dma_start(out=ksT[:], in_=zq[:, 6, :]), [C.b_zqk[6]], [b_ks])
        dma(lambda q: q.dma_start(out=kwT[:], in_=zq[:, 7, :]), [C.b_zqk[7]], [b_kw])
        zv = C.zv.rearrange("(t p) a h d -> p t a h d", p=128)
        for hh in range(2):
            dma(lambda q, hh=hh: q.dma_start(out=VS[:, :, hh, 0:64], in_=zv[:, :, 0, hh, :]), [C.b_zv], [b_VS])
            dma(lambda q, hh=hh: q.dma_start(out=VW[:, :, hh, 0:64], in_=zv[:, :, 1, hh, :]), [C.b_zv], [b_VW])
        dve(lambda v: v.memset(VS[:, :, :, 64:65], 1.0), [], [b_VS])
        dve(lambda v: v.memset(VW[:, :, :, 64:65], 1.0), [], [b_VW])
        for hq in range(8):
            dma(lambda q, hq=hq: q.dma_start(out=btab[:, hq, :], in_=C.btab[:, hq, :]), [C.b_btab], [b_bt])

        if C.dbg.get("nsa_stop", 99) <= 1:
            return
        kcS = T("kcS", [128, 32, 128], BF16); b_kcS = Buf()
        for wi, src, b_src in ((0, kcT, b_kc), (1, vcT, b_vc)):
            S.op("pool", lambda v, src=src: v.tensor_copy(out=kcS[:], in_=src[:, 0:SEQ].rearrange("p (n l) -> p l n", l=16)[:, :, :].unsqueeze(1)) if False else
                 v.tensor_copy(out=kcS[:, 0:16, :], in_=src[:, 0:SEQ].rearrange("p (n l) -> p l n", l=16)), reads=[b_src], writes=[b_kcS])
            S.op("pool", lambda v, src=src: v.tensor_copy(out=kcS[:, 16:32, :], in_=src[:, 16:SEQ + 16].rearrange("p (n l) -> p l n", l=16)),
                 reads=[b_src], writes=[b_kcS])
            for ll in range(32):
                pe(lambda p, wi=wi, ll=ll: p.matmul(psM[0:64, 256:257], lhsT=w1b[wi][0][:, ll, :], rhs=peb[:, wi, ll:ll + 1],
                                                   start=(ll == 0), stop=(ll == 31)), [b_w1], [b_psM])
            act(lambda a, wi=wi: a.copy(out=cst[:, wi:wi + 1], in_=psM[0:64, 256:257]), [b_psM], [b_cstv])
            if C.dbg.get("nsa_stop", 99) <= 1.2:
                return
            for hh in range(2):
                for ll in range(32):
                    pe(lambda p, wi=wi, hh=hh, ll=ll, src=src: p.matmul(
                        psM[0:64, hh * 128:(hh + 1) * 128], lhsT=w1b[wi][hh][:, ll, :],
                        rhs=kcS[:, ll, :], start=(ll == 0), stop=(ll == 31)),
                        [b_w1, b_kcS], [b_psM])
            if C.dbg.get("nsa_stop", 99) <= 1.5:
                return
            act(lambda a, wi=wi: a.activation(out=xm[:], in_=psM[0:64, 0:256], func=AF.Identity, bias=cst[:, wi:wi + 1]),
                [b_psM, b_cstv], [b_xm])
            if C.dbg.get("nsa_stop", 99) <= 1.6:
                return
            gelu_tanh(S, xm[:], b_xm, xt[:], b_xt, hmid[wi][:], b_hm[wi])
            if C.dbg.get("nsa_stop", 99) <= 1.7:
                return
            if wi == 0:
                pe(lambda p: p.matmul(psM[:, 384:512], lhsT=w2pb[:, 0, :], rhs=hmid[0][:, 0:128], start=True, stop=False), [b_w1, b_hm[0]], [b_psM])
                pe(lambda p: p.matmul(psM[:, 384:512], lhsT=w2pb[:, 1, :], rhs=hmid[0][:, 128:256], start=False, stop=True), [b_w1, b_hm[0]], [b_psM])
                act(lambda a: a.copy(out=kc2[:], in_=psM[:, 384:512]), [b_psM], [b_kc2])
            else:
                for hh in range(2):
                    pe(lambda p, hh=hh: p.matmul(psM[:, 384 + hh * 64:384 + (hh + 1) * 64], lhsT=hmid[1][:, hh * 128:(hh + 1) * 128], rhs=w2vb[:],
                                                start=True, stop=True), [b_w1, b_hm[1]], [b_psM])
                act(lambda a: a.copy(out=VC[:, :, 0:64], in_=psM[:, 384:512].rearrange("p (h d) -> p h d", h=2)), [b_psM], [b_VC])
        for hh in range(2):
            dve(lambda v, hh=hh: v.tensor_copy(out=VC[:, hh, 64:97], in_=vcc[:]), [b_cst], [b_VC])

        if C.dbg.get("nsa_stop", 99) <= 2:
            return
        C.dump("kc2", kc2[:], [128, 128], BF16, [b_kc2])
        C.dump("VC", VC[:].rearrange("p h c -> p (h c)"), [128, 194], BF16, [b_VC])
        si = [0]
        oi = [0]
        brs = C.dbg.get("nsa_branches", (0, 1, 2))
        if 0 not in brs:
            dve(lambda v: v.memset(acc[:], 0.0), [], [b_acc[nb_][hq_] for nb_ in range(4) for hq_ in range(8)])

        def evac(po, W, h, g, nb, branch, first):
            hq = 4 * h + g
            tts = slice(4 * nb, 4 * nb + 4)
            ps3 = psO[po][:, 0:4 * W].rearrange("p (t c) -> p t c", c=W)
            col = h * 12 + g * 3 + branch
            dve(lambda v: v.tensor_scalar(out=rd[:], in0=ps3[:, :, 64], scalar1=1e-30, scalar2=None, op0=ALU.max), [b_psO[po]], [b_rd])
            dve(lambda v: v.reciprocal(out=rd[:], in_=rd[:]), [b_rd], [b_rd])
            dve(lambda v: v.tensor_tensor(out=wg[:], in0=rd[:], in1=zg[:, tts, col], op=ALU.mult), [b_rd, b_zg], [b_wg])
            wgb = wg[:].unsqueeze(2).to_broadcast([128, 4, 64])
            if branch not in brs:
                pass
            elif first:
                dve(lambda v: v.tensor_tensor(out=acc[:, tts, hq, :], in0=ps3[:, :, 0:64], in1=wgb, op=ALU.mult),
                    [b_psO[po], b_wg], [b_acc[nb][hq]])
            else:
                dve(lambda v: v.tensor_tensor(out=tmpo[:], in0=ps3[:, :, 0:64], in1=wgb, op=ALU.mult), [b_psO[po], b_wg], [b_tmpo])
                dve(lambda v: v.tensor_tensor(out=acc[:, tts, hq, :], in0=acc[:, tts, hq, :], in1=tmpo[:], op=ALU.add),
                    [b_tmpo, b_acc[nb][hq]], [b_acc[nb][hq]])
            if branch == 0:
                rdb = rd[:].unsqueeze(2).to_broadcast([128, 4, 32])
                if g == 0:
                    dve(lambda v: v.tensor_tensor(out=imp[:, tts, h, :], in0=ps3[:, :, 65:97], in1=rdb, op=ALU.mult), [b_psO[po], b_rd], [b_imp[nb]])
                else:
                    dve(lambda v: v.tensor_tensor(out=tmpi[:], in0=ps3[:, :, 65:97], in1=rdb, op=ALU.mult), [b_psO[po], b_rd], [b_tmpi])
                    dve(lambda v: v.tensor_tensor(out=imp[:, tts, h, :], in0=imp[:, tts, h, :], in1=tmpi[:], op=ALU.add), [b_tmpi, b_imp[nb]], [b_imp[nb]])

        NB_ = len(psS)

        def run_pipeline(items, look=2):
            n = len(items)
            for j in range(min(look, n)):
                items[j][0](j % NB_)
            for j in range(n):
                if j + look < n:
                    items[j + look][0]((j + look) % NB_)
                items[j][1](j % NB_)

        items = []
        for h in range(2):
            hs = slice(64 * h, 64 * h + 64)
            for g in range(4):
                hq = 4 * h + g
                for nb in range(4):
                    ns = slice(nb * 512, (nb + 1) * 512)

                    def s1(i, g=g, ns=ns, hs=hs, hq=hq, nb=nb):
                        pe(lambda p: p.matmul(psS[i][:], lhsT=kc2[hs, :], rhs=qT[hs, g, ns], start=True, stop=False),
                           [b_kc2, b_q[g]], [b_psS[i]])
                        pe(lambda p: p.matmul(psS[i][:], lhsT=C.identb[:], rhs=btab[:, hq, OFF_BC + nb * 512:OFF_BC + (nb + 1) * 512],
                                              start=False, stop=True), [C.b_const, b_bt], [b_psS[i]])
                        act(lambda a: a.activation(out=PT[i][:], in_=psS[i][:], func=AF.Exp), [b_psS[i]], [b_PT[i]])

                    def s2(i, h=h, g=g, nb=nb):
                        po = oi[0] % 2; oi[0] += 1
                        for ql in range(4):
                            pe(lambda p, ql=ql: p.matmul(psO[po][:, ql * 97:(ql + 1) * 97], lhsT=PT[i][:, ql * 128:(ql + 1) * 128],
                                                         rhs=VC[:, h, :], start=(ql == 0), stop=(ql == 3), skip_group_check=True),
                               [b_PT[i], b_VC], [b_psO[po]])
                        evac(po, 97, h, g, nb, 0, True)
                    items.append((s1, s2))
        run_pipeline(items)
        for h in range(2):
            for tt in range(NT):
                dve(lambda v, tt=tt, h=h: v.tensor_tensor(out=sc[:], in0=imp[:, tt, h, :], in1=cmask[:, tt, :], op=ALU.add), [b_imp[tt // 4], b_cm], [b_sc])
                dve(lambda v: v.max(out=top8[:], in_=sc[:]), [b_sc], [b_top])
                dve(lambda v: v.tensor_scalar(out=nsb[:], in0=sc[:], scalar1=top8[:, 7:8], scalar2=NEG, op0=ALU.is_lt, op1=ALU.mult),
                    [b_sc, b_top], [b_nsb])
                pe(lambda p, tt=tt: p.transpose(psT[0:32, (tt % 8) * 128:(tt % 8 + 1) * 128], nsb[:], C.identb[:]), [b_nsb, C.b_const], [b_psT])
                if tt % 8 == 7:
                    act(lambda a, tt=tt, h=h: a.copy(out=negT[:, h, (tt // 8) * 1024:(tt // 8 + 1) * 1024], in_=psT[0:32, :]),
                        [b_psT], [b_neg[h][tt // 8]])
        items = []
        for h in range(2):
            hs = slice(64 * h, 64 * h + 64)
            for g in range(4):
                hq = 4 * h + g
                for nb in range(4):
                    for branch in (1, 2):
                        if branch not in brs:
                            continue
                        if branch == 1:
                            kts = list(range(0, 4 * nb + 4)); K_, b_K, V_, b_V = ksT, b_ks, VS, b_VS
                        else:
                            kts = list(range(max(0, 4 * nb - 2), 4 * nb + 4)); K_, b_K, V_, b_V = kwT, b_kw, VW, b_VW
                        blk = {"po": None}
                        for kt in kts:
                            t_lo = max(512 * nb, 128 * kt)
                            t_hi = 512 * (nb + 1)
                            if branch == 2:
                                t_hi = min(t_hi, 128 * kt + 384)
                            N = t_hi - t_lo

                            def s1(i, kt=kt, g=g, h=h, hq=hq, hs=hs, t_lo=t_lo, t_hi=t_hi, N=N, K_=K_, b_K=b_K, branch=branch, nb=nb):
                                pe(lambda p: p.matmul(psS[i][:, 0:N], lhsT=K_[hs, kt * 128:(kt + 1) * 128], rhs=qT[hs, g, t_lo:t_hi],
                                                      start=True, stop=False), [b_K, b_q[g]], [b_psS[i]])
                                if branch == 1:
                                    b_lo, b_hi = max(t_lo, 128 * kt), min(t_hi, 128 * kt + 256)
                                    if b_hi > b_lo:
                                        pe(lambda p: p.matmul(psS[i][:, b_lo - t_lo:b_hi - t_lo], lhsT=C.identb[:],
                                                              rhs=btab[:, hq, OFF_BS + b_lo - 128 * kt:OFF_BS + b_hi - 128 * kt],
                                                              start=False, stop=False), [C.b_const, b_bt], [b_psS[i]])
                                    pe(lambda p: p.matmul(psS[i][:, 0:N], lhsT=expb[:, kt, :], rhs=negT[:, h, t_lo:t_hi], start=False, stop=True),
                                       [b_cst, b_neg[h][nb // 2]], [b_psS[i]])
                                else:
                                    pe(lambda p: p.matmul(psS[i][:, 0:N], lhsT=C.identb[:],
                                                          rhs=btab[:, hq, OFF_BW + t_lo - 128 * kt:OFF_BW + t_hi - 128 * kt],
                                                          start=False, stop=True), [C.b_const, b_bt], [b_psS[i]])
                                act(lambda a: a.activation(out=PT[i][:, 0:N], in_=psS[i][:, 0:N], func=AF.Exp), [b_psS[i]], [b_PT[i]])

                            def s2(i, kt=kt, g=g, h=h, t_lo=t_lo, t_hi=t_hi, V_=V_, b_V=b_V, branch=branch, nb=nb, blk=blk,
                                   first=(kt == kts[0]), last=(kt == kts[-1])):
                                if first:
                                    blk["po"] = oi[0] % 2; oi[0] += 1
                                po = blk["po"]
                                fm = first
                                for qt in range(t_lo // 128, t_hi // 128):
                                    ql = qt - 4 * nb
                                    pe(lambda p, ql=ql, qt=qt, fm=fm: p.matmul(
                                        psO[po][:, ql * 65:(ql + 1) * 65], lhsT=PT[i][:, qt * 128 - t_lo:qt * 128 - t_lo + 128],
                                        rhs=V_[:, kt, h, :], start=fm, stop=(kt == qt), skip_group_check=True),
                                        [b_PT[i], b_V], [b_psO[po]])
                                    fm = False
                                if last:
                                    evac(po, 65, h, g, nb, branch, False)
                            items.append((s1, s2))
        run_pipeline(items)
        if C.dbg.get("nsa_stop", 99) <= 5:
            return
        for tt in range(NT):
            act(lambda a, tt=tt: a.copy(out=accb[:], in_=acc[:, tt, :, :].rearrange("p h d -> p (h d)")),
                [b_acc[tt // 4][hq] for hq in range(8)], [b_accb])
            for ft in range(4):
                pe(lambda p, ft=ft: p.transpose(psT[:, ft * 128:(ft + 1) * 128], accb[:, ft * 128:(ft + 1) * 128], C.identb[:]),
                   [b_accb, C.b_const], [b_psT])
            dve(lambda v, tt=tt: v.tensor_copy(out=ost[:, :, tt * 128:(tt + 1) * 128], in_=psT[:, 0:512].rearrange("p (f t) -> p f t", f=4)),
                [b_psT], [b_ost[tt]])
        for ft in range(4):
            dma(lambda q, ft=ft: q.dma_start(out=C.mixT[256 + ft * 128:256 + (ft + 1) * 128, :], in_=ost[:, ft, :]),
                b_ost, [C.b_mixT[2 + ft]])


def phase_O1(C, s, l, src):
    nc, S = C.nc, C.S
    with ExitStack() as es:
        wob = es.enter_context(nc.sbuf_tensor("O1_wob", [128, 8, D], BF16))
        b_wob = [Buf() for _ in range(8)]
        stage = [es.enter_context(nc.sbuf_tensor("O1_wst%d" % i, [128, D], F32)) for i in range(2)]
        b_stage = [Buf() for _ in range(2)]
        mixT = es.enter_context(nc.sbuf_tensor("O1_mixT", [128, 8, SEQ], BF16))
        b_mix = [Buf() for _ in range(8)]
        ps = [es.enter_context(nc.psum_tensor("O1_ps%d" % i, [128, 512], F32)) for i in range(4)]
        b_ps = [Buf() for _ in range(4)]
        ps_tr = [es.enter_context(nc.psum_tensor("O1_pstr%d" % i, [128, D], BF16)) for i in range(2)]
        b_ps_tr = [Buf() for _ in range(2)]
        ht = [es.enter_context(nc.sbuf_tensor("O1_h%d" % i, [128, D], F32)) for i in range(2)]
        b_ht = [Buf() for _ in range(2)]
        h1 = [es.enter_context(nc.sbuf_tensor("O1_h1%d" % i, [128, D], F32)) for i in range(2)]
        b_h1 = [Buf() for _ in range(2)]
        sq = es.enter_context(nc.sbuf_tensor("O1_sq", [128, D], BF16))
        ss = [es.enter_context(nc.sbuf_tensor("O1_ss%d" % i, [128, 1], F32)) for i in range(2)]
        rs = [es.enter_context(nc.sbuf_tensor("O1_rs%d" % i, [128, 1], F32)) for i in range(2)]
        xn = [es.enter_context(nc.sbuf_tensor("O1_xn%d" % i, [128, D], BF16)) for i in range(2)]
        b_tmp = [Buf() for _ in range(2)]
        b_xn = [Buf() for _ in range(2)]
        hn2 = [es.enter_context(nc.sbuf_tensor("O1_hn2_%d" % i, [128, 8, 128], BF16)) for i in range(2)]
        b_hn2 = [Buf() for _ in range(2)]

        gt_ = es.enter_context(nc.sbuf_tensor("O1_gain", [128, D], F32))
        C.b_gain = Buf()
        S.op("sp", lambda q: q.dma_start(out=gt_[:], in_=C.grep_in[:, 2 + l, :]), writes=[C.b_gain], dma=True)
        w_src = C.w_out[l].rearrange("(c p) f -> p c f", p=128)
        load_cast_weight(C, S, nc, stage, b_stage, lambda k: wob[:, k, :], lambda k: w_src[:, k, :], 8,
                         None, b_wob, eng=("act", "dve"))
        mix_src = C.mixT.rearrange("(c p) t -> p c t", p=128)
        for kt in range(8):
            S.op("sp", lambda q, kt=kt: q.dma_start(out=mixT[:, kt, :], in_=mix_src[:, kt, :]),
                 reads=[C.b_mixT[kt]], writes=[b_mix[kt]], dma=True)
        mmi = 0
        for tt in range(NT):
            i = tt % 2
            S.op("sp", lambda q, tt=tt, i=i: q.dma_start(out=ht[i][:], in_=src[s, tt * 128:(tt + 1) * 128, :]),
                 reads=[C.b_hres[s][tt]], writes=[b_ht[i]], dma=True)
            for half in range(2):
                pi = mmi % 4
                mmi += 1
                for kt in range(8):
                    S.op("pe", lambda pe, tt=tt, kt=kt, pi=pi, half=half: pe.matmul(
                        ps[pi][:], lhsT=mixT[:, kt, tt * 128:(tt + 1) * 128],
                        rhs=wob[:, kt, half * 512:(half + 1) * 512], start=(kt == 0), stop=(kt == 7)),
                        reads=[b_mix[kt], b_wob[kt]], writes=[b_ps[pi]])
                S.op("dve", lambda v, i=i, pi=pi, half=half: v.tensor_tensor(
                    out=h1[i][:, half * 512:(half + 1) * 512], in0=ps[pi][:], in1=ht[i][:, half * 512:(half + 1) * 512],
                    op=ALU.add), reads=[b_ps[pi], b_ht[i]], writes=[b_h1[i]])
            S.op("sp", lambda q, tt=tt, i=i: q.dma_start(out=C.hres[s, tt * 128:(tt + 1) * 128, :], in_=h1[i][:]),
                 reads=[b_h1[i]], writes=[C.b_hres[s][tt]], dma=True)
            rms_tile(C, S, gt_[:], h1[i][:], b_h1[i], sq, ss[i], rs[i], xn[i], b_tmp[i], b_xn[i])
            for ct in range(8):
                S.op("pe", lambda pe, ct=ct, i=i: pe.transpose(ps_tr[i][:, ct * 128:(ct + 1) * 128],
                                                                 xn[i][:, ct * 128:(ct + 1) * 128], C.identb[:]),
                     reads=[b_xn[i], C.b_const], writes=[b_ps_tr[i]])
            S.op("act", lambda a, i=i: a.copy(out=hn2[i][:], in_=ps_tr[i][:].rearrange("p (c t) -> p c t", c=8)),
                 reads=[b_ps_tr[i]], writes=[b_hn2[i]])
            S.op("sp", lambda q, tt=tt, i=i: q.dma_start(
                out=C.hn2T.rearrange("(c p) t -> p c t", p=128)[:, :, tt * 128:(tt + 1) * 128], in_=hn2[i][:]),
                reads=[b_hn2[i]], writes=[C.b_hn2T[tt]], dma=True)


def phase_O2(C, s, l, last):
    nc, S = C.nc, C.S
    HT = 1024
    with ExitStack() as es:
        hn2T = es.enter_context(nc.sbuf_tensor("O2_hn2T", [128, 8, HT], BF16))
        b_hn = [Buf() for _ in range(8)]
        actT = es.enter_context(nc.sbuf_tensor("O2_actT", [128, NFT, HT], BF16))
        b_act = [[Buf() for _ in range(2)] for _ in range(NFT)]
        wdb = es.enter_context(nc.sbuf_tensor("O2_wdb", [128, NFT, D], BF16))
        b_wdb = [Buf() for _ in range(NFT)]
        stage_d = [es.enter_context(nc.sbuf_tensor("O2_wdst%d" % i, [128, D], F32)) for i in range(2)]
        b_stage_d = [Buf() for _ in range(2)]
        stage_g = [es.enter_context(nc.sbuf_tensor("O2_wgst%d" % i, [128, 8, 128], F32)) for i in range(2)]
        stage_u = [es.enter_context(nc.sbuf_tensor("O2_wust%d" % i, [128, 8, 128], F32)) for i in range(2)]
        b_stage_g = [Buf() for _ in range(2)]
        b_stage_u = [Buf() for _ in range(2)]
        wgb = [es.enter_context(nc.sbuf_tensor("O2_wgb%d" % i, [128, 8, 128], BF16)) for i in range(2)]
        wub = [es.enter_context(nc.sbuf_tensor("O2_wub%d" % i, [128, 8, 128], BF16)) for i in range(2)]
        b_wgb = [Buf() for _ in range(2)]
        b_wub = [Buf() for _ in range(2)]
        psg = [es.enter_context(nc.psum_tensor("O2_psg%d" % i, [128, 512], F32)) for i in range(2)]
        psu = [es.enter_context(nc.psum_tensor("O2_psu%d" % i, [128, 512], F32)) for i in range(2)]
        psd = [es.enter_context(nc.psum_tensor("O2_psd%d" % i, [128, 512], F32)) for i in range(2)]
        b_psg = [Buf() for _ in range(2)]
        b_psu = [Buf() for _ in range(2)]
        b_psd = [Buf() for _ in range(2)]
        sg = [es.enter_context(nc.sbuf_tensor("O2_sg%d" % i, [128, 512], F32)) for i in range(2)]
        b_sg = [Buf() for _ in range(2)]
        h1 = [es.enter_context(nc.sbuf_tensor("O2_h1_%d" % i, [128, D], F32)) for i in range(2)]
        b_h1 = [Buf() for _ in range(2)]
        h2 = [es.enter_context(nc.sbuf_tensor("O2_h2_%d" % i, [128, D], F32)) for i in range(2)]
        b_h2 = [Buf() for _ in range(2)]
        if last:
            sq = es.enter_context(nc.sbuf_tensor("O2_sq", [128, D], BF16))
            ss = [es.enter_context(nc.sbuf_tensor("O2_ss%d" % i, [128, 1], F32)) for i in range(2)]
            rs = [es.enter_context(nc.sbuf_tensor("O2_rs%d" % i, [128, 1], F32)) for i in range(2)]
            yo = [es.enter_context(nc.sbuf_tensor("O2_yo%d" % i, [128, D], F32)) for i in range(2)]
            b_tmp = [Buf() for _ in range(2)]
            b_yo = [Buf() for _ in range(2)]

        wd_src = C.w_down[l].rearrange("(c p) f -> p c f", p=128)
        load_cast_weight(C, S, nc, stage_d, b_stage_d, lambda k: wdb[:, k, :], lambda k: wd_src[:, k, :], NFT,
                         None, b_wdb, eng=("dve", "pool"))
        wg_src = C.w_gate[l].rearrange("(c p) f -> p c f", p=128)
        wu_src = C.w_up[l].rearrange("(c p) f -> p c f", p=128)
        hn_src = C.hn2T.rearrange("(c p) t -> p c t", p=128)
        gi = 0
        for hf in range(2):
            t0 = hf * HT
            for ct in range(8):
                S.op("sp", lambda q, ct=ct, t0=t0: q.dma_start(out=hn2T[:, ct, :], in_=hn_src[:, ct, t0:t0 + HT]),
                     reads=C.b_hn2T[hf * 8:(hf + 1) * 8], writes=[b_hn[ct]], dma=True)
            for ft in range(NFT):
                j = gi % 2
                gi += 1
                fs = slice(ft * 128, (ft + 1) * 128)
                S.op("sp", lambda q, j=j, fs=fs: q.dma_start(out=stage_g[j][:], in_=wg_src[:, :, fs]),
                     writes=[b_stage_g[j]], dma=True)
                S.op("sp", lambda q, j=j, fs=fs: q.dma_start(out=stage_u[j][:], in_=wu_src[:, :, fs]),
                     writes=[b_stage_u[j]], dma=True)
                S.op("act", lambda a, j=j: a.copy(out=wgb[j][:], in_=stage_g[j][:]), reads=[b_stage_g[j]], writes=[b_wgb[j]])
                S.op("pool", lambda v, j=j: v.tensor_copy(out=wub[j][:], in_=stage_u[j][:]), reads=[b_stage_u[j]], writes=[b_wub[j]])
                for nb in range(2):
                    pi = (ft * 2 + nb) % 2
                    ns = slice(nb * 512, (nb + 1) * 512)
                    for ct in range(8):
                        S.op("pe", lambda pe, j=j, ct=ct, pi=pi, ns=ns: pe.matmul(
                            psg[pi][:], lhsT=wgb[j][:, ct, :], rhs=hn2T[:, ct, ns], start=(ct == 0), stop=(ct == 7)),
                            reads=[b_wgb[j], b_hn[ct]], writes=[b_psg[pi]])
                    for ct in range(8):
                        S.op("pe", lambda pe, j=j, ct=ct, pi=pi, ns=ns: pe.matmul(
                            psu[pi][:], lhsT=wub[j][:, ct, :], rhs=hn2T[:, ct, ns], start=(ct == 0), stop=(ct == 7)),
                            reads=[b_wub[j], b_hn[ct]], writes=[b_psu[pi]])
                    S.op("act", lambda a, pi=pi: a.activation(out=sg[pi][:], in_=psg[pi][:], func=AF.Silu),
                         reads=[b_psg[pi]], writes=[b_sg[pi]])
                    S.op("dve", lambda v, pi=pi, ft=ft, ns=ns: v.tensor_tensor(
                        out=actT[:, ft, ns], in0=psu[pi][:], in1=sg[pi][:], op=ALU.mult),
                        reads=[b_psu[pi], b_sg[pi]], writes=[b_act[ft][nb]])
            for tl in range(8):
                tt = hf * 8 + tl
                i = tt % 2
                S.op("sp", lambda q, tt=tt, i=i: q.dma_start(out=h1[i][:], in_=C.hres[s, tt * 128:(tt + 1) * 128, :]),
                     reads=[C.b_hres[s][tt]], writes=[b_h1[i]], dma=True)
                for half in range(2):
                    pi = (tt * 2 + half) % 2
                    for ft in range(NFT):
                        S.op("pe", lambda pe, ft=ft, tl=tl, pi=pi, half=half: pe.matmul(
                            psd[pi][:], lhsT=actT[:, ft, tl * 128:(tl + 1) * 128],
                            rhs=wdb[:, ft, half * 512:(half + 1) * 512], start=(ft == 0), stop=(ft == NFT - 1)),
                            reads=[b_act[ft][tl // 4], b_wdb[ft]], writes=[b_psd[pi]])
                    S.op("dve", lambda v, i=i, pi=pi, half=half: v.tensor_tensor(
                        out=h2[i][:, half * 512:(half + 1) * 512], in0=psd[pi][:],
                        in1=h1[i][:, half * 512:(half + 1) * 512], op=ALU.add),
                        reads=[b_psd[pi], b_h1[i]], writes=[b_h2[i]])
                if not last:
                    S.op("sp", lambda q, tt=tt, i=i: q.dma_start(out=C.hres[s, tt * 128:(tt + 1) * 128, :], in_=h2[i][:]),
                         reads=[b_h2[i]], writes=[C.b_hres[s][tt]], dma=True)
                else:
                    S.op("act", lambda a, i=i: a.activation(out=sq[:], in_=h2[i][:], func=AF.Square, accum_out=ss[i][:]),
                         reads=[b_h2[i]], writes=[b_tmp[i]])
                    S.op("act", lambda a, i=i: a.activation(out=rs[i][:], in_=ss[i][:], func=AF.Sqrt, scale=1.0 / D, bias=C.epsc[:]),
                         reads=[b_tmp[i], C.b_const], writes=[b_tmp[i]])
                    S.op("dve", lambda v, i=i: v.reciprocal(out=rs[i][:], in_=rs[i][:]), reads=[b_tmp[i]], writes=[b_tmp[i]])
                    S.op("dve", lambda v, i=i: v.scalar_tensor_tensor(
                        out=yo[i][:], in0=h2[i][:], scalar=rs[i][:], in1=C.gfin_sb[:], op0=ALU.mult, op1=ALU.mult),
                        reads=[b_h2[i], b_tmp[i], C.b_const], writes=[b_yo[i]])
                    S.op("sp", lambda q, tt=tt, i=i: q.dma_start(out=C.out[s, tt * 128:(tt + 1) * 128, :], in_=yo[i][:]),
                         reads=[b_yo[i]], writes=[C.b_out], dma=True, is_out=True)


def host_layout(inputs):
    f = lambda k: np.ascontiguousarray(np.asarray(inputs[k], dtype=np.float32))
    perm = in_perm()
    com = {}
    com["w_in"] = np.ascontiguousarray(f("w_in")[:, :, perm])
    com["w_out"] = f("w_out")
    com["w_gate"] = f("w_gate")
    com["w_up"] = f("w_up")
    com["w_down"] = f("w_down")
    vecs = [f("norm_mix")[0], f("norm_mix")[1], f("norm_ffn")[0], f("norm_ffn")[1], f("norm_final")]
    com["gains"] = np.ascontiguousarray(np.stack([v.reshape(8, 128).T for v in vecs], axis=1))
    com["gfin"] = np.ascontiguousarray(np.broadcast_to(f("norm_final")[None, :], (128, D)))
    com["grep"] = np.ascontiguousarray(np.broadcast_to(np.stack(vecs[:4], axis=0)[None, :, :], (128, 4, D)))
    com["ident"] = np.eye(128, dtype=np.float32)
    L = DEPTH
    com.update(nsa_host(inputs))
    s5p = np.zeros((L, 128, 8, 3), np.float32)
    pads = {k: np.zeros((L, 128, 8, 128), np.float32) for k in ("s5_bre", "s5_bim", "s5_cre", "s5_cim")}
    lre, lim, ldt = f("s5_lam_re"), f("s5_lam_im"), f("s5_log_dt")
    bre, bim, cre, cim = f("s5_b_re"), f("s5_b_im"), f("s5_c_re"), f("s5_c_im")
    for l in range(L):
        for g in range(16):
            st, po, co = g // 2, (g % 2) * 64, (g % 8) * 16
            s5p[l, po:po + 64, st, 0] = lre[l, g]
            s5p[l, po:po + 64, st, 1] = lim[l, g]
            s5p[l, po:po + 64, st, 2] = ldt[l, g]
            pads["s5_bre"][l, po:po + 64, st, co:co + 16] = bre[l, g]
            pads["s5_bim"][l, po:po + 64, st, co:co + 16] = bim[l, g]
            pads["s5_cre"][l, po:po + 64, st, co:co + 16] = cre[l, g].T
            pads["s5_cim"][l, po:po + 64, st, co:co + 16] = cim[l, g].T
    com["s5p"] = s5p
    com.update(pads)
    com["s5_dsk"] = np.ascontiguousarray(f("s5_d").reshape(L, 2, 128).transpose(0, 2, 1))
    com["s5_wglu"] = f("s5_w_glu")
    lruv = np.zeros((L, 128, 2, 8), np.float32)
    cw = f("lru_conv_w")
    for l in range(L):
        for ct in range(2):
            cs = slice(ct * 128, (ct + 1) * 128)
            for i in range(4):
                lruv[l, :, ct, i] = cw[l, i, cs]
            lruv[l, :, ct, 4] = f("lru_conv_b")[l, cs]
            lruv[l, :, ct, 5] = f("lru_b_a")[l, cs]
            lruv[l, :, ct, 6] = f("lru_b_x")[l, cs]
            lruv[l, :, ct, 7] = f("lru_lam")[l, cs]
    com["lruv"] = lruv
    for nm, key in (("lru_wa", "lru_w_a"), ("lru_wx", "lru_w_x")):
        w = f(key)
        bd = np.zeros((L, 128, 2, 128), np.float32)
        for l in range(L):
            for hh in range(4):
                ct, o = hh // 2, (hh % 2) * 64
                bd[l, o:o + 64, ct, o:o + 64] = w[l, hh]
        com[nm] = bd
    return com


def t5_bucket_np(dist):
    n = np.maximum(dist, 0)
    nf = np.maximum(n, 1).astype(np.float32)
    large = 16 + (np.log(nf / np.float32(16)) / np.float32(math.log(128 / 16)) * np.float32(16)).astype(np.int32)
    large = np.minimum(large, 31)
    return np.where(n < 16, n, large)


def nsa_host(inputs):
    import ml_dtypes
    f = lambda k: np.ascontiguousarray(np.asarray(inputs[k], dtype=np.float32))
    tbl = f("rel_bias_table")
    out = {}
    k = np.arange(128)[:, None]
    c = np.arange(256)[None, :]
    d_s = c - k
    c = np.arange(384)[None, :]
    d_w = c - k
    t = np.arange(SEQ)[None, :]
    d_c = t - (16 * k + 31)
    raw = np.zeros((128, 8, NBT), np.float32)
    for hq in range(8):
        col = tbl[:, hq]
        raw[:, hq, OFF_BS:OFF_BS + 256] = np.where(d_s >= 0, col[t5_bucket_np(d_s)], np.float32(NEG))
        raw[:, hq, OFF_BW:OFF_BW + 384] = np.where((d_w >= 0) & (d_w < 256), col[t5_bucket_np(d_w)], np.float32(NEG))
        raw[:, hq, OFF_BC:OFF_BC + SEQ] = np.where((d_c >= 0) & (k < 127), col[t5_bucket_np(d_c)], np.float32(NEG))
    out["btab_raw"] = raw
    out["cvec"] = np.ascontiguousarray(np.broadcast_to(tbl[31][None, :], (128, 8)))
    ex = np.zeros((32, NT, 128), np.float32)
    for kt in range(NT):
        for kk in range(128):
            ex[(kt * 128 + kk) // 64, kt, kk] = 1.0
    out["expand"] = ex.astype(ml_dtypes.bfloat16)
    cs = np.arange(128) * 16
    ss = np.arange(32) * 64
    ov = ((cs[:, None] < ss[None, :] + 64) & (ss[None, :] < cs[:, None] + 32)).astype(np.float32)
    ov[127] = 0.0
    out["vca_const"] = np.concatenate([np.ones((128, 1), np.float32), ov], axis=1).astype(ml_dtypes.bfloat16)
    tt = np.arange(SEQ)
    cur = (tt // 64)[:, None]
    ids = np.arange(32)[None, :]
    forced = (ids == 0) | (ids == cur) | (ids == cur - 1)
    avail = ids * 64 <= tt[:, None]
    cm = np.where(avail, np.where(forced, 1e4, 0.0), -1e9).astype(np.float32)
    out["cmask"] = np.ascontiguousarray(cm.reshape(NT, 128, 32).transpose(1, 0, 2))
    L = DEPTH
    w1 = np.stack([f("nsa_w1_k"), f("nsa_w1_v")], axis=1)
    w1r = w1.reshape(L, 2, 32, 64, 64).transpose(0, 1, 3, 2, 4)
    w1z = np.zeros((L, 2, 2, 128, 32, 64), np.float32)
    w1z[:, :, 0, 0:64] = w1r
    w1z[:, :, 1, 64:128] = w1r
    out["nsa_w1"] = w1z
    pe = np.stack([f("nsa_pe_k"), f("nsa_pe_v")], axis=1)
    peT = pe.transpose(0, 3, 1, 2)
    out["nsa_peT"] = np.ascontiguousarray(np.concatenate([peT, np.zeros_like(peT)], axis=1))
    w2k = f("nsa_w2_k")
    w2p = np.zeros((L, 64, 2, 128), np.float32)
    w2p[:, :, 0, 0:64] = w2k
    w2p[:, :, 1, 64:128] = w2k
    out["nsa_w2pad"] = w2p
    out["nsa_w2v"] = f("nsa_w2_v")
    return out


_CACHE = {}


def kernel(**inputs):
    x = np.ascontiguousarray(np.asarray(inputs["x"], dtype=np.float32))
    com = host_layout(inputs)
    if "nc" not in _CACHE:
        _CACHE["nc"] = build_program()[0]
    nc = _CACHE["nc"]
    in_maps = []
    for c in range(8):
        m = dict(com)
        m["x"] = np.ascontiguousarray(x[c * NSEQ:(c + 1) * NSEQ])
        in_maps.append(m)
    res = run_bass_kernel_spmd(nc, in_maps, core_ids=list(range(8)))
    return np.concatenate([r["out"] for r in res.results], axis=0).astype(np.float32)
```

```python
import math
import numpy as np
import concourse.bass as bass
import concourse.mybir as mybir
from concourse.bass_utils import run_bass_kernel_spmd
from contextlib import ExitStack

F32 = mybir.dt.float32
BF16 = mybir.dt.bfloat16
AF = mybir.ActivationFunctionType
ALU = mybir.AluOpType
AX = mybir.AxisListType

D = 1024
SEQ = 2048
NT = 16
DFF = 2816
NFT = 22
INW = 2072
NFM = 14
NTM = 280
NSEQ = 2
DEPTH = 2
EPS = 1e-6
NEG = -30000.0

ENGS = ("pe", "act", "dve", "pool", "sp")


class Buf:
    __slots__ = ("name", "last_w", "readers")

    def __init__(self, name=""):
        self.name = name
        self.last_w = None
        self.readers = []


class Op:
    __slots__ = ("eng", "emit", "waits", "sig", "pos", "dma", "sem", "val", "cnt", "prewait", "gsn", "raw")


class Sched:
    NDSEM = 12

    def __init__(self, nc):
        self.nc = nc
        self.ops = {e: [] for e in ENGS}
        self.known = {e: {f: -1 for f in ENGS} for e in ENGS}
        self.dma_waited = {e: set() for e in ENGS}
        self.dma_count = {e: 0 for e in ENGS}
        self.pending_dma = []
        self.out_dmas = []
        self.nbar = 0

    def op(self, eng, emit, reads=(), writes=(), dma=False, is_out=False):
        o = Op()
        o.eng = eng
        o.emit = emit
        o.sig = False
        o.dma = dma
        o.raw = False
        o.pos = len(self.ops[eng])
        o.waits = []
        o.prewait = None
        o.cnt = None
        deps = []
        for b in reads:
            if b.last_w is not None:
                deps.append(b.last_w)
        for b in writes:
            if b.last_w is not None:
                deps.append(b.last_w)
            deps.extend(b.readers)
        seen = set()
        for y in deps:
            if id(y) in seen:
                continue
            seen.add(id(y))
            self._add_wait(o, y)
        if dma:
            i = self.dma_count[eng]
            self.dma_count[eng] += 1
            o.sem = (eng, i % self.NDSEM)
            o.val = 16 * (i // self.NDSEM + 1)
            if i >= self.NDSEM:
                o.prewait = (o.sem, 16 * (i // self.NDSEM))
            self.pending_dma.append(o)
            if is_out:
                self.out_dmas.append(o)
        self.ops[eng].append(o)
        for b in writes:
            b.last_w = o
            b.readers = []
        for b in reads:
            if not dma:
                b.readers = [r for r in b.readers if r.dma or r.eng != eng]
            b.readers.append(o)
        return o

    def _add_wait(self, o, y):
        e = o.eng
        if y.dma:
            if id(y) in self.dma_waited[e]:
                return
            self.dma_waited[e].add(id(y))
            o.waits.append(y)
        else:
            f = y.eng
            if f == e and e == "pe":
                return
            if y.pos <= self.known[e][f]:
                return
            self.known[e][f] = y.pos
            y.sig = True
            o.waits.append(y)

    def barrier(self):
        lasts = []
        for f in ENGS:
            for y in reversed(self.ops[f]):
                if not y.dma and not y.raw:
                    lasts.append(y)
                    break
        o = Op()
        o.eng = "sp"; o.emit = None; o.sig = False; o.dma = False; o.raw = True
        o.pos = len(self.ops["sp"]); o.waits = []; o.prewait = None; o.cnt = None
        for y in lasts:
            if y.pos > self.known["sp"][y.eng]:
                self.known["sp"][y.eng] = y.pos
                y.sig = True
                o.waits.append(y)
        for y in self.pending_dma:
            if id(y) not in self.dma_waited["sp"]:
                o.waits.append(y)
        self.pending_dma = []
        self.nbar += 1
        o.val = self.nbar
        o.emit = "bar_sig"
        self.ops["sp"].append(o)
        for e in ENGS:
            if e == "sp":
                continue
            w = Op()
            w.eng = e; w.emit = "bar_wait"; w.sig = False; w.dma = False; w.raw = True
            w.pos = len(self.ops[e]); w.waits = []; w.prewait = None; w.cnt = None
            w.val = self.nbar
            self.ops[e].append(w)
        for e in ENGS:
            for f in ENGS:
                self.known[e][f] = len(self.ops[f]) - 1
            self.dma_waited[e] = set()

    def finish(self):
        o = Op()
        o.eng = "sp"; o.emit = "nop"; o.sig = False; o.dma = False; o.raw = True
        o.pos = len(self.ops["sp"]); o.waits = list(self.out_dmas) + [y for y in self.pending_dma]
        o.prewait = None; o.cnt = None; o.val = 0
        self.ops["sp"].append(o)

    def emit_all(self, es):
        nc = self.nc
        esem = {e: es.enter_context(nc.semaphore("es_" + e)) for e in ENGS}
        dsem = {}
        for e in ENGS:
            if self.dma_count[e] > 0:
                for i in range(min(self.NDSEM, self.dma_count[e])):
                    dsem[(e, i)] = es.enter_context(nc.semaphore("ds_%s_%d" % (e, i)))
        bsem = es.enter_context(nc.semaphore("barrier"))
        for e in ENGS:
            c = 0
            for o in self.ops[e]:
                if o.sig:
                    c += 1
                    o.cnt = c
        block = es.enter_context(nc.Block())
        decos = {"pe": block.tensor, "act": block.scalar, "dve": block.vector,
                 "pool": block.gpsimd, "sp": block.sync}
        stats = {}
        for e in ENGS:
            ops = self.ops[e]
            stats[e] = len(ops)
            if not ops:
                continue

            def body(eng, ops=ops):
                for o in ops:
                    for y in o.waits:
                        if y.dma:
                            eng.wait_ge(dsem[y.sem], y.val)
                        else:
                            eng.wait_ge(esem[y.eng], y.cnt)
                    if o.prewait is not None:
                        eng.wait_ge(dsem[o.prewait[0]], o.prewait[1])
                    if o.raw:
                        if o.emit == "bar_sig":
                            eng.nop().then_inc(bsem, 1)
                        elif o.emit == "bar_wait":
                            eng.wait_ge(bsem, o.val)
                        continue
                    ins = o.emit(eng)
                    if o.dma:
                        ins.then_inc(dsem[o.sem], 16)
                    elif o.sig:
                        ins.then_inc(esem[o.eng], 1)

            decos[e](body)
        return stats


def in_perm():
    o_u, o_q, o_kv, o_g, o_lx, o_lg = 0, 256, 768, 1536, 1560, 1816
    p = list(range(o_u, o_u + 256))
    for j in range(4):
        p += list(range(o_q + 64 * j, o_q + 64 * j + 64))
        p += list(range(o_q + 64 * (4 + j), o_q + 64 * (4 + j) + 64))
    for i in (0, 1, 2, 4):
        p += list(range(o_kv + 128 * i, o_kv + 128 * i + 128))
    p += list(range(o_lx, o_lx + 256))
    p += list(range(o_lg, o_lg + 256))
    for i in (3, 5):
        p += list(range(o_kv + 128 * i, o_kv + 128 * i + 128))
    p += list(range(o_g, o_g + 24))
    assert len(p) == INW
    return np.array(p)


class Ctx:
    pass


class _NC:
    def __init__(self, nc):
        self._nc = nc
        self._n = 0

    def __getattr__(self, k):
        return getattr(self._nc, k)

    def sbuf_tensor(self, name, *a, **kw):
        self._n += 1
        return self._nc.sbuf_tensor("%s_%d" % (name, self._n), *a, **kw)

    def psum_tensor(self, name, *a, **kw):
        self._n += 1
        return self._nc.psum_tensor("%s_%d" % (name, self._n), *a, **kw)


def bview(t, bufs):
    return t, bufs


def build_program(dbg=None):
    dbg = dbg or {}
    nc = _NC(bass.Bass("TRN2", target_bir_lowering=False))
    S = Sched(nc)
    C = Ctx()
    C.nc = nc
    C.S = S
    C.dbg = dbg
    C.bcast = lambda ap, n: ap.to_broadcast([128, n])

    def dump(name, ap, shape, dt, reads):
        if name not in dbg.get("dump", ()):
            return
        d = nc.dram_tensor("dump_" + name, list(shape), dt, kind="ExternalOutput").ap()
        S.op("sp", lambda q: q.dma_start(out=d, in_=ap), reads=reads, writes=[Buf()], dma=True, is_out=True)
    C.dump = dump

    def din(name, shape, dt=F32):
        return nc.dram_tensor(name, list(shape), dt, kind="ExternalInput").ap()

    def dscr(name, shape, dt=F32):
        kind = "ExternalOutput" if name in dbg.get("expose", ()) else "Internal"
        return nc.dram_tensor(name, list(shape), dt, kind=kind).ap()

    C.x = din("x", [NSEQ, SEQ, D])
    C.w_in = din("w_in", [DEPTH, D, INW])
    C.w_out = din("w_out", [DEPTH, D, D])
    C.w_gate = din("w_gate", [DEPTH, D, DFF])
    C.w_up = din("w_up", [DEPTH, D, DFF])
    C.w_down = din("w_down", [DEPTH, DFF, D])
    C.gains = din("gains", [128, 5, 8])
    C.gfin = din("gfin", [128, D])
    C.grep_in = din("grep", [128, 4, D])
    C.ident = din("ident", [128, 128])
    C.s5p = din("s5p", [DEPTH, 128, 8, 3])
    C.s5_bre = din("s5_bre", [DEPTH, 128, 8, 128])
    C.s5_bim = din("s5_bim", [DEPTH, 128, 8, 128])
    C.s5_cre = din("s5_cre", [DEPTH, 128, 8, 128])
    C.s5_cim = din("s5_cim", [DEPTH, 128, 8, 128])
    C.s5_dsk = din("s5_dsk", [DEPTH, 128, 2])
    C.s5_wglu = din("s5_wglu", [DEPTH, 256, 512])
    C.btab_raw = din("btab_raw", [128, 8, NBT])
    C.cvec = din("cvec", [128, 8])
    C.expand = din("expand", [32, NT, 128], BF16)
    C.vca_const = din("vca_const", [128, 33], BF16)
    C.cmask = din("cmask", [128, NT, 32])
    C.nsa_w1 = din("nsa_w1", [DEPTH, 2, 2, 128, 32, 64])
    C.nsa_peT = din("nsa_peT", [DEPTH, 128, 2, 32])
    C.nsa_w2pad = din("nsa_w2pad", [DEPTH, 64, 2, 128])
    C.nsa_w2v = din("nsa_w2v", [DEPTH, 64, 64])
    C.btab = dscr("btab", [128, 8, NBT], BF16)
    C.b_btab = Buf()
    C.lruv = din("lruv", [DEPTH, 128, 2, 8])
    C.lru_wa = din("lru_wa", [DEPTH, 128, 2, 128])
    C.lru_wx = din("lru_wx", [DEPTH, 128, 2, 128])
    C.out = nc.dram_tensor("out", [NSEQ, SEQ, D], F32, kind="ExternalOutput").ap()

    C.hres = dscr("hres", [NSEQ, SEQ, D])
    C.zu = dscr("zu", [256, SEQ])
    C.zlru = dscr("zlru", [512, SEQ])
    C.zqk = dscr("zqk", [8 * 128, SEQ], BF16)
    C.zv = dscr("zv", [SEQ, 2, 2, 64], BF16)
    C.zg = dscr("zg", [128, NT * 24])
    if "mixT_in" in dbg:
        C.mixT = din("mixT", [D, SEQ], BF16)
    else:
        C.mixT = dscr("mixT", [D, SEQ], BF16)
    C.hn2T = dscr("hn2T", [D, SEQ], BF16)
    C.b_hn2T = [Buf() for _ in range(NT)]
    C.b_hres = [[Buf("hres%d_%d" % (s, t)) for t in range(NT)] for s in range(NSEQ)]
    C.b_mixT = [Buf("mixT%d" % i) for i in range(8)]
    C.b_zu = [Buf() for _ in range(2)]
    C.b_zlru = [Buf() for _ in range(4)]
    C.b_zqk = [Buf() for _ in range(8)]
    C.b_zv = Buf()
    C.b_zg = Buf()
    C.b_out = Buf()

    with ExitStack() as es:
        C.identb = es.enter_context(nc.sbuf_tensor("identb", [128, 128], BF16))
        C.identf = es.enter_context(nc.sbuf_tensor("identf", [128, 128], F32))
        C.gn = es.enter_context(nc.sbuf_tensor("gn", [128, 5, 8], F32))
        C.gfin_sb = es.enter_context(nc.sbuf_tensor("gfin_sb", [128, D], F32))
        C.epsc = es.enter_context(nc.sbuf_tensor("epsc", [128, 1], F32))
        C.b_const = Buf("const")
        S.op("sp", lambda q: q.dma_start(out=C.identf[:], in_=C.ident[:, :]), writes=[C.b_const], dma=True)
        S.op("sp", lambda q: q.dma_start(out=C.gn[:], in_=C.gains[:, :, :]), writes=[C.b_const], dma=True)
        S.op("sp", lambda q: q.dma_start(out=C.gfin_sb[:], in_=C.gfin[:, :]), writes=[C.b_const], dma=True)
        S.op("dve", lambda v: v.tensor_copy(out=C.identb[:], in_=C.identf[:]), reads=[C.b_const], writes=[C.b_const])
        S.op("dve", lambda v: v.memset(C.epsc[:], EPS), writes=[C.b_const])
        S.barrier()
        if "NSA" in dbg.get("phases", ("NSA",)):
            phase_btab(C)
            S.barrier()

        phases = dbg.get("phases", ("A", "S5", "LRU", "NSA", "O1", "O2"))
        fns = {"A": lambda s, l, src: phase_A(C, s, l, src),
               "S5": lambda s, l, src: (phase_S5B(C, s, l) if C.dbg.get("s5b", 1) else phase_S5(C, s, l)),
               "LRU": lambda s, l, src: phase_LRU(C, s, l),
               "NSA": lambda s, l, src: phase_NSA(C, s, l),
               "O1": lambda s, l, src: phase_O1(C, s, l, src),
               "O2": lambda s, l, src: phase_O2(C, s, l, last=(l == DEPTH - 1))}
        for s in range(dbg.get("nseq", NSEQ)):
            for l in range(dbg.get("nlayer", DEPTH)):
                src = C.x if l == 0 else C.hres
                for ph in phases:
                    fns[ph](s, l, src)
                    S.barrier()
        S.finish()
        stats = S.emit_all(es)
    C.stats = stats
    return nc._nc, C


def rms_tile(C, S, grep, htile, b_h, sqjunk, ss, rstd, xn, b_tmp, b_xn):
    S.op("act", lambda a: a.activation(out=sqjunk[:], in_=htile, func=AF.Square, accum_out=ss[:]),
         reads=[b_h], writes=[b_tmp])
    S.op("act", lambda a: a.activation(out=rstd[:], in_=ss[:], func=AF.Sqrt, scale=1.0 / D, bias=C.epsc[:]),
         reads=[b_tmp, C.b_const], writes=[b_tmp])
    S.op("dve", lambda v: v.reciprocal(out=rstd[:], in_=rstd[:]), reads=[b_tmp], writes=[b_tmp])
    S.op("dve", lambda v: v.scalar_tensor_tensor(out=xn[:], in0=htile, scalar=rstd[:], in1=grep, op0=ALU.mult, op1=ALU.mult),
         reads=[b_h, b_tmp, C.b_gain], writes=[b_xn])


def norm_transpose_phase(C, S, es, nc, src_ap, s, b_src, hnT, b_hnT, ps_tr, b_ps_tr, grep):
    ht = [es.enter_context(nc.sbuf_tensor("nt_h%d" % i, [128, D], F32)) for i in range(2)]
    b_ht = [Buf() for _ in range(2)]
    sq = es.enter_context(nc.sbuf_tensor("nt_sq", [128, D], BF16))
    ss = [es.enter_context(nc.sbuf_tensor("nt_ss%d" % i, [128, 1], F32)) for i in range(2)]
    rs = [es.enter_context(nc.sbuf_tensor("nt_rs%d" % i, [128, 1], F32)) for i in range(2)]
    xn = [es.enter_context(nc.sbuf_tensor("nt_xn%d" % i, [128, D], BF16)) for i in range(2)]
    b_tmp = [Buf() for _ in range(2)]
    b_xn = [Buf() for _ in range(2)]
    for tt in range(NT):
        i = tt % 2
        S.op("sp", lambda q, tt=tt, i=i: q.dma_start(out=ht[i][:], in_=src_ap[s, tt * 128:(tt + 1) * 128, :]),
             reads=[b_src[tt]], writes=[b_ht[i]], dma=True)
        rms_tile(C, S, grep, ht[i][:], b_ht[i], sq, ss[i], rs[i], xn[i], b_tmp[i], b_xn[i])
        for ct in range(8):
            S.op("pe", lambda pe, ct=ct, i=i: pe.transpose(ps_tr[i][:, ct * 128:(ct + 1) * 128],
                                                             xn[i][:, ct * 128:(ct + 1) * 128], C.identb[:]),
                 reads=[b_xn[i], C.b_const], writes=[b_ps_tr[i]])
        eng = "act" if tt % 2 == 0 else "dve"
        if eng == "act":
            S.op("act", lambda a, tt=tt, i=i: a.copy(out=hnT[:, :, tt * 128:(tt + 1) * 128],
                                                     in_=ps_tr[i][:].rearrange("p (c t) -> p c t", c=8)),
                 reads=[b_ps_tr[i]], writes=[b_hnT[tt]])
        else:
            S.op("dve", lambda v, tt=tt, i=i: v.tensor_copy(out=hnT[:, :, tt * 128:(tt + 1) * 128],
                                                            in_=ps_tr[i][:].rearrange("p (c t) -> p c t", c=8)),
                 reads=[b_ps_tr[i]], writes=[b_hnT[tt]])


def load_cast_weight(C, S, nc, stage, b_stage, dst_ap_fn, src_ap_fn, n, gain_fn, b_dst, eng="act"):
    for k in range(n):
        i = k % len(stage)
        S.op("sp", lambda q, k=k, i=i: q.dma_start(out=stage[i][:], in_=src_ap_fn(k)),
             writes=[b_stage[i]], dma=True)
        g = gain_fn(k) if gain_fn is not None else None
        e = eng if isinstance(eng, str) else eng[k % len(eng)]
        if e == "act":
            if g is not None:
                S.op("act", lambda a, k=k, i=i, g=g: a.activation(out=dst_ap_fn(k), in_=stage[i][:], func=AF.Copy, scale=g),
                     reads=[b_stage[i], C.b_const], writes=[b_dst[k] if isinstance(b_dst, list) else b_dst])
            else:
                S.op("act", lambda a, k=k, i=i: a.copy(out=dst_ap_fn(k), in_=stage[i][:]),
                     reads=[b_stage[i]], writes=[b_dst[k] if isinstance(b_dst, list) else b_dst])
        else:
            if g is not None:
                S.op(e, lambda v, k=k, i=i, g=g: v.tensor_scalar(out=dst_ap_fn(k), in0=stage[i][:], scalar1=g, scalar2=None, op0=ALU.mult),
                     reads=[b_stage[i], C.b_const], writes=[b_dst[k] if isinstance(b_dst, list) else b_dst])
            else:
                S.op(e, lambda v, k=k, i=i: v.tensor_copy(out=dst_ap_fn(k), in_=stage[i][:]),
                     reads=[b_stage[i]], writes=[b_dst[k] if isinstance(b_dst, list) else b_dst])


def phase_A(C, s, l, src):
    nc, S = C.nc, C.S
    with ExitStack() as es:
        winb = es.enter_context(nc.sbuf_tensor("A_winb", [128, 8, INW], BF16))
        b_winb = [Buf() for _ in range(8)]
        stage = [es.enter_context(nc.sbuf_tensor("A_wst%d" % i, [128, INW], F32)) for i in range(2)]
        b_stage = [Buf() for _ in range(2)]
        hnT = es.enter_context(nc.sbuf_tensor("A_hnT", [128, 8, SEQ], BF16))
        b_hnT = [Buf() for _ in range(NT)]
        ps_tr = [es.enter_context(nc.psum_tensor("A_pstr%d" % i, [128, D], BF16)) for i in range(2)]
        b_ps_tr = [Buf() for _ in range(2)]
        ps_mm = [es.enter_context(nc.psum_tensor("A_psmm%d" % i, [128, 512], F32)) for i in range(4)]
        b_ps_mm = [Buf() for _ in range(4)]
        ev32 = [es.enter_context(nc.sbuf_tensor("A_ev32_%d" % i, [128, 512], F32)) for i in range(2)]
        ev16 = [es.enter_context(nc.sbuf_tensor("A_ev16_%d" % i, [128, 512], BF16)) for i in range(2)]
        b_ev32 = [Buf() for _ in range(2)]
        b_ev16 = [Buf() for _ in range(2)]
        vt = [es.enter_context(nc.sbuf_tensor("A_vt%d" % i, [128, 256], BF16)) for i in range(2)]
        gt_all = es.enter_context(nc.sbuf_tensor("A_gtall", [128, NT, 24], F32))
        b_vt = [Buf() for _ in range(2)]
        b_gt = [Buf() for _ in range(2)]

        w_src = C.w_in[l].rearrange("(c p) f -> p c f", p=128)
        load_cast_weight(C, S, nc, stage, b_stage,
                         lambda k: winb[:, k, :], lambda k: w_src[:, k, :], 8,
                         None, b_winb, eng=("act", "dve"))
        if C.dbg.get("A_steps", 9) < 2:
            return
        b_src = C.b_hres[s]
        gt_ = es.enter_context(nc.sbuf_tensor("A_gain", [128, D], F32))
        C.b_gain = Buf()
        S.op("sp", lambda q: q.dma_start(out=gt_[:], in_=C.grep_in[:, l, :]), writes=[C.b_gain], dma=True)
        norm_transpose_phase(C, S, es, nc, src, s, b_src, hnT, b_hnT, ps_tr, b_ps_tr, gt_[:])
        if C.dbg.get("A_steps", 9) < 3:
            return

        k32 = 0
        k16 = 0
        mmi = 0
        for ft in range(NFM):
            for nb in range(4):
                pi = mmi % 4
                mmi += 1
                for ct in range(8):
                    S.op("pe", lambda pe, ft=ft, nb=nb, ct=ct, pi=pi: pe.matmul(
                        ps_mm[pi][:], lhsT=winb[:, ct, ft * 128:(ft + 1) * 128],
                        rhs=hnT[:, ct, nb * 512:(nb + 1) * 512], start=(ct == 0), stop=(ct == 7)),
                        reads=[b_winb[ct]] + b_hnT[nb * 4:(nb + 1) * 4], writes=[b_ps_mm[pi]])
                cs = slice(nb * 512, (nb + 1) * 512)
                if ft < 2 or ft >= 10:
                    j = k32 % 2
                    k32 += 1
                    if ft < 2:
                        dst, bd = C.zu[ft * 128:(ft + 1) * 128, cs], C.b_zu[ft]
                    else:
                        dst, bd = C.zlru[(ft - 10) * 128:(ft - 9) * 128, cs], C.b_zlru[ft - 10]
                    S.op("dve", lambda v, j=j, pi=pi: v.tensor_copy(out=ev32[j][:], in_=ps_mm[pi][:]),
                         reads=[b_ps_mm[pi]], writes=[b_ev32[j]])
                    S.op("sp", lambda q, j=j, dst=dst: q.dma_start(out=dst, in_=ev32[j][:]),
                         reads=[b_ev32[j]], writes=[bd], dma=True)
                else:
                    j = k16 % 2
                    k16 += 1
                    sc = 0.125 if ft < 6 else 1.0
                    dst, bd = C.zqk[(ft - 2) * 128:(ft - 1) * 128, cs], C.b_zqk[ft - 2]
                    S.op("act", lambda a, j=j, pi=pi, sc=sc: a.mul(out=ev16[j][:], in_=ps_mm[pi][:], mul=sc),
                         reads=[b_ps_mm[pi]], writes=[b_ev16[j]])
                    S.op("sp", lambda q, j=j, dst=dst: q.dma_start(out=dst, in_=ev16[j][:]),
                         reads=[b_ev16[j]], writes=[bd], dma=True)
        if C.dbg.get("A_steps", 9) < 4:
            return
        for tt in range(NT):
            pi = mmi % 4
            mmi += 1
            j = tt % 2
            for ct in range(8):
                S.op("pe", lambda pe, tt=tt, ct=ct, pi=pi: pe.matmul(
                    ps_mm[pi][:, 0:NTM], lhsT=hnT[:, ct, tt * 128:(tt + 1) * 128],
                    rhs=winb[:, ct, NFM * 128:INW], start=(ct == 0), stop=(ct == 7)),
                    reads=[b_winb[ct], b_hnT[tt]], writes=[b_ps_mm[pi]])
            a4 = C.dbg.get("A4", "vg")
            if "v" in a4:
                S.op("dve", lambda v, j=j, pi=pi: v.tensor_copy(out=vt[j][:], in_=ps_mm[pi][:, 0:256]),
                     reads=[b_ps_mm[pi]], writes=[b_vt[j]])
                S.op("sp", lambda q, j=j, tt=tt: q.dma_start(
                    out=C.zv[tt * 128:(tt + 1) * 128].rearrange("t a h d -> t (a h d)"), in_=vt[j][:]),
                    reads=[b_vt[j]], writes=[C.b_zv], dma=True)
            if "g" in a4:
                S.op("act", lambda a, tt=tt, pi=pi: a.activation(out=gt_all[:, tt, :], in_=ps_mm[pi][:, 256:280], func=AF.Sigmoid),
                     reads=[b_ps_mm[pi]], writes=[b_gt[0]])
        S.op("sp", lambda q: q.dma_start(out=C.zg[:, :], in_=gt_all[:].rearrange("p t g -> p (t g)")),
             reads=[b_gt[0]], writes=[C.b_zg], dma=True)


GELU_C = 1.5957691216057308


def gelu_tanh(S, x, b_x, t, b_t, out, b_out, n=None):
    S.op("act", lambda a: a.activation(out=t, in_=x, func=AF.Square), reads=[b_x], writes=[b_t])
    S.op("dve", lambda v: v.tensor_scalar(out=t, in0=t, scalar1=0.044715, scalar2=1.0, op0=ALU.mult, op1=ALU.add),
         reads=[b_t], writes=[b_t])
    S.op("dve", lambda v: v.tensor_tensor(out=t, in0=t, in1=x, op=ALU.mult), reads=[b_t, b_x], writes=[b_t])
    S.op("act", lambda a: a.activation(out=t, in_=t, func=AF.Sigmoid, scale=GELU_C), reads=[b_t], writes=[b_t])
    S.op("dve", lambda v: v.tensor_tensor(out=out, in0=t, in1=x, op=ALU.mult), reads=[b_t, b_x], writes=[b_out])


def phase_LRU(C, s, l):
    nc, S = C.nc, C.S
    N = SEQ
    with ExitStack() as es:
        def T(name, shape, dt=F32):
            return es.enter_context(nc.sbuf_tensor("L_" + name, shape, dt))
        lv = T("lv", [128, 2, 8])
        wa = T("wa", [128, 2, 128]); wx = T("wx", [128, 2, 128])
        b_par = Buf()
        xpad = T("xpad", [128, N + 3]); xc = T("xc", [128, N]); r = T("r", [128, N]); gi = T("gi", [128, N])
        a = T("a", [128, N]); a2 = T("a2", [128, N]); bt = T("bt", [128, N]); h = T("h", [128, N])
        g = T("g", [128, N]); tg = T("tg", [128, N]); y = T("y", [128, N], BF16)
        sp = T("sp", [128, 4]); ones = T("ones", [128, 1])
        b = {k: Buf(k) for k in ("xpad", "xc", "r", "gi", "a", "a2", "bt", "h", "g", "tg", "y", "sp")}
        ps = [es.enter_context(nc.psum_tensor("L_ps%d" % i, [128, 512], F32)) for i in range(4)]
        b_ps = [Buf() for _ in range(4)]
        S.op("sp", lambda q: q.dma_start(out=lv[:], in_=C.lruv[l]), writes=[b_par], dma=True)
        S.op("sp", lambda q: q.dma_start(out=wa[:], in_=C.lru_wa[l]), writes=[b_par], dma=True)
        S.op("sp", lambda q: q.dma_start(out=wx[:], in_=C.lru_wx[l]), writes=[b_par], dma=True)
        S.op("dve", lambda v: v.memset(ones[:], 1.0), writes=[b_par])
        S.op("dve", lambda v: v.memset(xpad[:, 0:3], 0.0), writes=[b["xpad"]])
        mmi = 0
        for ct in range(2):
            S.op("sp", lambda q, ct=ct: q.dma_start(out=xpad[:, 3:3 + N], in_=C.zlru[ct * 128:(ct + 1) * 128, :]),
                 reads=[C.b_zlru[ct]], writes=[b["xpad"]], dma=True)
            S.op("sp", lambda q, ct=ct: q.dma_start(out=g[:], in_=C.zlru[256 + ct * 128:256 + (ct + 1) * 128, :]),
                 reads=[C.b_zlru[2 + ct]], writes=[b["g"]], dma=True)
            S.op("act", lambda e, ct=ct: e.activation(out=sp[:, 0:1], in_=lv[:, ct, 7:8], func=AF.Exp, scale=-1.0),
                 reads=[b_par], writes=[b["sp"]])
            S.op("act", lambda e: e.activation(out=sp[:, 1:2], in_=sp[:, 0:1], func=AF.Ln, scale=1.0, bias=ones[:]),
                 reads=[b["sp"], b_par], writes=[b["sp"]])
            S.op("dve", lambda v: v.tensor_scalar(out=sp[:, 2:3], in0=sp[:, 1:2], scalar1=-8.0, scalar2=None, op0=ALU.mult),
                 reads=[b["sp"]], writes=[b["sp"]])
            S.op("dve", lambda v: v.tensor_scalar(out=sp[:, 3:4], in0=sp[:, 1:2], scalar1=-16.0, scalar2=None, op0=ALU.mult),
                 reads=[b["sp"]], writes=[b["sp"]])
            S.op("dve", lambda v, ct=ct: v.tensor_scalar(out=xc[:], in0=xpad[:, 3:3 + N], scalar1=lv[:, ct, 3:4],
                                                         scalar2=lv[:, ct, 4:5], op0=ALU.mult, op1=ALU.add),
                 reads=[b["xpad"], b_par], writes=[b["xc"]])
            for i in range(3):
                S.op("dve", lambda v, ct=ct, i=i: v.scalar_tensor_tensor(out=xc[:], in0=xpad[:, i:i + N], scalar=lv[:, ct, i:i + 1],
                                                                         in1=xc[:], op0=ALU.mult, op1=ALU.add),
                     reads=[b["xpad"], b["xc"], b_par], writes=[b["xc"]])
            for (w, dst, bi, nm) in ((wa, r, 5, "r"), (wx, gi, 6, "gi")):
                for nb in range(4):
                    pi = mmi % 4
                    mmi += 1
                    ns = slice(nb * 512, (nb + 1) * 512)
                    S.op("pe", lambda pe, w=w, ct=ct, pi=pi, ns=ns: pe.matmul(ps[pi][:], lhsT=w[:, ct, :], rhs=xc[:, ns],
                                                                             start=True, stop=True),
                         reads=[b_par, b["xc"]], writes=[b_ps[pi]])
                    S.op("act", lambda e, dst=dst, pi=pi, ns=ns, ct=ct, bi=bi: e.activation(
                        out=dst[:, ns], in_=ps[pi][:], func=AF.Sigmoid, bias=lv[:, ct, bi:bi + 1]),
                        reads=[b_ps[pi], b_par], writes=[b[nm]])
            S.op("act", lambda e: e.activation(out=a[:], in_=r[:], func=AF.Exp, scale=sp[:, 2:3]),
                 reads=[b["r"], b["sp"]], writes=[b["a"]])
            S.op("act", lambda e: e.activation(out=a2[:], in_=r[:], func=AF.Exp, scale=sp[:, 3:4]),
                 reads=[b["r"], b["sp"]], writes=[b["a2"]])
            S.op("act", lambda e: e.activation(out=a2[:], in_=a2[:], func=AF.Sqrt, scale=-1.0, bias=ones[:]),
                 reads=[b["a2"], b_par], writes=[b["a2"]])
            S.op("dve", lambda v: v.tensor_tensor(out=bt[:], in0=gi[:], in1=xc[:], op=ALU.mult),
                 reads=[b["gi"], b["xc"]], writes=[b["bt"]])
            S.op("dve", lambda v: v.tensor_tensor(out=bt[:], in0=bt[:], in1=a2[:], op=ALU.mult),
                 reads=[b["bt"], b["a2"]], writes=[b["bt"]])
            S.op("dve", lambda v: v.tensor_tensor_scan(out=h[:], data0=a[:], data1=bt[:], initial=0.0,
                                                       op0=ALU.mult, op1=ALU.add),
                 reads=[b["a"], b["bt"]], writes=[b["h"]])
            gelu_tanh(S, g[:], b["g"], tg[:], b["tg"], tg[:], b["tg"])
            S.op("dve", lambda v: v.tensor_tensor(out=y[:], in0=h[:], in1=tg[:], op=ALU.mult),
                 reads=[b["h"], b["tg"]], writes=[b["y"]])
            S.op("sp", lambda q, ct=ct: q.dma_start(out=C.mixT[768 + ct * 128:768 + (ct + 1) * 128, :], in_=y[:]),
                 reads=[b["y"]], writes=[C.b_mixT[6 + ct]], dma=True)


TC = 512
NLEV = 9
MAGIC = 12582912.0
TWO_PI = 2.0 * math.pi
CW1 = 6.28125
CW2 = TWO_PI - 6.28125


def phase_S5(C, s, l):
    nc, S = C.nc, C.S
    with ExitStack() as es:
        def T(name, shape, dt=F32):
            return es.enter_context(nc.sbuf_tensor("S_" + name, shape, dt))
        par = T("par", [128, 8, 3]); b_par = Buf()
        Bre = T("Bre", [128, 8, 128]); Bim = T("Bim", [128, 8, 128]); b_B = Buf()
        Cre = T("Cre", [128, 8, 128]); Cim = T("Cim", [128, 8, 128]); b_Cf = Buf()
        Creb = T("Creb", [128, 8, 128], BF16); nCimb = T("nCimb", [128, 8, 128], BF16); b_Cb = Buf()
        dsk = T("dsk", [128, 2]); wgl = T("wgl", [128, 2, 512]); wglb = T("wglb", [128, 2, 512], BF16); b_w = Buf()
        sm = T("sm", [128, 24, 8]); b_sm = Buf()
        wre = T("wre", [128, NLEV + 1, 8]); wim = T("wim", [128, NLEV + 1, 8]); b_wp = Buf()
        hpi = T("hpi", [128, 1])
        Ec = T("Ec", [128, 8, TC]); Es = T("Es", [128, 8, TC]); b_E = [Buf() for _ in range(8)]
        Mre = T("Mre", [128, 8, 128], BF16); Mim = T("Mim", [128, 8, 128], BF16); Mt = T("Mt", [128, 128]); b_M = Buf(); b_Mt = Buf()
        WB = T("WB", [128, 2, 8, 128], BF16); b_WB = Buf()
        u32 = T("u32", [128, 2, SEQ]); ub = T("ub", [128, 2, SEQ], BF16); b_u = [Buf() for _ in range(2)]; b_ub = [Buf() for _ in range(2)]
        gr = T("gr", [128, 8, TC]); gim = T("gim", [128, 8, TC]); b_g = [Buf() for _ in range(8)]
        init = T("init", [128, 2, 8]); b_init = Buf()
        t1 = T("t1", [128, TC]); t2 = T("t2", [128, TC]); xr = T("xr", [128, TC]); xi = T("xi", [128, TC])
        b_t1 = Buf(); b_t2 = Buf(); b_xr = Buf(); b_xi = Buf()
        p1 = T("p1", [128, TC]); p2 = T("p2", [128, TC]); b_p1 = Buf(); b_p2 = Buf()
        p3 = T("p3", [128, TC]); b_p3 = Buf()
        hre = T("hre", [128, 8, TC], BF16); him = T("him", [128, 8, TC], BF16); b_h = [Buf() for _ in range(8)]
        yv = T("yv", [128, TC]); yt = T("yt", [128, TC]); b_yv = Buf(); b_yt = Buf()
        yg = T("yg", [128, 2, TC], BF16); b_yg = [Buf() for _ in range(2)]
        sgl = T("sgl", [128, TC]); b_sgl = Buf()
        og = [T("og%d" % i, [128, TC], BF16) for i in range(2)]; b_og = [Buf() for _ in range(2)]
        psX = [es.enter_context(nc.psum_tensor("S_psX%d" % i, [128, 512], F32)) for i in range(4)]
        b_psX = [Buf() for _ in range(4)]
        psY = [es.enter_context(nc.psum_tensor("S_psY%d" % i, [128, 512], F32)) for i in range(2)]
        b_psY = [Buf() for _ in range(2)]
        psT = es.enter_context(nc.psum_tensor("S_psT", [128, 8, 128], BF16)); b_psT = Buf()

        def dve(fn, reads, writes):
            S.op("dve", fn, reads=reads, writes=writes)

        def act(fn, reads, writes):
            S.op("act", fn, reads=reads, writes=writes)

        S.op("sp", lambda q: q.dma_start(out=par[:], in_=C.s5p[l]), writes=[b_par], dma=True)
        S.op("sp", lambda q: q.dma_start(out=Bre[:], in_=C.s5_bre[l]), writes=[b_B], dma=True)
        S.op("sp", lambda q: q.dma_start(out=Bim[:], in_=C.s5_bim[l]), writes=[b_B], dma=True)
        S.op("sp", lambda q: q.dma_start(out=Cre[:], in_=C.s5_cre[l]), writes=[b_Cf], dma=True)
        S.op("sp", lambda q: q.dma_start(out=Cim[:], in_=C.s5_cim[l]), writes=[b_Cf], dma=True)
        S.op("sp", lambda q: q.dma_start(out=dsk[:], in_=C.s5_dsk[l]), writes=[b_w], dma=True)
        S.op("sp", lambda q: q.dma_start(out=wgl[:], in_=C.s5_wglu[l].rearrange("(c p) f -> p c f", p=128)), writes=[b_w], dma=True)
        for ct in range(2):
            S.op("sp", lambda q, ct=ct: q.dma_start(out=u32[:, ct, :], in_=C.zu[ct * 128:(ct + 1) * 128, :]),
                 reads=[C.b_zu[ct]], writes=[b_u[ct]], dma=True)
            S.op("act", lambda v, ct=ct: v.copy(out=ub[:, ct, :], in_=u32[:, ct, :]), reads=[b_u[ct]], writes=[b_ub[ct]])
        act(lambda a: a.copy(out=wglb[:], in_=wgl[:]), [b_w], [b_w])
        act(lambda a: a.copy(out=Creb[:], in_=Cre[:]), [b_Cf], [b_Cb])
        act(lambda a: a.mul(out=nCimb[:], in_=Cim[:], mul=-1.0), [b_Cf], [b_Cb])
        dve(lambda v: v.memset(hpi[:], math.pi / 2.0), [], [b_sm])

        LR, LI, LDT = par[:, :, 0], par[:, :, 1], par[:, :, 2]
        sl = lambda i: sm[:, i, :]
        DT, MAG, ANG, K_, R_, AR, SN, CS, ABR, ABI, DEN, T1_, T2_, FRE, FIM = range(15)
        act(lambda a: a.activation(out=sl(DT), in_=LDT, func=AF.Exp), [b_par], [b_sm])
        dve(lambda v: v.tensor_tensor(out=sl(MAG), in0=LR, in1=sl(DT), op=ALU.mult), [b_par, b_sm], [b_sm])
        act(lambda a: a.activation(out=sl(MAG), in_=sl(MAG), func=AF.Exp), [b_sm], [b_sm])
        dve(lambda v: v.tensor_tensor(out=sl(ANG), in0=LI, in1=sl(DT), op=ALU.mult), [b_par, b_sm], [b_sm])
        dve(lambda v: v.tensor_scalar(out=sl(K_), in0=sl(ANG), scalar1=1.0 / TWO_PI, scalar2=MAGIC, op0=ALU.mult, op1=ALU.add), [b_sm], [b_sm])
        dve(lambda v: v.tensor_scalar(out=sl(K_), in0=sl(K_), scalar1=-MAGIC, scalar2=None, op0=ALU.add), [b_sm], [b_sm])
        dve(lambda v: v.scalar_tensor_tensor(out=sl(R_), in0=sl(K_), scalar=-CW1, in1=sl(ANG), op0=ALU.mult, op1=ALU.add), [b_sm], [b_sm])
        dve(lambda v: v.scalar_tensor_tensor(out=sl(R_), in0=sl(K_), scalar=-CW2, in1=sl(R_), op0=ALU.mult, op1=ALU.add), [b_sm], [b_sm])
        dve(lambda v: v.tensor_scalar(out=sl(R_), in0=sl(R_), scalar1=math.pi, scalar2=-math.pi, op0=ALU.min, op1=ALU.max), [b_sm], [b_sm])
        act(lambda a: a.activation(out=sl(AR), in_=sl(R_), func=AF.Abs), [b_sm], [b_sm])
        act(lambda a: a.activation(out=sl(SN), in_=sl(R_), func=AF.Sin), [b_sm], [b_sm])
        act(lambda a: a.activation(out=sl(CS), in_=sl(AR), func=AF.Sin, scale=-1.0, bias=hpi[:]), [b_sm], [b_sm])
        dve(lambda v: v.tensor_tensor(out=sl(ABR), in0=sl(MAG), in1=sl(CS), op=ALU.mult), [b_sm], [b_sm])
        dve(lambda v: v.tensor_tensor(out=sl(ABI), in0=sl(MAG), in1=sl(SN), op=ALU.mult), [b_sm], [b_sm])
        dve(lambda v: v.tensor_tensor(out=sl(DEN), in0=LR, in1=LR, op=ALU.mult), [b_par, b_sm], [b_sm])
        dve(lambda v: v.tensor_tensor(out=sl(T1_), in0=LI, in1=LI, op=ALU.mult), [b_par, b_sm], [b_sm])
        dve(lambda v: v.tensor_tensor(out=sl(DEN), in0=sl(DEN), in1=sl(T1_), op=ALU.add), [b_sm], [b_sm])
        dve(lambda v: v.reciprocal(out=sl(DEN), in_=sl(DEN)), [b_sm], [b_sm])
        dve(lambda v: v.tensor_scalar(out=sl(T1_), in0=sl(ABR), scalar1=-1.0, scalar2=None, op0=ALU.add), [b_sm], [b_sm])
        dve(lambda v: v.tensor_tensor(out=sl(FRE), in0=sl(T1_), in1=LR, op=ALU.mult), [b_par, b_sm], [b_sm])
        dve(lambda v: v.tensor_tensor(out=sl(T2_), in0=sl(ABI), in1=LI, op=ALU.mult), [b_par, b_sm], [b_sm])
        dve(lambda v: v.tensor_tensor(out=sl(FRE), in0=sl(FRE), in1=sl(T2_), op=ALU.add), [b_sm], [b_sm])
        dve(lambda v: v.tensor_tensor(out=sl(FRE), in0=sl(FRE), in1=sl(DEN), op=ALU.mult), [b_sm], [b_sm])
        dve(lambda v: v.tensor_tensor(out=sl(FIM), in0=sl(ABI), in1=LR, op=ALU.mult), [b_par, b_sm], [b_sm])
        dve(lambda v: v.tensor_tensor(out=sl(T2_), in0=sl(T1_), in1=LI, op=ALU.mult), [b_par, b_sm], [b_sm])
        dve(lambda v: v.tensor_tensor(out=sl(FIM), in0=sl(FIM), in1=sl(T2_), op=ALU.subtract), [b_sm], [b_sm])
        dve(lambda v: v.tensor_tensor(out=sl(FIM), in0=sl(FIM), in1=sl(DEN), op=ALU.mult), [b_sm], [b_sm])
        dve(lambda v: v.tensor_copy(out=wre[:, 0, :], in_=sl(CS)), [b_sm], [b_wp])
        dve(lambda v: v.tensor_copy(out=wim[:, 0, :], in_=sl(SN)), [b_sm], [b_wp])
        for k in range(NLEV):
            dve(lambda v, k=k: v.tensor_tensor(out=sl(T1_), in0=wre[:, k, :], in1=wre[:, k, :], op=ALU.mult), [b_wp, b_sm], [b_sm])
            dve(lambda v, k=k: v.tensor_tensor(out=sl(T2_), in0=wim[:, k, :], in1=wim[:, k, :], op=ALU.mult), [b_wp, b_sm], [b_sm])
            dve(lambda v, k=k: v.tensor_tensor(out=wre[:, k + 1, :], in0=sl(T1_), in1=sl(T2_), op=ALU.subtract), [b_sm, b_wp], [b_wp])
            dve(lambda v, k=k: v.tensor_tensor(out=sl(T1_), in0=wre[:, k, :], in1=wim[:, k, :], op=ALU.mult), [b_wp, b_sm], [b_sm])
            dve(lambda v, k=k: v.tensor_scalar(out=wim[:, k + 1, :], in0=sl(T1_), scalar1=2.0, scalar2=None, op0=ALU.mult), [b_sm, b_wp], [b_wp])
        tA = T("tA", [128, 8, TC // 2]); tB = T("tB", [128, 8, TC // 2]); b_tA = Buf(); b_tB = Buf()
        b_Eall = Buf()
        dve(lambda v: v.memset(Ec[:, :, 0:1], 1.0), [], [b_Eall])
        dve(lambda v: v.memset(Es[:, :, 0:1], 0.0), [], [b_Eall])
        for k in range(NLEV):
            n = 1 << k
            wrb = wre[:, k, :].unsqueeze(2).to_broadcast([128, 8, n])
            wib = wim[:, k, :].unsqueeze(2).to_broadcast([128, 8, n])
            dve(lambda v, n=n, wib=wib: v.tensor_tensor(out=tA[:, :, 0:n], in0=Es[:, :, 0:n], in1=wib, op=ALU.mult), [b_Eall, b_wp], [b_tA])
            dve(lambda v, n=n, wrb=wrb: v.tensor_tensor(out=tB[:, :, 0:n], in0=Ec[:, :, 0:n], in1=wrb, op=ALU.mult), [b_Eall, b_wp], [b_tB])
            dve(lambda v, n=n: v.tensor_tensor(out=Ec[:, :, n:2 * n], in0=tB[:, :, 0:n], in1=tA[:, :, 0:n], op=ALU.subtract), [b_tA, b_tB], [b_Eall])
            dve(lambda v, n=n, wib=wib: v.tensor_tensor(out=tA[:, :, 0:n], in0=Ec[:, :, 0:n], in1=wib, op=ALU.mult), [b_Eall, b_wp], [b_tA])
            dve(lambda v, n=n, wrb=wrb: v.tensor_tensor(out=tB[:, :, 0:n], in0=Es[:, :, 0:n], in1=wrb, op=ALU.mult), [b_Eall, b_wp], [b_tB])
            dve(lambda v, n=n: v.tensor_tensor(out=Es[:, :, n:2 * n], in0=tB[:, :, 0:n], in1=tA[:, :, 0:n], op=ALU.add), [b_tA, b_tB], [b_Eall])
        for st in range(8):
            dve(lambda v: v.engine_nop() if hasattr(v, "engine_nop") else v.memset(hpi[:], math.pi / 2.0), [b_Eall], [b_E[st]]) if False else None
            b_E[st].last_w = b_Eall.last_w
        frb = sm[:, FRE, :].unsqueeze(2).to_broadcast([128, 8, 128])
        fib = sm[:, FIM, :].unsqueeze(2).to_broadcast([128, 8, 128])
        MtA = T("MtA", [128, 8, 128]); MtB = T("MtB", [128, 8, 128])
        for (dst, A_, B_, sgn) in ((Mre, Bre, Bim, ALU.subtract), (Mim, Bim, Bre, ALU.add)):
            dve(lambda v, B_=B_: v.tensor_tensor(out=MtA[:], in0=B_[:], in1=fib, op=ALU.mult), [b_B, b_sm], [b_Mt])
            dve(lambda v, A_=A_: v.tensor_tensor(out=MtB[:], in0=A_[:], in1=frb, op=ALU.mult), [b_B, b_sm], [b_Mt])
            dve(lambda v, dst=dst, sgn=sgn: v.tensor_tensor(out=dst[:], in0=MtB[:], in1=MtA[:], op=sgn), [b_Mt], [b_M])
        for ri, Msrc in enumerate((Mre, Mim)):
            for st in range(8):
                S.op("pe", lambda pe, st=st, Msrc=Msrc: pe.transpose(psT[:, st, :], Msrc[:, st, :], C.identb[:]),
                     reads=[b_M, C.b_const], writes=[b_psT])
            act(lambda a, ri=ri: a.copy(out=WB[:, ri, :, :], in_=psT[:]), [b_psT], [b_WB])
        dve(lambda v: v.memset(init[:], 0.0), [], [b_init])

        xi_ = 0
        for k in range(SEQ // TC):
            ts = slice(k * TC, (k + 1) * TC)
            for st in range(8):
                ct = st // 4
                pr, pim = xi_ % 4, (xi_ + 1) % 4
                xi_ += 2
                S.op("pe", lambda pe, st=st, ct=ct, pr=pr, ts=ts: pe.matmul(psX[pr][:], lhsT=WB[:, 0, st, :], rhs=ub[:, ct, ts], start=True, stop=True),
                     reads=[b_WB, b_ub[ct]], writes=[b_psX[pr]])
                S.op("pe", lambda pe, st=st, ct=ct, pim=pim, ts=ts: pe.matmul(psX[pim][:], lhsT=WB[:, 1, st, :], rhs=ub[:, ct, ts], start=True, stop=True),
                     reads=[b_WB, b_ub[ct]], writes=[b_psX[pim]])
                dve(lambda v, st=st, pr=pr: v.tensor_tensor(out=t1[:], in0=psX[pr][:], in1=Ec[:, st, :], op=ALU.mult), [b_psX[pr], b_E[st]], [b_t1])
                dve(lambda v, st=st, pim=pim: v.tensor_tensor(out=t2[:], in0=psX[pim][:], in1=Es[:, st, :], op=ALU.mult), [b_psX[pim], b_E[st]], [b_t2])
                dve(lambda v: v.tensor_tensor(out=xr[:], in0=t1[:], in1=t2[:], op=ALU.add), [b_t1, b_t2], [b_xr])
                dve(lambda v, st=st, pim=pim: v.tensor_tensor(out=t1[:], in0=psX[pim][:], in1=Ec[:, st, :], op=ALU.mult), [b_psX[pim], b_E[st]], [b_t1])
                dve(lambda v, st=st, pr=pr: v.tensor_tensor(out=t2[:], in0=psX[pr][:], in1=Es[:, st, :], op=ALU.mult), [b_psX[pr], b_E[st]], [b_t2])
                dve(lambda v: v.tensor_tensor(out=xi[:], in0=t1[:], in1=t2[:], op=ALU.subtract), [b_t1, b_t2], [b_xi])
                dve(lambda v, st=st: v.tensor_tensor_scan(out=gr[:, st, :], data0=C.bcast(sm[:, MAG, st:st + 1], TC), data1=xr[:],
                                                          initial=init[:, 0, st:st + 1], op0=ALU.mult, op1=ALU.add),
                    [b_sm, b_xr, b_init], [b_g[st]])
                dve(lambda v, st=st: v.tensor_tensor_scan(out=gim[:, st, :], data0=C.bcast(sm[:, MAG, st:st + 1], TC), data1=xi[:],
                                                          initial=init[:, 1, st:st + 1], op0=ALU.mult, op1=ALU.add),
                    [b_sm, b_xi, b_init], [b_g[st]])
                pl = lambda fn, reads, writes: S.op("pool", fn, reads=reads, writes=writes)
                dve(lambda v, st=st: v.tensor_tensor(out=p3[:], in0=gr[:, st, :], in1=Ec[:, st, :], op=ALU.mult), [b_g[st], b_E[st]], [b_p3])
                pl(lambda v, st=st: v.tensor_tensor(out=p2[:], in0=gim[:, st, :], in1=Es[:, st, :], op=ALU.mult), [b_g[st], b_E[st]], [b_p2])
                pl(lambda v, st=st: v.tensor_tensor(out=hre[:, st, :], in0=p3[:], in1=p2[:], op=ALU.subtract), [b_p3, b_p2], [b_h[st]])
                pl(lambda v, st=st: v.tensor_tensor(out=p1[:], in0=gim[:, st, :], in1=Ec[:, st, :], op=ALU.mult), [b_g[st], b_E[st]], [b_p1])
                pl(lambda v, st=st: v.tensor_tensor(out=p2[:], in0=gr[:, st, :], in1=Es[:, st, :], op=ALU.mult), [b_g[st], b_E[st]], [b_p2])
                pl(lambda v, st=st: v.tensor_tensor(out=him[:, st, :], in0=p1[:], in1=p2[:], op=ALU.add), [b_p1, b_p2], [b_h[st]])
            if k + 1 < SEQ // TC:
                glr, gli = gr[:, :, TC - 1], gim[:, :, TC - 1]
                wr_, wi_ = wre[:, NLEV, :], wim[:, NLEV, :]
                dve(lambda v: v.tensor_tensor(out=sl(T1_), in0=glr, in1=wr_, op=ALU.mult), b_g + [b_wp, b_sm], [b_sm])
                dve(lambda v: v.tensor_tensor(out=sl(T2_), in0=gli, in1=wi_, op=ALU.mult), b_g + [b_wp, b_sm], [b_sm])
                dve(lambda v: v.tensor_tensor(out=init[:, 0, :], in0=sl(T1_), in1=sl(T2_), op=ALU.subtract), [b_sm, b_init], [b_init])
                dve(lambda v: v.tensor_tensor(out=sl(T1_), in0=gli, in1=wr_, op=ALU.mult), b_g + [b_wp, b_sm], [b_sm])
                dve(lambda v: v.tensor_tensor(out=sl(T2_), in0=glr, in1=wi_, op=ALU.mult), b_g + [b_wp, b_sm], [b_sm])
                dve(lambda v: v.tensor_tensor(out=init[:, 1, :], in0=sl(T1_), in1=sl(T2_), op=ALU.add), [b_sm, b_init], [b_init])
            for ct in range(2):
                py = ct
                for j in range(4):
                    st = ct * 4 + j
                    S.op("pe", lambda pe, st=st, py=py, j=j: pe.matmul(psY[py][:], lhsT=Creb[:, st, :], rhs=hre[:, st, :], start=(j == 0), stop=False),
                         reads=[b_Cb, b_h[st]], writes=[b_psY[py]])
                    S.op("pe", lambda pe, st=st, py=py, j=j: pe.matmul(psY[py][:], lhsT=nCimb[:, st, :], rhs=him[:, st, :], start=False, stop=(j == 3)),
                         reads=[b_Cb, b_h[st]], writes=[b_psY[py]])
                dve(lambda v, ct=ct, py=py, ts=ts: v.scalar_tensor_tensor(out=yv[:], in0=u32[:, ct, ts], scalar=dsk[:, ct:ct + 1], in1=psY[py][:],
                                                                         op0=ALU.mult, op1=ALU.add), [b_u[ct], b_w, b_psY[py]], [b_yv])
                gelu_tanh(S, yv[:], b_yv, yt[:], b_yt, yg[:, ct, :], b_yg[ct])
            for j in range(2):
                S.op("pe", lambda pe, j=j: pe.matmul(psY[0][:], lhsT=wglb[:, 0, j * 128:(j + 1) * 128], rhs=yg[:, 0, :], start=True, stop=False),
                     reads=[b_w] + b_yg, writes=[b_psY[0]])
                S.op("pe", lambda pe, j=j: pe.matmul(psY[0][:], lhsT=wglb[:, 1, j * 128:(j + 1) * 128], rhs=yg[:, 1, :], start=False, stop=True),
                     reads=[b_w] + b_yg, writes=[b_psY[0]])
                S.op("pe", lambda pe, j=j: pe.matmul(psY[1][:], lhsT=wglb[:, 0, 256 + j * 128:256 + (j + 1) * 128], rhs=yg[:, 0, :], start=True, stop=False),
                     reads=[b_w] + b_yg, writes=[b_psY[1]])
                S.op("pe", lambda pe, j=j: pe.matmul(psY[1][:], lhsT=wglb[:, 1, 256 + j * 128:256 + (j + 1) * 128], rhs=yg[:, 1, :], start=False, stop=True),
                     reads=[b_w] + b_yg, writes=[b_psY[1]])
                act(lambda a: a.activation(out=sgl[:], in_=psY[1][:], func=AF.Sigmoid), [b_psY[1]], [b_sgl])
                dve(lambda v, j=j: v.tensor_tensor(out=og[j][:], in0=psY[0][:], in1=sgl[:], op=ALU.mult), [b_psY[0], b_sgl], [b_og[j]])
                S.op("sp", lambda q, j=j, ts=ts: q.dma_start(out=C.mixT[j * 128:(j + 1) * 128, ts], in_=og[j][:]),
                     reads=[b_og[j]], writes=[C.b_mixT[j]], dma=True)


LB = 8
NBLK = SEQ // LB
NW = 11


def phase_S5B(C, s, l):
    nc, S = C.nc, C.S
    with ExitStack() as es:
        def T(name, shape, dt=F32):
            return es.enter_context(nc.sbuf_tensor("B_" + name, shape, dt))
        par = T("par", [128, 8, 3]); b_par = Buf()
        Bre = T("Bre", [128, 8, 128]); Bim = T("Bim", [128, 8, 128]); b_B = Buf()
        Cre = T("Cre", [128, 8, 128]); Cim = T("Cim", [128, 8, 128]); b_Cf = Buf()
        dsk = T("dsk", [128, 2]); wgl = T("wgl", [128, 2, 512]); wglb = T("wglb", [128, 2, 512], BF16); b_w = Buf()
        sm = T("sm", [128, 24, 8]); b_sm = Buf()
        wre = T("wre", [128, NW + 1, 8]); wim = T("wim", [128, NW + 1, 8]); b_wp = Buf()
        hpi = T("hpi", [128, 1])
        pwr = T("pwr", [128, LB + 1, 8]); pwi = T("pwi", [128, LB + 1, 8]); b_pw = Buf()
        r8 = T("r8", [128, 8]); b_r8 = Buf()
        Ec = T("Ec", [128, 8, NBLK]); Es = T("Es", [128, 8, NBLK]); b_Eall = Buf()
        tA = T("tA", [128, 8, NBLK // 2]); tB = T("tB", [128, 8, NBLK // 2]); b_tA = Buf(); b_tB = Buf()
        MbR = T("MbR", [128, 8, 128]); MbI = T("MbI", [128, 8, 128]); b_Mb = Buf()
        MtA = T("MtA", [128, 8, 128]); MtB = T("MtB", [128, 8, 128]); b_Mt = Buf()
        PtA = T("PtA", [128, 8, 128]); PtB = T("PtB", [128, 8, 128]); b_Pt = Buf()
        vst = [T("vst%d" % i, [128, 8, 128], BF16) for i in range(2)]; b_vst = [Buf() for _ in range(2)]
        W1T = T("W1T", [128, LB, 2, 8, 128], BF16); b_W1T = Buf()
        CaR = T("CaR", [128, LB + 1, 8, 128], BF16); nCaI = T("nCaI", [128, LB + 1, 8, 128], BF16); b_Ca = Buf()
        BbR = T("BbR", [128, 8, 128], BF16); BbI = T("BbI", [128, 8, 128], BF16); b_Bb = Buf()
        Kt = T("Kt", [128, 2, LB, 128], BF16); b_Kt = Buf()
        u32 = T("u32", [128, 2, SEQ]); b_u = [Buf() for _ in range(2)]
        uS = T("uS", [128, 2, LB, NBLK], BF16); b_uS = [Buf() for _ in range(2)]
        Hs = T("Hs", [128, 8, 2, NBLK], BF16); b_Hs = [Buf() for _ in range(8)]
        t1 = T("t1", [128, NBLK]); t2 = T("t2", [128, NBLK]); xr = T("xr", [128, NBLK]); xi = T("xi", [128, NBLK])
        gr = T("gr", [128, NBLK]); gi = T("gi", [128, NBLK]); p1 = T("p1", [128, NBLK]); p2 = T("p2", [128, NBLK])
        b_t1 = Buf(); b_t2 = Buf(); b_xr = Buf(); b_xi = Buf(); b_gr = Buf(); b_gi = Buf(); b_p1 = Buf(); b_p2 = Buf()
        yv = T("yv", [128, 512]); yt = T("yt", [128, 512]); b_yv = Buf(); b_yt = Buf()
        yg = T("yg", [128, 2, 512], BF16); b_yg = [Buf() for _ in range(2)]
        sgl = T("sgl", [128, 512]); b_sgl = Buf()
        og = [T("og%d" % i, [128, 512], BF16) for i in range(2)]; b_og = [Buf() for _ in range(2)]
        psT = es.enter_context(nc.psum_tensor("B_psT", [128, 8, 128], BF16)); b_psT = Buf()
        psSb = es.enter_context(nc.psum_tensor("B_psS", [128, 2, NBLK], F32)); b_psS = Buf()
        psK = es.enter_context(nc.psum_tensor("B_psK", [128, 4, 128], F32)); b_psK = Buf()
        psY = [es.enter_context(nc.psum_tensor("B_psY%d" % i, [128, 512], F32)) for i in range(2)]; b_psY = [Buf() for _ in range(2)]
        psG = [es.enter_context(nc.psum_tensor("B_psG%d" % i, [128, 512], F32)) for i in range(2)]; b_psG = [Buf() for _ in range(2)]

        def dve(fn, reads, writes):
            S.op("dve", fn, reads=reads, writes=writes)

        def act(fn, reads, writes):
            S.op("act", fn, reads=reads, writes=writes)

        def pl(fn, reads, writes):
            S.op("pool", fn, reads=reads, writes=writes)

        def pe(fn, reads, writes):
            S.op("pe", fn, reads=reads, writes=writes)

        S.op("sp", lambda q: q.dma_start(out=par[:], in_=C.s5p[l]), writes=[b_par], dma=True)
        S.op("sp", lambda q: q.dma_start(out=Bre[:], in_=C.s5_bre[l]), writes=[b_B], dma=True)
        S.op("sp", lambda q: q.dma_start(out=Bim[:], in_=C.s5_bim[l]), writes=[b_B], dma=True)
        S.op("sp", lambda q: q.dma_start(out=Cre[:], in_=C.s5_cre[l]), writes=[b_Cf], dma=True)
        S.op("sp", lambda q: q.dma_start(out=Cim[:], in_=C.s5_cim[l]), writes=[b_Cf], dma=True)
        S.op("sp", lambda q: q.dma_start(out=dsk[:], in_=C.s5_dsk[l]), writes=[b_w], dma=True)
        S.op("sp", lambda q: q.dma_start(out=wgl[:], in_=C.s5_wglu[l].rearrange("(c p) f -> p c f", p=128)), writes=[b_w], dma=True)
        for ct in range(2):
            S.op("sp", lambda q, ct=ct: q.dma_start(out=u32[:, ct, :], in_=C.zu[ct * 128:(ct + 1) * 128, :]),
                 reads=[C.b_zu[ct]], writes=[b_u[ct]], dma=True)
            pl(lambda v, ct=ct: v.tensor_copy(out=uS[:, ct, :, :], in_=u32[:, ct, :].rearrange("p (k j) -> p j k", j=LB)), [b_u[ct]], [b_uS[ct]])
        act(lambda a: a.copy(out=wglb[:], in_=wgl[:]), [b_w], [b_w])
        dve(lambda v: v.memset(hpi[:], math.pi / 2.0), [], [b_sm])
        LR, LI, LDT = par[:, :, 0], par[:, :, 1], par[:, :, 2]
        sl = lambda i: sm[:, i, :]
        DT, MAG, ANG, K_, R_, AR, SN, CS, ABR, ABI, DEN, T1_, T2_, FRE, FIM = range(15)
        act(lambda a: a.activation(out=sl(DT), in_=LDT, func=AF.Exp), [b_par], [b_sm])
        dve(lambda v: v.tensor_tensor(out=sl(MAG), in0=LR, in1=sl(DT), op=ALU.mult), [b_par, b_sm], [b_sm])
        act(lambda a: a.activation(out=sl(MAG), in_=sl(MAG), func=AF.Exp), [b_sm], [b_sm])
        dve(lambda v: v.tensor_tensor(out=sl(ANG), in0=LI, in1=sl(DT), op=ALU.mult), [b_par, b_sm], [b_sm])
        dve(lambda v: v.tensor_scalar(out=sl(K_), in0=sl(ANG), scalar1=1.0 / TWO_PI, scalar2=MAGIC, op0=ALU.mult, op1=ALU.add), [b_sm], [b_sm])
        dve(lambda v: v.tensor_scalar(out=sl(K_), in0=sl(K_), scalar1=-MAGIC, scalar2=None, op0=ALU.add), [b_sm], [b_sm])
        dve(lambda v: v.scalar_tensor_tensor(out=sl(R_), in0=sl(K_), scalar=-CW1, in1=sl(ANG), op0=ALU.mult, op1=ALU.add), [b_sm], [b_sm])
        dve(lambda v: v.scalar_tensor_tensor(out=sl(R_), in0=sl(K_), scalar=-CW2, in1=sl(R_), op0=ALU.mult, op1=ALU.add), [b_sm], [b_sm])
        dve(lambda v: v.tensor_scalar(out=sl(R_), in0=sl(R_), scalar1=math.pi, scalar2=-math.pi, op0=ALU.min, op1=ALU.max), [b_sm], [b_sm])
        act(lambda a: a.activation(out=sl(AR), in_=sl(R_), func=AF.Abs), [b_sm], [b_sm])
        act(lambda a: a.activation(out=sl(SN), in_=sl(R_), func=AF.Sin), [b_sm], [b_sm])
        act(lambda a: a.activation(out=sl(CS), in_=sl(AR), func=AF.Sin, scale=-1.0, bias=hpi[:]), [b_sm], [b_sm])
        dve(lambda v: v.tensor_tensor(out=sl(ABR), in0=sl(MAG), in1=sl(CS), op=ALU.mult), [b_sm], [b_sm])
        dve(lambda v: v.tensor_tensor(out=sl(ABI), in0=sl(MAG), in1=sl(SN), op=ALU.mult), [b_sm], [b_sm])
        dve(lambda v: v.tensor_tensor(out=sl(DEN), in0=LR, in1=LR, op=ALU.mult), [b_par, b_sm], [b_sm])
        dve(lambda v: v.tensor_tensor(out=sl(T1_), in0=LI, in1=LI, op=ALU.mult), [b_par, b_sm], [b_sm])
        dve(lambda v: v.tensor_tensor(out=sl(DEN), in0=sl(DEN), in1=sl(T1_), op=ALU.add), [b_sm], [b_sm])
        dve(lambda v: v.reciprocal(out=sl(DEN), in_=sl(DEN)), [b_sm], [b_sm])
        dve(lambda v: v.tensor_scalar(out=sl(T1_), in0=sl(ABR), scalar1=-1.0, scalar2=None, op0=ALU.add), [b_sm], [b_sm])
        dve(lambda v: v.tensor_tensor(out=sl(FRE), in0=sl(T1_), in1=LR, op=ALU.mult), [b_par, b_sm], [b_sm])
        dve(lambda v: v.tensor_tensor(out=sl(T2_), in0=sl(ABI), in1=LI, op=ALU.mult), [b_par, b_sm], [b_sm])
        dve(lambda v: v.tensor_tensor(out=sl(FRE), in0=sl(FRE), in1=sl(T2_), op=ALU.add), [b_sm], [b_sm])
        dve(lambda v: v.tensor_tensor(out=sl(FRE), in0=sl(FRE), in1=sl(DEN), op=ALU.mult), [b_sm], [b_sm])
        dve(lambda v: v.tensor_tensor(out=sl(FIM), in0=sl(ABI), in1=LR, op=ALU.mult), [b_par, b_sm], [b_sm])
        dve(lambda v: v.tensor_tensor(out=sl(T2_), in0=sl(T1_), in1=LI, op=ALU.mult), [b_par, b_sm], [b_sm])
        dve(lambda v: v.tensor_tensor(out=sl(FIM), in0=sl(FIM), in1=sl(T2_), op=ALU.subtract), [b_sm], [b_sm])
        dve(lambda v: v.tensor_tensor(out=sl(FIM), in0=sl(FIM), in1=sl(DEN), op=ALU.mult), [b_sm], [b_sm])
        dve(lambda v: v.tensor_copy(out=wre[:, 0, :], in_=sl(CS)), [b_sm], [b_wp])
        dve(lambda v: v.tensor_copy(out=wim[:, 0, :], in_=sl(SN)), [b_sm], [b_wp])
        for k in range(NW):
            dve(lambda v, k=k: v.tensor_tensor(out=sl(T1_), in0=wre[:, k, :], in1=wre[:, k, :], op=ALU.mult), [b_wp, b_sm], [b_sm])
            dve(lambda v, k=k: v.tensor_tensor(out=sl(T2_), in0=wim[:, k, :], in1=wim[:, k, :], op=ALU.mult), [b_wp, b_sm], [b_sm])
            dve(lambda v, k=k: v.tensor_tensor(out=wre[:, k + 1, :], in0=sl(T1_), in1=sl(T2_), op=ALU.subtract), [b_sm, b_wp], [b_wp])
            dve(lambda v, k=k: v.tensor_tensor(out=sl(T1_), in0=wre[:, k, :], in1=wim[:, k, :], op=ALU.mult), [b_wp, b_sm], [b_sm])
            dve(lambda v, k=k: v.tensor_scalar(out=wim[:, k + 1, :], in0=sl(T1_), scalar1=2.0, scalar2=None, op0=ALU.mult), [b_sm, b_wp], [b_wp])

        dve(lambda v: v.memset(pwr[:, 0, :], 1.0), [], [b_pw])
        dve(lambda v: v.memset(pwi[:, 0, :], 0.0), [], [b_pw])
        for q_ in range(LB):
            dve(lambda v, q_=q_: v.tensor_tensor(out=sl(T1_), in0=pwr[:, q_, :], in1=sl(ABR), op=ALU.mult), [b_pw, b_sm], [b_sm])
            dve(lambda v, q_=q_: v.tensor_tensor(out=sl(T2_), in0=pwi[:, q_, :], in1=sl(ABI), op=ALU.mult), [b_pw, b_sm], [b_sm])
            dve(lambda v, q_=q_: v.tensor_tensor(out=pwr[:, q_ + 1, :], in0=sl(T1_), in1=sl(T2_), op=ALU.subtract), [b_sm, b_pw], [b_pw])
            dve(lambda v, q_=q_: v.tensor_tensor(out=sl(T1_), in0=pwr[:, q_, :], in1=sl(ABI), op=ALU.mult), [b_pw, b_sm], [b_sm])
            dve(lambda v, q_=q_: v.tensor_tensor(out=sl(T2_), in0=pwi[:, q_, :], in1=sl(ABR), op=ALU.mult), [b_pw, b_sm], [b_sm])
            dve(lambda v, q_=q_: v.tensor_tensor(out=pwi[:, q_ + 1, :], in0=sl(T1_), in1=sl(T2_), op=ALU.add), [b_sm, b_pw], [b_pw])
        dve(lambda v: v.tensor_tensor(out=r8[:], in0=sl(MAG), in1=sl(MAG), op=ALU.mult), [b_sm], [b_r8])
        dve(lambda v: v.tensor_tensor(out=r8[:], in0=r8[:], in1=r8[:], op=ALU.mult), [b_r8], [b_r8])
        dve(lambda v: v.tensor_tensor(out=r8[:], in0=r8[:], in1=r8[:], op=ALU.mult), [b_r8], [b_r8])
        dve(lambda v: v.memset(Ec[:, :, 0:1], 1.0), [], [b_Eall])
        dve(lambda v: v.memset(Es[:, :, 0:1], 0.0), [], [b_Eall])
        for k in range(8):
            n = 1 << k
            wrb = wre[:, k + 3, :].unsqueeze(2).to_broadcast([128, 8, n])
            wib = wim[:, k + 3, :].unsqueeze(2).to_broadcast([128, 8, n])
            dve(lambda v, n=n, wib=wib: v.tensor_tensor(out=tA[:, :, 0:n], in0=Es[:, :, 0:n], in1=wib, op=ALU.mult), [b_Eall, b_wp], [b_tA])
            dve(lambda v, n=n, wrb=wrb: v.tensor_tensor(out=tB[:, :, 0:n], in0=Ec[:, :, 0:n], in1=wrb, op=ALU.mult), [b_Eall, b_wp], [b_tB])
            dve(lambda v, n=n: v.tensor_tensor(out=Ec[:, :, n:2 * n], in0=tB[:, :, 0:n], in1=tA[:, :, 0:n], op=ALU.subtract), [b_tA, b_tB], [b_Eall])
            dve(lambda v, n=n, wib=wib: v.tensor_tensor(out=tA[:, :, 0:n], in0=Ec[:, :, 0:n], in1=wib, op=ALU.mult), [b_Eall, b_wp], [b_tA])
            dve(lambda v, n=n, wrb=wrb: v.tensor_tensor(out=tB[:, :, 0:n], in0=Es[:, :, 0:n], in1=wrb, op=ALU.mult), [b_Eall, b_wp], [b_tB])
            dve(lambda v, n=n: v.tensor_tensor(out=Es[:, :, n:2 * n], in0=tB[:, :, 0:n], in1=tA[:, :, 0:n], op=ALU.add), [b_tA, b_tB], [b_Eall])
        frb = sm[:, FRE, :].unsqueeze(2).to_broadcast([128, 8, 128])
        fib = sm[:, FIM, :].unsqueeze(2).to_broadcast([128, 8, 128])
        for (dst, A_, B_, sgn) in ((MbR, Bre, Bim, ALU.subtract), (MbI, Bim, Bre, ALU.add)):
            dve(lambda v, B_=B_: v.tensor_tensor(out=MtA[:], in0=B_[:], in1=fib, op=ALU.mult), [b_B, b_sm], [b_Mt])
            dve(lambda v, A_=A_: v.tensor_tensor(out=MtB[:], in0=A_[:], in1=frb, op=ALU.mult), [b_B, b_sm], [b_Mt])
            dve(lambda v, dst=dst, sgn=sgn: v.tensor_tensor(out=dst[:], in0=MtB[:], in1=MtA[:], op=sgn), [b_Mt], [b_Mb])
        act(lambda a: a.copy(out=BbR[:], in_=MbR[:]), [b_Mb], [b_Bb])
        act(lambda a: a.copy(out=BbI[:], in_=MbI[:]), [b_Mb], [b_Bb])

        def cmul(eng, q_, Xr, Xi, outR, outI, negI, bx, bo, tmpa, tmpb, b_tmp):
            prb = pwr[:, q_, :].unsqueeze(2).to_broadcast([128, 8, 128])
            pib = pwi[:, q_, :].unsqueeze(2).to_broadcast([128, 8, 128])
            op = (lambda fn, r, w: S.op(eng, fn, reads=r, writes=w))
            op(lambda v: v.tensor_tensor(out=tmpa[:], in0=Xr[:], in1=prb, op=ALU.mult), [bx, b_pw], [b_tmp])
            op(lambda v: v.tensor_tensor(out=tmpb[:], in0=Xi[:], in1=pib, op=ALU.mult), [bx, b_pw], [b_tmp])
            op(lambda v: v.tensor_tensor(out=outR, in0=tmpa[:], in1=tmpb[:], op=ALU.subtract), [b_tmp], [bo])
            op(lambda v: v.tensor_tensor(out=tmpa[:], in0=Xi[:], in1=prb, op=ALU.mult), [bx, b_pw], [b_tmp])
            op(lambda v: v.tensor_tensor(out=tmpb[:], in0=Xr[:], in1=pib, op=ALU.mult), [bx, b_pw], [b_tmp])
            if negI:
                op(lambda v: v.scalar_tensor_tensor(out=outI, in0=tmpa[:], scalar=-1.0, in1=tmpb[:], op0=ALU.mult, op1=ALU.subtract),
                   [b_tmp], [bo])
            else:
                op(lambda v: v.tensor_tensor(out=outI, in0=tmpa[:], in1=tmpb[:], op=ALU.add), [b_tmp], [bo])

        for q_ in range(LB + 1):
            if q_ % 3 == 2:
                cmul("pool", q_, Cre, Cim, CaR[:, q_, :, :], nCaI[:, q_, :, :], False, b_Cf, b_Ca, PtA, PtB, b_Pt)
                pl(lambda v, q_=q_: v.tensor_scalar(out=nCaI[:, q_, :, :], in0=nCaI[:, q_, :, :], scalar1=-1.0, scalar2=0.0, op0=ALU.mult, op1=ALU.add),
                   [b_Ca], [b_Ca])
            else:
                cmul("dve", q_, Cre, Cim, CaR[:, q_, :, :], nCaI[:, q_, :, :], True, b_Cf, b_Ca, MtA, MtB, b_Mt)
        for j in range(LB):
            q_ = LB - 1 - j
            vi = j % 2
            for ri in range(2):
                pass
            cmul("dve", q_, MbR, MbI, vst[0][:], vst[1][:], False, b_Mb, b_vst[0], MtA, MtB, b_Mt)
            b_vst[1].last_w = b_vst[0].last_w
            for ri in range(2):
                for st in range(8):
                    pe(lambda p, st=st, ri=ri: p.transpose(psT[:, st, :], vst[ri][:, st, :], C.identb[:]), [b_vst[0], C.b_const], [b_psT])
                act(lambda a, j=j, ri=ri: a.copy(out=W1T[:, j, ri, :, :], in_=psT[:]), [b_psT], [b_W1T])
        for ct in range(2):
            for d0 in (0, 4):
                for dd in range(4):
                    d = d0 + dd
                    for jj in range(4):
                        st = ct * 4 + jj
                        pe(lambda p, st=st, d=d, dd=dd, jj=jj: p.matmul(psK[:, dd, :], lhsT=BbR[:, st, :], rhs=CaR[:, d, st, :],
                                                                     start=(jj == 0), stop=False), [b_Bb, b_Ca], [b_psK])
                        pe(lambda p, st=st, d=d, dd=dd, jj=jj: p.matmul(psK[:, dd, :], lhsT=BbI[:, st, :], rhs=nCaI[:, d, st, :],
                                                                     start=False, stop=(jj == 3)), [b_Bb, b_Ca], [b_psK])
                act(lambda a, ct=ct, d0=d0: a.copy(out=Kt[:, ct, d0:d0 + 4, :], in_=psK[:]), [b_psK], [b_Kt])
        dve(lambda v: v.memset(Hs[:, :, :, 0:1], 0.0), [], b_Hs)
        for st in range(8):
            ct = st // 4
            for ri in range(2):
                for j in range(LB):
                    pe(lambda p, st=st, ri=ri, j=j, ct=ct: p.matmul(psSb[:, ri, :], lhsT=W1T[:, j, ri, st, :], rhs=uS[:, ct, j, :],
                                                                 start=(j == 0), stop=(j == LB - 1)), [b_W1T, b_uS[ct]], [b_psS])
            Xr, Xi = psSb[:, 0, :], psSb[:, 1, :]
            dve(lambda v, st=st, Xr=Xr: v.tensor_tensor(out=t1[:], in0=Xr, in1=Ec[:, st, :], op=ALU.mult), [b_psS, b_Eall], [b_t1])
            dve(lambda v, st=st, Xi=Xi: v.tensor_tensor(out=t2[:], in0=Xi, in1=Es[:, st, :], op=ALU.mult), [b_psS, b_Eall], [b_t2])
            dve(lambda v: v.tensor_tensor(out=xr[:], in0=t1[:], in1=t2[:], op=ALU.add), [b_t1, b_t2], [b_xr])
            dve(lambda v, st=st, Xi=Xi: v.tensor_tensor(out=t1[:], in0=Xi, in1=Ec[:, st, :], op=ALU.mult), [b_psS, b_Eall], [b_t1])
            dve(lambda v, st=st, Xr=Xr: v.tensor_tensor(out=t2[:], in0=Xr, in1=Es[:, st, :], op=ALU.mult), [b_psS, b_Eall], [b_t2])
            dve(lambda v: v.tensor_tensor(out=xi[:], in0=t1[:], in1=t2[:], op=ALU.subtract), [b_t1, b_t2], [b_xi])
            dve(lambda v, st=st: v.tensor_tensor_scan(out=gr[:], data0=C.bcast(r8[:, st:st + 1], NBLK), data1=xr[:], initial=0.0,
                                                      op0=ALU.mult, op1=ALU.add), [b_r8, b_xr], [b_gr])
            dve(lambda v, st=st: v.tensor_tensor_scan(out=gi[:], data0=C.bcast(r8[:, st:st + 1], NBLK), data1=xi[:], initial=0.0,
                                                      op0=ALU.mult, op1=ALU.add), [b_r8, b_xi], [b_gi])
            pl(lambda v, st=st: v.tensor_tensor(out=p1[:], in0=gr[:], in1=Ec[:, st, :], op=ALU.mult), [b_gr, b_Eall], [b_p1])
            pl(lambda v, st=st: v.tensor_tensor(out=p2[:], in0=gi[:], in1=Es[:, st, :], op=ALU.mult), [b_gi, b_Eall], [b_p2])
            pl(lambda v, st=st: v.tensor_tensor(out=Hs[:, st, 0, 1:NBLK], in0=p1[:, 0:NBLK - 1], in1=p2[:, 0:NBLK - 1], op=ALU.subtract),
               [b_p1, b_p2], [b_Hs[st]])
            pl(lambda v, st=st: v.tensor_tensor(out=p1[:], in0=gi[:], in1=Ec[:, st, :], op=ALU.mult), [b_gi, b_Eall], [b_p1])
            pl(lambda v, st=st: v.tensor_tensor(out=p2[:], in0=gr[:], in1=Es[:, st, :], op=ALU.mult), [b_gr, b_Eall], [b_p2])
            pl(lambda v, st=st: v.tensor_tensor(out=Hs[:, st, 1, 1:NBLK], in0=p1[:, 0:NBLK - 1], in1=p2[:, 0:NBLK - 1], op=ALU.add),
               [b_p1, b_p2], [b_Hs[st]])
        yi = 0
        for qr in range(4):
            ks = slice(qr * 64, (qr + 1) * 64)
            ts = slice(qr * 512, (qr + 1) * 512)
            for ct in range(2):
                py = yi % 2; yi += 1
                first = True
                for tau in range(LB):
                    outap = psY[py][:, tau * 64:(tau + 1) * 64]
                    for jj in range(4):
                        st = ct * 4 + jj
                        pe(lambda p, outap=outap, tau=tau, st=st, ks=ks, fm=first: p.matmul(
                            outap, lhsT=CaR[:, tau + 1, st, :], rhs=Hs[:, st, 0, ks], start=fm, stop=False, skip_group_check=True),
                            [b_Ca, b_Hs[st]], [b_psY[py]])
                        first = False
                        pe(lambda p, outap=outap, tau=tau, st=st, ks=ks: p.matmul(
                            outap, lhsT=nCaI[:, tau + 1, st, :], rhs=Hs[:, st, 1, ks], start=False, stop=False, skip_group_check=True),
                            [b_Ca, b_Hs[st]], [b_psY[py]])
                    for j in range(tau + 1):
                        pe(lambda p, outap=outap, tau=tau, j=j, ct=ct, ks=ks: p.matmul(
                            outap, lhsT=Kt[:, ct, tau - j, :], rhs=uS[:, ct, j, ks], start=False, stop=(j == tau), skip_group_check=True),
                            [b_Kt, b_uS[ct]], [b_psY[py]])
                dve(lambda v, ct=ct, py=py, ts=ts: v.scalar_tensor_tensor(
                    out=yv[:].rearrange("p (k t) -> p k t", t=LB), in0=u32[:, ct, ts].rearrange("p (k t) -> p k t", t=LB), scalar=dsk[:, ct:ct + 1],
                    in1=psY[py][:].rearrange("p (t k) -> p k t", t=LB), op0=ALU.mult, op1=ALU.add), [b_u[ct], b_w, b_psY[py]], [b_yv])
                gelu_tanh(S, yv[:], b_yv, yt[:], b_yt, yg[:, ct, :], b_yg[ct])
            for j in range(2):
                pe(lambda p, j=j: p.matmul(psG[0][:], lhsT=wglb[:, 0, j * 128:(j + 1) * 128], rhs=yg[:, 0, :], start=True, stop=False), [b_w] + b_yg, [b_psG[0]])
                pe(lambda p, j=j: p.matmul(psG[0][:], lhsT=wglb[:, 1, j * 128:(j + 1) * 128], rhs=yg[:, 1, :], start=False, stop=True), [b_w] + b_yg, [b_psG[0]])
                pe(lambda p, j=j: p.matmul(psG[1][:], lhsT=wglb[:, 0, 256 + j * 128:256 + (j + 1) * 128], rhs=yg[:, 0, :], start=True, stop=False), [b_w] + b_yg, [b_psG[1]])
                pe(lambda p, j=j: p.matmul(psG[1][:], lhsT=wglb[:, 1, 256 + j * 128:256 + (j + 1) * 128], rhs=yg[:, 1, :], start=False, stop=True), [b_w] + b_yg, [b_psG[1]])
                act(lambda a: a.activation(out=sgl[:], in_=psG[1][:], func=AF.Sigmoid), [b_psG[1]], [b_sgl])
                dve(lambda v, j=j: v.tensor_tensor(out=og[j][:], in0=psG[0][:], in1=sgl[:], op=ALU.mult), [b_psG[0], b_sgl], [b_og[j]])
                S.op("sp", lambda q, j=j, ts=ts: q.dma_start(out=C.mixT[j * 128:(j + 1) * 128, ts], in_=og[j][:]),
                     reads=[b_og[j]], writes=[C.b_mixT[j]], dma=True)


NBT = 2688
OFF_BS, OFF_BW, OFF_BC = 0, 256, 640


def phase_btab(C):
    nc, S = C.nc, C.S
    with ExitStack() as es:
        raw = [es.enter_context(nc.sbuf_tensor("BT_raw%d" % i, [128, NBT], F32)) for i in range(2)]
        cv = es.enter_context(nc.sbuf_tensor("BT_cv", [128, 8], F32))
        ob = [es.enter_context(nc.sbuf_tensor("BT_ob%d" % i, [128, NBT], BF16)) for i in range(2)]
        b_raw = [Buf() for _ in range(2)]; b_ob = [Buf() for _ in range(2)]; b_cv = Buf()
        S.op("sp", lambda q: q.dma_start(out=cv[:], in_=C.cvec[:, :]), writes=[b_cv], dma=True)
        for hq in range(8):
            i = hq % 2
            S.op("sp", lambda q, hq=hq, i=i: q.dma_start(out=raw[i][:], in_=C.btab_raw[:, hq, :]), writes=[b_raw[i]], dma=True)
            S.op("dve", lambda v, hq=hq, i=i: v.tensor_scalar(out=ob[i][:], in0=raw[i][:], scalar1=cv[:, hq:hq + 1], scalar2=None, op0=ALU.subtract),
                 reads=[b_raw[i], b_cv], writes=[b_ob[i]])
            S.op("sp", lambda q, hq=hq, i=i: q.dma_start(out=C.btab[:, hq, :], in_=ob[i][:]), reads=[b_ob[i]], writes=[C.b_btab], dma=True)


def phase_NSA(C, s, l):
    nc, S = C.nc, C.S
    with ExitStack() as es:
        def T(name, shape, dt=F32):
            return es.enter_context(nc.sbuf_tensor("N_" + name, shape, dt))
        qT = T("qT", [128, 4, SEQ], BF16); b_q = [Buf() for _ in range(4)]
        kcT = T("kcT", [128, SEQ + 32], BF16); vcT = T("vcT", [128, SEQ + 32], BF16); b_kc = Buf(); b_vc = Buf()
        ksT = [T("ksT%d" % i, [128, SEQ], BF16) for i in range(2)]; kwT = [T("kwT%d" % i, [128, SEQ], BF16) for i in range(2)]; b_ks = Buf(); b_kw = Buf()
        VS = T("VS", [128, NT, 2, 65], BF16); VW = T("VW", [128, NT, 2, 65], BF16); b_VS = Buf(); b_VW = Buf()
        btab = T("btab", [128, 8, NBT], BF16); b_bt = Buf()
        zg = T("zg", [128, NT, 24]); cmask = T("cmask", [128, NT, 32]); b_zg = Buf(); b_cm = Buf()
        expb = T("expb", [128, NT, 128], BF16); vcc = T("vcc", [128, 33], BF16); b_cst = Buf()
        acc = T("acc", [128, NT, 8, 64]); b_acc = [[Buf() for _ in range(8)] for _ in range(4)]
        imp = T("imp", [128, NT, 2, 32]); b_imp = [Buf() for _ in range(4)]
        negT = T("negT", [128, 2, SEQ], BF16); b_neg = [[Buf() for _ in range(2)] for _ in range(2)]
        PT = [T("PT%d" % i, [128, 512], BF16) for i in range(4)]; b_PT = [Buf() for _ in range(4)]
        w1f = T("w1f", [128, 32, 64]); w1b = [[T("w1b%d_%d" % (i, j), [128, 32, 64], BF16) for j in range(2)] for i in range(2)]; b_w1f = Buf(); b_w1 = Buf()
        pef = T("pef", [128, 2, 32]); peb = T("peb", [128, 2, 32], BF16)
        w2pf = T("w2pf", [64, 2, 128]); w2pb = T("w2pb", [64, 2, 128], BF16); w2vf = T("w2vf", [64, 64]); w2vb = T("w2vb", [64, 64], BF16)
        cst = T("cst", [64, 2]); xm = T("xm", [64, 256]); xt = T("xt", [64, 256]); hmid = [T("hmid%d" % i, [64, 256], BF16) for i in range(2)]
        b_cstv = Buf(); b_xm = Buf(); b_xt = Buf(); b_hm = [Buf() for _ in range(2)]
        kc2 = T("kc2", [128, 2, 128], BF16); VC = T("VC", [128, 2, 97], BF16); b_kc2 = Buf(); b_VC = Buf()
        rd = T("rd", [128, 4]); wg = T("wg", [128, 4]); tmpo = T("tmpo", [128, 4, 64]); tmpi = T("tmpi", [128, 4, 32])
        b_rd = Buf(); b_wg = Buf(); b_tmpo = Buf(); b_tmpi = Buf()
        sc = T("sc", [128, 32]); top8 = T("top8", [128, 8]); nsb = T("nsb", [128, 32], BF16); b_sc = Buf(); b_top = Buf(); b_nsb = Buf()
        accb = T("accb", [128, 512], BF16); b_accb = Buf()
        ost = T("ost", [128, 4, SEQ], BF16); b_ost = [Buf() for _ in range(NT)]
        psS = [es.enter_context(nc.psum_tensor("N_psS%d" % i, [128, 512], F32)) for i in range(4)]; b_psS = [Buf() for _ in range(4)]
        psO = [es.enter_context(nc.psum_tensor("N_psO%d" % i, [128, 512], F32)) for i in range(2)]; b_psO = [Buf() for _ in range(2)]
        psM = es.enter_context(nc.psum_tensor("N_psM", [128, 512], F32)); b_psM = Buf()
        psT = es.enter_context(nc.psum_tensor("N_psT", [128, 1024], BF16)); b_psT = Buf()

        def dve(fn, reads, writes):
            S.op("dve", fn, reads=reads, writes=writes)

        def act(fn, reads, writes):
            S.op("act", fn, reads=reads, writes=writes)

        def pe(fn, reads, writes):
            S.op("pe", fn, reads=reads, writes=writes)

        def dma(fn, reads, writes):
            S.op("sp", fn, reads=reads, writes=writes, dma=True)

        S.op("pool", lambda v: v.memset(negT[:, :, :], 0.0), reads=[], writes=[b_neg[hh_][kk_] for hh_ in range(2) for kk_ in range(2)])
        zq = C.zqk.rearrange("(c p) t -> p c t", p=128)
        dve(lambda v: v.memset(kcT[:, SEQ:SEQ + 32], 0.0), [], [b_kc])
        dve(lambda v: v.memset(vcT[:, SEQ:SEQ + 32], 0.0), [], [b_vc])
        dma(lambda q: q.dma_start(out=kcT[:, 0:SEQ], in_=zq[:, 4, :]), [C.b_zqk[4]], [b_kc])
        dma(lambda q: q.dma_start(out=vcT[:, 0:SEQ], in_=zq[:, 5, :]), [C.b_zqk[5]], [b_vc])
        for wi in range(2):
            for hh in range(2):
                dma(lambda q, wi=wi, hh=hh: q.dma_start(out=w1f[:], in_=C.nsa_w1[l, wi, hh]), [], [b_w1f])
                act(lambda a, wi=wi, hh=hh: a.copy(out=w1b[wi][hh][:], in_=w1f[:]), [b_w1f], [b_w1])
        dma(lambda q: q.dma_start(out=pef[:], in_=C.nsa_peT[l]), [], [b_w1f])
        dma(lambda q: q.dma_start(out=w2pf[:], in_=C.nsa_w2pad[l]), [], [b_w1f])
        dma(lambda q: q.dma_start(out=w2vf[:], in_=C.nsa_w2v[l]), [], [b_w1f])
        dve(lambda v: v.tensor_copy(out=peb[:], in_=pef[:]), [b_w1f], [b_w1])
        dve(lambda v: v.tensor_copy(out=w2pb[:], in_=w2pf[:]), [b_w1f], [b_w1])
        dve(lambda v: v.tensor_copy(out=w2vb[:], in_=w2vf[:]), [b_w1f], [b_w1])
        dma(lambda q: q.dma_start(out=vcc[:], in_=C.vca_const[:, :]), [], [b_cst])
        S.op("pool", lambda v: v.memset(expb[:, :, :], 0.0), reads=[], writes=[b_cst])
        dma(lambda q: q.dma_start(out=expb[0:32, :, :], in_=C.expand[:, :, :]), [], [b_cst])
        dma(lambda q: q.dma_start(out=cmask[:], in_=C.cmask[:, :, :]), [], [b_cm])
        dma(lambda q: q.dma_start(out=zg[:].rearrange("p t g -> p (t g)"), in_=C.zg[:, :]), [C.b_zg], [b_zg])
        for j in range(4):
            dma(lambda q, j=j: q.dma_start(out=qT[:, j, :], in_=zq[:, j, :]), [C.b_zqk[j]], [b_q[j]])
        for hh in range(2):
            lo, hi = 64 * hh, 64 * hh + 64
            olo, ohi = 64 * (1 - hh), 64 * (1 - hh) + 64
            S.op("pool", lambda v, hh=hh, olo=olo, ohi=ohi: v.memset(ksT[hh][olo:ohi, :], 0.0), reads=[], writes=[b_ks])
            S.op("pool", lambda v, hh=hh, olo=olo, ohi=ohi: v.memset(kwT[hh][olo:ohi, :], 0.0), reads=[], writes=[b_kw])
            dma(lambda q, hh=hh, lo=lo, hi=hi: q.dma_start(out=ksT[hh][lo:hi, :], in_=zq[lo:hi, 6, :]), [C.b_zqk[6]], [b_ks])
            dma(lambda q, hh=hh, lo=lo, hi=hi: q.dma_start(out=kwT[hh][lo:hi, :], in_=zq[lo:hi, 7, :]), [C.b_zqk[7]], [b_kw])
        zv = C.zv.rearrange("(t p) a h d -> p t a h d", p=128)
        for hh in range(2):
            dma(lambda q, hh=hh: q.dma_start(out=VS[:, :, hh, 0:64], in_=zv[:, :, 0, hh, :]), [C.b_zv], [b_VS])
            dma(lambda q, hh=hh: q.dma_start(out=VW[:, :, hh, 0:64], in_=zv[:, :, 1, hh, :]), [C.b_zv], [b_VW])
        dve(lambda v: v.memset(VS[:, :, :, 64:65], 1.0), [], [b_VS])
        dve(lambda v: v.memset(VW[:, :, :, 64:65], 1.0), [], [b_VW])
        for hq in range(8):
            dma(lambda q, hq=hq: q.dma_start(out=btab[:, hq, :], in_=C.btab[:, hq, :]), [C.b_btab], [b_bt])

        if C.dbg.get("nsa_stop", 99) <= 1:
            return
        kcS = T("kcS", [128, 32, 128], BF16); b_kcS = Buf()
        for wi, src, b_src in ((0, kcT, b_kc), (1, vcT, b_vc)):
            S.op("pool", lambda v, src=src: v.tensor_copy(out=kcS[:], in_=src[:, 0:SEQ].rearrange("p (n l) -> p l n", l=16)[:, :, :].unsqueeze(1)) if False else
                 v.tensor_copy(out=kcS[:, 0:16, :], in_=src[:, 0:SEQ].rearrange("p (n l) -> p l n", l=16)), reads=[b_src], writes=[b_kcS])
            S.op("pool", lambda v, src=src: v.tensor_copy(out=kcS[:, 16:32, :], in_=src[:, 16:SEQ + 16].rearrange("p (n l) -> p l n", l=16)),
                 reads=[b_src], writes=[b_kcS])
            for ll in range(32):
                pe(lambda p, wi=wi, ll=ll: p.matmul(psM[0:64, 256:257], lhsT=w1b[wi][0][:, ll, :], rhs=peb[:, wi, ll:ll + 1],
                                                   start=(ll == 0), stop=(ll == 31)), [b_w1], [b_psM])
            act(lambda a, wi=wi: a.copy(out=cst[:, wi:wi + 1], in_=psM[0:64, 256:257]), [b_psM], [b_cstv])
            if C.dbg.get("nsa_stop", 99) <= 1.2:
                return
            for hh in range(2):
                for ll in range(32):
                    pe(lambda p, wi=wi, hh=hh, ll=ll, src=src: p.matmul(
                        psM[0:64, hh * 128:(hh + 1) * 128], lhsT=w1b[wi][hh][:, ll, :],
                        rhs=kcS[:, ll, :], start=(ll == 0), stop=(ll == 31)),
                        [b_w1, b_kcS], [b_psM])
            if C.dbg.get("nsa_stop", 99) <= 1.5:
                return
            act(lambda a, wi=wi: a.activation(out=xm[:], in_=psM[0:64, 0:256], func=AF.Identity, bias=cst[:, wi:wi + 1]),
                [b_psM, b_cstv], [b_xm])
            if C.dbg.get("nsa_stop", 99) <= 1.6:
                return
            gelu_tanh(S, xm[:], b_xm, xt[:], b_xt, hmid[wi][:], b_hm[wi])
            if C.dbg.get("nsa_stop", 99) <= 1.7:
                return
            if wi == 0:
                pe(lambda p: p.matmul(psM[:, 384:512], lhsT=w2pb[:, 0, :], rhs=hmid[0][:, 0:128], start=True, stop=False), [b_w1, b_hm[0]], [b_psM])
                pe(lambda p: p.matmul(psM[:, 384:512], lhsT=w2pb[:, 1, :], rhs=hmid[0][:, 128:256], start=False, stop=True), [b_w1, b_hm[0]], [b_psM])
                dve(lambda v: v.memset(kc2[:], 0.0), [], [b_kc2])
                act(lambda a: a.copy(out=kc2[0:64, 0, :], in_=psM[0:64, 384:512]), [b_psM], [b_kc2])
                act(lambda a: a.copy(out=kc2[64:128, 1, :], in_=psM[64:128, 384:512]), [b_psM], [b_kc2])
            else:
                for hh in range(2):
                    pe(lambda p, hh=hh: p.matmul(psM[:, 384 + hh * 64:384 + (hh + 1) * 64], lhsT=hmid[1][:, hh * 128:(hh + 1) * 128], rhs=w2vb[:],
                                                start=True, stop=True), [b_w1, b_hm[1]], [b_psM])
                act(lambda a: a.copy(out=VC[:, :, 0:64], in_=psM[:, 384:512].rearrange("p (h d) -> p h d", h=2)), [b_psM], [b_VC])
        for hh in range(2):
            dve(lambda v, hh=hh: v.tensor_copy(out=VC[:, hh, 64:97], in_=vcc[:]), [b_cst], [b_VC])

        if C.dbg.get("nsa_stop", 99) <= 2:
            return

        C.dump("VC", VC[:].rearrange("p h c -> p (h c)"), [128, 194], BF16, [b_VC])
        si = [0]
        oi = [0]
        brs = C.dbg.get("nsa_branches", (0, 1, 2))
        if 0 not in brs:
            dve(lambda v: v.memset(acc[:], 0.0), [], [b_acc[nb_][hq_] for nb_ in range(4) for hq_ in range(8)])

        def evac(po, W, h, g, nb, branch, first):
            hq = 4 * h + g
            tts = slice(4 * nb, 4 * nb + 4)
            ps3 = psO[po][:, 0:4 * W].rearrange("p (t c) -> p t c", c=W)
            col = h * 12 + g * 3 + branch
            dve(lambda v: v.tensor_scalar(out=rd[:], in0=ps3[:, :, 64], scalar1=1e-30, scalar2=None, op0=ALU.max), [b_psO[po]], [b_rd])
            dve(lambda v: v.reciprocal(out=rd[:], in_=rd[:]), [b_rd], [b_rd])
            dve(lambda v: v.tensor_tensor(out=wg[:], in0=rd[:], in1=zg[:, tts, col], op=ALU.mult), [b_rd, b_zg], [b_wg])
            wgb = wg[:].unsqueeze(2).to_broadcast([128, 4, 64])
            if branch not in brs:
                pass
            elif first:
                dve(lambda v: v.tensor_tensor(out=acc[:, tts, hq, :], in0=ps3[:, :, 0:64], in1=wgb, op=ALU.mult),
                    [b_psO[po], b_wg], [b_acc[nb][hq]])
            else:
                dve(lambda v: v.tensor_tensor(out=tmpo[:], in0=ps3[:, :, 0:64], in1=wgb, op=ALU.mult), [b_psO[po], b_wg], [b_tmpo])
                dve(lambda v: v.tensor_tensor(out=acc[:, tts, hq, :], in0=acc[:, tts, hq, :], in1=tmpo[:], op=ALU.add),
                    [b_tmpo, b_acc[nb][hq]], [b_acc[nb][hq]])
            if branch == 0:
                rdb = rd[:].unsqueeze(2).to_broadcast([128, 4, 32])
                if g == 0:
                    dve(lambda v: v.tensor_tensor(out=imp[:, tts, h, :], in0=ps3[:, :, 65:97], in1=rdb, op=ALU.mult), [b_psO[po], b_rd], [b_imp[nb]])
                else:
                    dve(lambda v: v.tensor_tensor(out=tmpi[:], in0=ps3[:, :, 65:97], in1=rdb, op=ALU.mult), [b_psO[po], b_rd], [b_tmpi])
                    dve(lambda v: v.tensor_tensor(out=imp[:, tts, h, :], in0=imp[:, tts, h, :], in1=tmpi[:], op=ALU.add), [b_tmpi, b_imp[nb]], [b_imp[nb]])

        NB_ = len(psS)

        def run_pipeline(items, look=2):
            n = len(items)
            for j in range(min(look, n)):
                items[j][0](j % NB_)
            for j in range(n):
                if j + look < n:
                    items[j + look][0]((j + look) % NB_)
                items[j][1](j % NB_)

        items = []
        for h in range(2):
            hs = slice(64 * h, 64 * h + 64)
            for g in range(4):
                hq = 4 * h + g
                for nb in range(4):
                    ns = slice(nb * 512, (nb + 1) * 512)

                    def s1(i, g=g, ns=ns, hs=hs, hq=hq, nb=nb, h=h):
                        pe(lambda p: p.matmul(psS[i][:], lhsT=kc2[:, h, :], rhs=qT[:, g, ns], start=True, stop=False),
                           [b_kc2, b_q[g]], [b_psS[i]])
                        pe(lambda p: p.matmul(psS[i][:], lhsT=C.identb[:], rhs=btab[:, hq, OFF_BC + nb * 512:OFF_BC + (nb + 1) * 512],
                                              start=False, stop=True), [C.b_const, b_bt], [b_psS[i]])
                        act(lambda a: a.activation(out=PT[i][:], in_=psS[i][:], func=AF.Exp), [b_psS[i]], [b_PT[i]])

                    def s2(i, h=h, g=g, nb=nb):
                        po = oi[0] % 2; oi[0] += 1
                        for ql in range(4):
                            pe(lambda p, ql=ql: p.matmul(psO[po][:, ql * 97:(ql + 1) * 97], lhsT=PT[i][:, ql * 128:(ql + 1) * 128],
                                                         rhs=VC[:, h, :], start=(ql == 0), stop=(ql == 3), skip_group_check=True),
                               [b_PT[i], b_VC], [b_psO[po]])
                        evac(po, 97, h, g, nb, 0, True)
                    items.append((s1, s2))
        run_pipeline(items)
        for h in range(2):
            for tt in range(NT):
                dve(lambda v, tt=tt, h=h: v.tensor_tensor(out=sc[:], in0=imp[:, tt, h, :], in1=cmask[:, tt, :], op=ALU.add), [b_imp[tt // 4], b_cm], [b_sc])
                dve(lambda v: v.max(out=top8[:], in_=sc[:]), [b_sc], [b_top])
                dve(lambda v: v.tensor_scalar(out=nsb[:], in0=sc[:], scalar1=top8[:, 7:8], scalar2=NEG, op0=ALU.is_lt, op1=ALU.mult),
                    [b_sc, b_top], [b_nsb])
                pe(lambda p, tt=tt: p.transpose(psT[0:32, (tt % 8) * 128:(tt % 8 + 1) * 128], nsb[:], C.identb[:]), [b_nsb, C.b_const], [b_psT])
                if tt % 8 == 7:
                    act(lambda a, tt=tt, h=h: a.copy(out=negT[0:32, h, (tt // 8) * 1024:(tt // 8 + 1) * 1024], in_=psT[0:32, :]),
                        [b_psT], [b_neg[h][tt // 8]])
        items = []
        for h in range(2):
            hs = slice(64 * h, 64 * h + 64)
            for g in range(4):
                hq = 4 * h + g
                for nb in range(4):
                    for branch in (1, 2):
                        if branch not in brs:
                            continue
                        if branch == 1:
                            kts = list(range(0, 4 * nb + 4)); K_, b_K, V_, b_V = ksT[h], b_ks, VS, b_VS
                        else:
                            kts = list(range(max(0, 4 * nb - 2), 4 * nb + 4)); K_, b_K, V_, b_V = kwT[h], b_kw, VW, b_VW
                        blk = {"po": None}
                        for kt in kts:
                            t_lo = max(512 * nb, 128 * kt)
                            t_hi = 512 * (nb + 1)
                            if branch == 2:
                                t_hi = min(t_hi, 128 * kt + 384)
                            N = t_hi - t_lo

                            def s1(i, kt=kt, g=g, h=h, hq=hq, hs=hs, t_lo=t_lo, t_hi=t_hi, N=N, K_=K_, b_K=b_K, branch=branch, nb=nb):
                                pe(lambda p: p.matmul(psS[i][:, 0:N], lhsT=K_[:, kt * 128:(kt + 1) * 128], rhs=qT[:, g, t_lo:t_hi],
                                                      start=True, stop=False), [b_K, b_q[g]], [b_psS[i]])
                                if branch == 1:
                                    b_lo, b_hi = max(t_lo, 128 * kt), min(t_hi, 128 * kt + 256)
                                    if b_hi > b_lo:
                                        pe(lambda p: p.matmul(psS[i][:, b_lo - t_lo:b_hi - t_lo], lhsT=C.identb[:],
                                                              rhs=btab[:, hq, OFF_BS + b_lo - 128 * kt:OFF_BS + b_hi - 128 * kt],
                                                              start=False, stop=False), [C.b_const, b_bt], [b_psS[i]])
                                    pe(lambda p: p.matmul(psS[i][:, 0:N], lhsT=expb[:, kt, :], rhs=negT[:, h, t_lo:t_hi], start=False, stop=True),
                                       [b_cst, b_neg[h][nb // 2]], [b_psS[i]])
                                else:
                                    pe(lambda p: p.matmul(psS[i][:, 0:N], lhsT=C.identb[:],
                                                          rhs=btab[:, hq, OFF_BW + t_lo - 128 * kt:OFF_BW + t_hi - 128 * kt],
                                                          start=False, stop=True), [C.b_const, b_bt], [b_psS[i]])
                                act(lambda a: a.activation(out=PT[i][:, 0:N], in_=psS[i][:, 0:N], func=AF.Exp), [b_psS[i]], [b_PT[i]])

                            def s2(i, kt=kt, g=g, h=h, t_lo=t_lo, t_hi=t_hi, V_=V_, b_V=b_V, branch=branch, nb=nb, blk=blk,
                                   first=(kt == kts[0]), last=(kt == kts[-1])):
                                if first:
                                    blk["po"] = oi[0] % 2; oi[0] += 1
                                po = blk["po"]
                                fm = first
                                for qt in range(t_lo // 128, t_hi // 128):
                                    ql = qt - 4 * nb
                                    pe(lambda p, ql=ql, qt=qt, fm=fm: p.matmul(
                                        psO[po][:, ql * 65:(ql + 1) * 65], lhsT=PT[i][:, qt * 128 - t_lo:qt * 128 - t_lo + 128],
                                        rhs=V_[:, kt, h, :], start=fm, stop=(kt == qt), skip_group_check=True),
                                        [b_PT[i], b_V], [b_psO[po]])
                                    fm = False
                                if last:
                                    evac(po, 65, h, g, nb, branch, False)
                            items.append((s1, s2))
        run_pipeline(items)
        if C.dbg.get("nsa_stop", 99) <= 5:
            return
        for tt in range(NT):
            act(lambda a, tt=tt: a.copy(out=accb[:], in_=acc[:, tt, :, :].rearrange("p h d -> p (h d)")),
                [b_acc[tt // 4][hq] for hq in range(8)], [b_accb])
            for ft in range(4):
                pe(lambda p, ft=ft: p.transpose(psT[:, ft * 128:(ft + 1) * 128], accb[:, ft * 128:(ft + 1) * 128], C.identb[:]),
                   [b_accb, C.b_const], [b_psT])
            dve(lambda v, tt=tt: v.tensor_copy(out=ost[:, :, tt * 128:(tt + 1) * 128], in_=psT[:, 0:512].rearrange("p (f t) -> p f t", f=4)),
                [b_psT], [b_ost[tt]])
        for ft in range(4):
            dma(lambda q, ft=ft: q.dma_start(out=C.mixT[256 + ft * 128:256 + (ft + 1) * 128, :], in_=ost[:, ft, :]),
                b_ost, [C.b_mixT[2 + ft]])


def phase_O1(C, s, l, src):
    nc, S = C.nc, C.S
    with ExitStack() as es:
        wob = es.enter_context(nc.sbuf_tensor("O1_wob", [128, 8, D], BF16))
        b_wob = [Buf() for _ in range(8)]
        stage = [es.enter_context(nc.sbuf_tensor("O1_wst%d" % i, [128, D], F32)) for i in range(2)]
        b_stage = [Buf() for _ in range(2)]
        mixT = es.enter_context(nc.sbuf_tensor("O1_mixT", [128, 8, SEQ], BF16))
        b_mix = [Buf() for _ in range(8)]
        ps = [es.enter_context(nc.psum_tensor("O1_ps%d" % i, [128, 512], F32)) for i in range(4)]
        b_ps = [Buf() for _ in range(4)]
        ps_tr = [es.enter_context(nc.psum_tensor("O1_pstr%d" % i, [128, D], BF16)) for i in range(2)]
        b_ps_tr = [Buf() for _ in range(2)]
        NB3 = 3
        ht = [es.enter_context(nc.sbuf_tensor("O1_h%d" % i, [128, D], F32)) for i in range(NB3)]
        b_ht = [Buf() for _ in range(NB3)]
        h1 = [es.enter_context(nc.sbuf_tensor("O1_h1%d" % i, [128, D], F32)) for i in range(NB3)]
        b_h1 = [Buf() for _ in range(NB3)]
        sq = es.enter_context(nc.sbuf_tensor("O1_sq", [128, D], BF16))
        ss = [es.enter_context(nc.sbuf_tensor("O1_ss%d" % i, [128, 1], F32)) for i in range(NB3)]
        rs = [es.enter_context(nc.sbuf_tensor("O1_rs%d" % i, [128, 1], F32)) for i in range(NB3)]
        xn = [es.enter_context(nc.sbuf_tensor("O1_xn%d" % i, [128, D], BF16)) for i in range(NB3)]
        b_tmp = [Buf() for _ in range(NB3)]
        b_xn = [Buf() for _ in range(NB3)]
        hn2 = [es.enter_context(nc.sbuf_tensor("O1_hn2_%d" % i, [128, 8, 128], BF16)) for i in range(2)]
        b_hn2 = [Buf() for _ in range(2)]

        gt_ = es.enter_context(nc.sbuf_tensor("O1_gain", [128, D], F32))
        C.b_gain = Buf()
        S.op("sp", lambda q: q.dma_start(out=gt_[:], in_=C.grep_in[:, 2 + l, :]), writes=[C.b_gain], dma=True)
        w_src = C.w_out[l].rearrange("(c p) f -> p c f", p=128)
        load_cast_weight(C, S, nc, stage, b_stage, lambda k: wob[:, k, :], lambda k: w_src[:, k, :], 8,
                         None, b_wob, eng=("act", "dve"))
        mix_src = C.mixT.rearrange("(c p) t -> p c t", p=128)
        for kt in range(8):
            S.op("sp", lambda q, kt=kt: q.dma_start(out=mixT[:, kt, :], in_=mix_src[:, kt, :]),
                 reads=[C.b_mixT[kt]], writes=[b_mix[kt]], dma=True)
        mmi = 0

        def mm_part(tt):
            nonlocal mmi
            i = tt % NB3
            S.op("sp", lambda q, tt=tt, i=i: q.dma_start(out=ht[i][:], in_=src[s, tt * 128:(tt + 1) * 128, :]),
                 reads=[C.b_hres[s][tt]], writes=[b_ht[i]], dma=True)
            for half in range(2):
                pi = mmi % 4
                mmi += 1
                for kt in range(8):
                    S.op("pe", lambda pe, tt=tt, kt=kt, pi=pi, half=half: pe.matmul(
                        ps[pi][:], lhsT=mixT[:, kt, tt * 128:(tt + 1) * 128],
                        rhs=wob[:, kt, half * 512:(half + 1) * 512], start=(kt == 0), stop=(kt == 7)),
                        reads=[b_mix[kt], b_wob[kt]], writes=[b_ps[pi]])
                S.op("dve", lambda v, i=i, pi=pi, half=half: v.tensor_tensor(
                    out=h1[i][:, half * 512:(half + 1) * 512], in0=ps[pi][:], in1=ht[i][:, half * 512:(half + 1) * 512],
                    op=ALU.add), reads=[b_ps[pi], b_ht[i]], writes=[b_h1[i]])
            S.op("sp", lambda q, tt=tt, i=i: q.dma_start(out=C.hres[s, tt * 128:(tt + 1) * 128, :], in_=h1[i][:]),
                 reads=[b_h1[i]], writes=[C.b_hres[s][tt]], dma=True)
            rms_tile(C, S, gt_[:], h1[i][:], b_h1[i], sq, ss[i], rs[i], xn[i], b_tmp[i], b_xn[i])

        def tr_part(tt):
            i = tt % NB3
            j = tt % 2
            for ct in range(8):
                S.op("pe", lambda pe, ct=ct, i=i, j=j: pe.transpose(ps_tr[j][:, ct * 128:(ct + 1) * 128],
                                                                      xn[i][:, ct * 128:(ct + 1) * 128], C.identb[:]),
                     reads=[b_xn[i], C.b_const], writes=[b_ps_tr[j]])
            S.op("act", lambda a, j=j: a.copy(out=hn2[j][:], in_=ps_tr[j][:].rearrange("p (c t) -> p c t", c=8)),
                 reads=[b_ps_tr[j]], writes=[b_hn2[j]])
            S.op("sp", lambda q, tt=tt, j=j: q.dma_start(
                out=C.hn2T.rearrange("(c p) t -> p c t", p=128)[:, :, tt * 128:(tt + 1) * 128], in_=hn2[j][:]),
                reads=[b_hn2[j]], writes=[C.b_hn2T[tt]], dma=True)

        for tt in range(NT):
            mm_part(tt)
            if tt >= 1:
                tr_part(tt - 1)
        tr_part(NT - 1)


def phase_O2(C, s, l, last):
    nc, S = C.nc, C.S
    HT = 1024
    with ExitStack() as es:
        hn2T = es.enter_context(nc.sbuf_tensor("O2_hn2T", [128, 8, HT], BF16))
        b_hn = [Buf() for _ in range(8)]
        actT = es.enter_context(nc.sbuf_tensor("O2_actT", [128, NFT, HT], BF16))
        b_act = [[Buf() for _ in range(2)] for _ in range(NFT)]
        wdb = es.enter_context(nc.sbuf_tensor("O2_wdb", [128, NFT, D], BF16))
        b_wdb = [Buf() for _ in range(NFT)]
        stage_d = [es.enter_context(nc.sbuf_tensor("O2_wdst%d" % i, [128, D], F32)) for i in range(2)]
        b_stage_d = [Buf() for _ in range(2)]
        stage_g = [es.enter_context(nc.sbuf_tensor("O2_wgst%d" % i, [128, 8, 128], F32)) for i in range(2)]
        stage_u = [es.enter_context(nc.sbuf_tensor("O2_wust%d" % i, [128, 8, 128], F32)) for i in range(2)]
        b_stage_g = [Buf() for _ in range(2)]
        b_stage_u = [Buf() for _ in range(2)]
        wgb = [es.enter_context(nc.sbuf_tensor("O2_wgb%d" % i, [128, 8, 128], BF16)) for i in range(2)]
        wub = [es.enter_context(nc.sbuf_tensor("O2_wub%d" % i, [128, 8, 128], BF16)) for i in range(2)]
        b_wgb = [Buf() for _ in range(2)]
        b_wub = [Buf() for _ in range(2)]
        psg = [es.enter_context(nc.psum_tensor("O2_psg%d" % i, [128, 512], F32)) for i in range(2)]
        psu = [es.enter_context(nc.psum_tensor("O2_psu%d" % i, [128, 512], F32)) for i in range(2)]
        psd = [es.enter_context(nc.psum_tensor("O2_psd%d" % i, [128, 512], F32)) for i in range(2)]
        b_psg = [Buf() for _ in range(2)]
        b_psu = [Buf() for _ in range(2)]
        b_psd = [Buf() for _ in range(2)]
        sg = [es.enter_context(nc.sbuf_tensor("O2_sg%d" % i, [128, 512], F32)) for i in range(2)]
        b_sg = [Buf() for _ in range(2)]
        h1 = [es.enter_context(nc.sbuf_tensor("O2_h1_%d" % i, [128, D], F32)) for i in range(2)]
        b_h1 = [Buf() for _ in range(2)]
        h2 = [es.enter_context(nc.sbuf_tensor("O2_h2_%d" % i, [128, D], F32)) for i in range(2)]
        b_h2 = [Buf() for _ in range(2)]
        if last:
            sq = es.enter_context(nc.sbuf_tensor("O2_sq", [128, D], BF16))
            ss = [es.enter_context(nc.sbuf_tensor("O2_ss%d" % i, [128, 1], F32)) for i in range(2)]
            rs = [es.enter_context(nc.sbuf_tensor("O2_rs%d" % i, [128, 1], F32)) for i in range(2)]
            yo = [es.enter_context(nc.sbuf_tensor("O2_yo%d" % i, [128, D], F32)) for i in range(2)]
            b_tmp = [Buf() for _ in range(2)]
            b_yo = [Buf() for _ in range(2)]

        wd_src = C.w_down[l].rearrange("(c p) f -> p c f", p=128)
        load_cast_weight(C, S, nc, stage_d, b_stage_d, lambda k: wdb[:, k, :], lambda k: wd_src[:, k, :], NFT,
                         None, b_wdb, eng=("dve", "pool"))
        wg_src = C.w_gate[l].rearrange("(c p) f -> p c f", p=128)
        wu_src = C.w_up[l].rearrange("(c p) f -> p c f", p=128)
        hn_src = C.hn2T.rearrange("(c p) t -> p c t", p=128)
        gi = 0
        for hf in range(2):
            t0 = hf * HT
            for ct in range(8):
                S.op("sp", lambda q, ct=ct, t0=t0: q.dma_start(out=hn2T[:, ct, :], in_=hn_src[:, ct, t0:t0 + HT]),
                     reads=C.b_hn2T[hf * 8:(hf + 1) * 8], writes=[b_hn[ct]], dma=True)
            for ft in range(NFT):
                j = gi % 2
                gi += 1
                fs = slice(ft * 128, (ft + 1) * 128)
                if not (C.dbg.get("ffn_skip2") and hf == 1):
                    S.op("sp", lambda q, j=j, fs=fs: q.dma_start(out=stage_g[j][:], in_=wg_src[:, :, fs]),
                         writes=[b_stage_g[j]], dma=True)
                    S.op("sp", lambda q, j=j, fs=fs: q.dma_start(out=stage_u[j][:], in_=wu_src[:, :, fs]),
                         writes=[b_stage_u[j]], dma=True)
                    S.op("act", lambda a, j=j: a.copy(out=wgb[j][:], in_=stage_g[j][:]), reads=[b_stage_g[j]], writes=[b_wgb[j]])
                    S.op("pool", lambda v, j=j: v.tensor_copy(out=wub[j][:], in_=stage_u[j][:]), reads=[b_stage_u[j]], writes=[b_wub[j]])
                for nb in range(2):
                    pi = (ft * 2 + nb) % 2
                    ns = slice(nb * 512, (nb + 1) * 512)
                    for ct in range(8):
                        S.op("pe", lambda pe, j=j, ct=ct, pi=pi, ns=ns: pe.matmul(
                            psg[pi][:], lhsT=wgb[j][:, ct, :], rhs=hn2T[:, ct, ns], start=(ct == 0), stop=(ct == 7)),
                            reads=[b_wgb[j], b_hn[ct]], writes=[b_psg[pi]])
                    for ct in range(8):
                        S.op("pe", lambda pe, j=j, ct=ct, pi=pi, ns=ns: pe.matmul(
                            psu[pi][:], lhsT=wub[j][:, ct, :], rhs=hn2T[:, ct, ns], start=(ct == 0), stop=(ct == 7)),
                            reads=[b_wub[j], b_hn[ct]], writes=[b_psu[pi]])
                    S.op("act", lambda a, pi=pi: a.activation(out=sg[pi][:], in_=psg[pi][:], func=AF.Silu),
                         reads=[b_psg[pi]], writes=[b_sg[pi]])
                    S.op("dve", lambda v, pi=pi, ft=ft, ns=ns: v.tensor_tensor(
                        out=actT[:, ft, ns], in0=psu[pi][:], in1=sg[pi][:], op=ALU.mult),
                        reads=[b_psu[pi], b_sg[pi]], writes=[b_act[ft][nb]])
            for tl in range(8):
                tt = hf * 8 + tl
                i = tt % 2
                S.op("sp", lambda q, tt=tt, i=i: q.dma_start(out=h1[i][:], in_=C.hres[s, tt * 128:(tt + 1) * 128, :]),
                     reads=[C.b_hres[s][tt]], writes=[b_h1[i]], dma=True)
                for half in range(2):
                    pi = (tt * 2 + half) % 2
                    for ft in range(NFT):
                        S.op("pe", lambda pe, ft=ft, tl=tl, pi=pi, half=half: pe.matmul(
                            psd[pi][:], lhsT=actT[:, ft, tl * 128:(tl + 1) * 128],
                            rhs=wdb[:, ft, half * 512:(half + 1) * 512], start=(ft == 0), stop=(ft == NFT - 1)),
                            reads=[b_act[ft][tl // 4], b_wdb[ft]], writes=[b_psd[pi]])
                    S.op("dve", lambda v, i=i, pi=pi, half=half: v.tensor_tensor(
                        out=h2[i][:, half * 512:(half + 1) * 512], in0=psd[pi][:],
                        in1=h1[i][:, half * 512:(half + 1) * 512], op=ALU.add),
                        reads=[b_psd[pi], b_h1[i]], writes=[b_h2[i]])
                if not last:
                    S.op("sp", lambda q, tt=tt, i=i: q.dma_start(out=C.hres[s, tt * 128:(tt + 1) * 128, :], in_=h2[i][:]),
                         reads=[b_h2[i]], writes=[C.b_hres[s][tt]], dma=True)
                else:
                    S.op("act", lambda a, i=i: a.activation(out=sq[:], in_=h2[i][:], func=AF.Square, accum_out=ss[i][:]),
                         reads=[b_h2[i]], writes=[b_tmp[i]])
                    S.op("act", lambda a, i=i: a.activation(out=rs[i][:], in_=ss[i][:], func=AF.Sqrt, scale=1.0 / D, bias=C.epsc[:]),
                         reads=[b_tmp[i], C.b_const], writes=[b_tmp[i]])
                    S.op("dve", lambda v, i=i: v.reciprocal(out=rs[i][:], in_=rs[i][:]), reads=[b_tmp[i]], writes=[b_tmp[i]])
                    S.op("dve", lambda v, i=i: v.scalar_tensor_tensor(
                        out=yo[i][:], in0=h2[i][:], scalar=rs[i][:], in1=C.gfin_sb[:], op0=ALU.mult, op1=ALU.mult),
                        reads=[b_h2[i], b_tmp[i], C.b_const], writes=[b_yo[i]])
                    S.op("sp", lambda q, tt=tt, i=i: q.dma_start(out=C.out[s, tt * 128:(tt + 1) * 128, :], in_=yo[i][:]),
                         reads=[b_yo[i]], writes=[C.b_out], dma=True, is_out=True)


def host_layout(inputs):
    f = lambda k: np.ascontiguousarray(np.asarray(inputs[k], dtype=np.float32))
    perm = in_perm()
    com = {}
    com["w_in"] = np.ascontiguousarray(f("w_in")[:, :, perm])
    com["w_out"] = f("w_out")
    com["w_gate"] = f("w_gate")
    com["w_up"] = f("w_up")
    com["w_down"] = f("w_down")
    vecs = [f("norm_mix")[0], f("norm_mix")[1], f("norm_ffn")[0], f("norm_ffn")[1], f("norm_final")]
    com["gains"] = np.ascontiguousarray(np.stack([v.reshape(8, 128).T for v in vecs], axis=1))
    com["gfin"] = np.ascontiguousarray(np.broadcast_to(f("norm_final")[None, :], (128, D)))
    com["grep"] = np.ascontiguousarray(np.broadcast_to(np.stack(vecs[:4], axis=0)[None, :, :], (128, 4, D)))
    com["ident"] = np.eye(128, dtype=np.float32)
    L = DEPTH
    com.update(nsa_host(inputs))
    s5p = np.zeros((L, 128, 8, 3), np.float32)
    pads = {k: np.zeros((L, 128, 8, 128), np.float32) for k in ("s5_bre", "s5_bim", "s5_cre", "s5_cim")}
    lre, lim, ldt = f("s5_lam_re"), f("s5_lam_im"), f("s5_log_dt")
    bre, bim, cre, cim = f("s5_b_re"), f("s5_b_im"), f("s5_c_re"), f("s5_c_im")
    for l in range(L):
        for g in range(16):
            st, po, co = g // 2, (g % 2) * 64, (g % 8) * 16
            s5p[l, po:po + 64, st, 0] = lre[l, g]
            s5p[l, po:po + 64, st, 1] = lim[l, g]
            s5p[l, po:po + 64, st, 2] = ldt[l, g]
            pads["s5_bre"][l, po:po + 64, st, co:co + 16] = bre[l, g]
            pads["s5_bim"][l, po:po + 64, st, co:co + 16] = bim[l, g]
            pads["s5_cre"][l, po:po + 64, st, co:co + 16] = cre[l, g].T
            pads["s5_cim"][l, po:po + 64, st, co:co + 16] = cim[l, g].T
    com["s5p"] = s5p
    com.update(pads)
    com["s5_dsk"] = np.ascontiguousarray(f("s5_d").reshape(L, 2, 128).transpose(0, 2, 1))
    com["s5_wglu"] = f("s5_w_glu")
    lruv = np.zeros((L, 128, 2, 8), np.float32)
    cw = f("lru_conv_w")
    for l in range(L):
        for ct in range(2):
            cs = slice(ct * 128, (ct + 1) * 128)
            for i in range(4):
                lruv[l, :, ct, i] = cw[l, i, cs]
            lruv[l, :, ct, 4] = f("lru_conv_b")[l, cs]
            lruv[l, :, ct, 5] = f("lru_b_a")[l, cs]
            lruv[l, :, ct, 6] = f("lru_b_x")[l, cs]
            lruv[l, :, ct, 7] = f("lru_lam")[l, cs]
    com["lruv"] = lruv
    for nm, key in (("lru_wa", "lru_w_a"), ("lru_wx", "lru_w_x")):
        w = f(key)
        bd = np.zeros((L, 128, 2, 128), np.float32)
        for l in range(L):
            for hh in range(4):
                ct, o = hh // 2, (hh % 2) * 64
                bd[l, o:o + 64, ct, o:o + 64] = w[l, hh]
        com[nm] = bd
    return com


def t5_bucket_np(dist):
    n = np.maximum(dist, 0)
    nf = np.maximum(n, 1).astype(np.float32)
    large = 16 + (np.log(nf / np.float32(16)) / np.float32(math.log(128 / 16)) * np.float32(16)).astype(np.int32)
    large = np.minimum(large, 31)
    return np.where(n < 16, n, large)


def nsa_host(inputs):
    import ml_dtypes
    f = lambda k: np.ascontiguousarray(np.asarray(inputs[k], dtype=np.float32))
    tbl = f("rel_bias_table")
    out = {}
    k = np.arange(128)[:, None]
    c = np.arange(256)[None, :]
    d_s = c - k
    c = np.arange(384)[None, :]
    d_w = c - k
    t = np.arange(SEQ)[None, :]
    d_c = t - (16 * k + 31)
    raw = np.zeros((128, 8, NBT), np.float32)
    for hq in range(8):
        col = tbl[:, hq]
        raw[:, hq, OFF_BS:OFF_BS + 256] = np.where(d_s >= 0, col[t5_bucket_np(d_s)], np.float32(NEG))
        raw[:, hq, OFF_BW:OFF_BW + 384] = np.where((d_w >= 0) & (d_w < 256), col[t5_bucket_np(d_w)], np.float32(NEG))
        raw[:, hq, OFF_BC:OFF_BC + SEQ] = np.where((d_c >= 0) & (k < 127), col[t5_bucket_np(d_c)], np.float32(NEG))
    out["btab_raw"] = raw
    out["cvec"] = np.ascontiguousarray(np.broadcast_to(tbl[31][None, :], (128, 8)))
    ex = np.zeros((32, NT, 128), np.float32)
    for kt in range(NT):
        for kk in range(128):
            ex[(kt * 128 + kk) // 64, kt, kk] = 1.0
    out["expand"] = ex.astype(ml_dtypes.bfloat16)
    cs = np.arange(128) * 16
    ss = np.arange(32) * 64
    ov = ((cs[:, None] < ss[None, :] + 64) & (ss[None, :] < cs[:, None] + 32)).astype(np.float32)
    ov[127] = 0.0
    out["vca_const"] = np.concatenate([np.ones((128, 1), np.float32), ov], axis=1).astype(ml_dtypes.bfloat16)
    tt = np.arange(SEQ)
    cur = (tt // 64)[:, None]
    ids = np.arange(32)[None, :]
    forced = (ids == 0) | (ids == cur) | (ids == cur - 1)
    avail = ids * 64 <= tt[:, None]
    cm = np.where(avail, np.where(forced, 1e4, 0.0), -1e9).astype(np.float32)
    out["cmask"] = np.ascontiguousarray(cm.reshape(NT, 128, 32).transpose(1, 0, 2))
    L = DEPTH
    w1 = np.stack([f("nsa_w1_k"), f("nsa_w1_v")], axis=1)
    w1r = w1.reshape(L, 2, 32, 64, 64).transpose(0, 1, 3, 2, 4)
    w1z = np.zeros((L, 2, 2, 128, 32, 64), np.float32)
    w1z[:, :, 0, 0:64] = w1r
    w1z[:, :, 1, 64:128] = w1r
    out["nsa_w1"] = w1z
    pe = np.stack([f("nsa_pe_k"), f("nsa_pe_v")], axis=1)
    peT = pe.transpose(0, 3, 1, 2)
    out["nsa_peT"] = np.ascontiguousarray(np.concatenate([peT, np.zeros_like(peT)], axis=1))
    w2k = f("nsa_w2_k")
    w2p = np.zeros((L, 64, 2, 128), np.float32)
    w2p[:, :, 0, 0:64] = w2k
    w2p[:, :, 1, 64:128] = w2k
    out["nsa_w2pad"] = w2p
    out["nsa_w2v"] = f("nsa_w2_v")
    return out


_CACHE = {}


def kernel(**inputs):
    x = np.ascontiguousarray(np.asarray(inputs["x"], dtype=np.float32))
    com = host_layout(inputs)
    if "nc" not in _CACHE:
        _CACHE["nc"] = build_program()[0]
    nc = _CACHE["nc"]
    in_maps = []
    for c in range(8):
        m = dict(com)
        m["x"] = np.ascontiguousarray(x[c * NSEQ:(c + 1) * NSEQ])
        in_maps.append(m)
    res = run_bass_kernel_spmd(nc, in_maps, core_ids=list(range(8)))
    return np.concatenate([r["out"] for r in res.results], axis=0).astype(np.float32)
```

```python
import math
import numpy as np
import concourse.bass as bass
import concourse.mybir as mybir
from concourse.bass_utils import run_bass_kernel_spmd
from contextlib import ExitStack

F32 = mybir.dt.float32
BF16 = mybir.dt.bfloat16
AF = mybir.ActivationFunctionType
ALU = mybir.AluOpType
AX = mybir.AxisListType

D = 1024
SEQ = 2048
NT = 16
DFF = 2816
NFT = 22
INW = 2072
NFM = 14
NTM = 280
NSEQ = 2
DEPTH = 2
EPS = 1e-6
NEG = -30000.0

ENGS = ("pe", "act", "dve", "pool", "sp")


class Buf:
    __slots__ = ("name", "last_w", "readers")

    def __init__(self, name=""):
        self.name = name
        self.last_w = None
        self.readers = []


class Op:
    __slots__ = ("eng", "emit", "waits", "sig", "pos", "dma", "sem", "val", "cnt", "prewait", "gsn", "raw")


class Sched:
    NDSEM = 12

    def __init__(self, nc):
        self.nc = nc
        self.ops = {e: [] for e in ENGS}
        self.known = {e: {f: -1 for f in ENGS} for e in ENGS}
        self.dma_waited = {e: set() for e in ENGS}
        self.dma_count = {e: 0 for e in ENGS}
        self.pending_dma = []
        self.out_dmas = []
        self.nbar = 0

    def op(self, eng, emit, reads=(), writes=(), dma=False, is_out=False):
        o = Op()
        o.eng = eng
        o.emit = emit
        o.sig = False
        o.dma = dma
        o.raw = False
        o.pos = len(self.ops[eng])
        o.waits = []
        o.prewait = None
        o.cnt = None
        deps = []
        for b in reads:
            if b.last_w is not None:
                deps.append(b.last_w)
        for b in writes:
            if b.last_w is not None:
                deps.append(b.last_w)
            deps.extend(b.readers)
        seen = set()
        for y in deps:
            if id(y) in seen:
                continue
            seen.add(id(y))
            self._add_wait(o, y)
        if dma:
            i = self.dma_count[eng]
            self.dma_count[eng] += 1
            o.sem = (eng, i % self.NDSEM)
            o.val = 16 * (i // self.NDSEM + 1)
            if i >= self.NDSEM:
                o.prewait = (o.sem, 16 * (i // self.NDSEM))
            self.pending_dma.append(o)
            if is_out:
                self.out_dmas.append(o)
        self.ops[eng].append(o)
        for b in writes:
            b.last_w = o
            b.readers = []
        for b in reads:
            if not dma:
                b.readers = [r for r in b.readers if r.dma or r.eng != eng]
            b.readers.append(o)
        return o

    def _add_wait(self, o, y):
        e = o.eng
        if y.dma:
            if id(y) in self.dma_waited[e]:
                return
            self.dma_waited[e].add(id(y))
            o.waits.append(y)
        else:
            f = y.eng
            if f == e and e == "pe":
                return
            if y.pos <= self.known[e][f]:
                return
            self.known[e][f] = y.pos
            y.sig = True
            o.waits.append(y)

    def barrier(self):
        lasts = []
        for f in ENGS:
            for y in reversed(self.ops[f]):
                if not y.dma and not y.raw:
                    lasts.append(y)
                    break
        o = Op()
        o.eng = "sp"; o.emit = None; o.sig = False; o.dma = False; o.raw = True
        o.pos = len(self.ops["sp"]); o.waits = []; o.prewait = None; o.cnt = None
        for y in lasts:
            if y.pos > self.known["sp"][y.eng]:
                self.known["sp"][y.eng] = y.pos
                y.sig = True
                o.waits.append(y)
        for y in self.pending_dma:
            if id(y) not in self.dma_waited["sp"]:
                o.waits.append(y)
        self.pending_dma = []
        self.nbar += 1
        o.val = self.nbar
        o.emit = "bar_sig"
        self.ops["sp"].append(o)
        for e in ENGS:
            if e == "sp":
                continue
            w = Op()
            w.eng = e; w.emit = "bar_wait"; w.sig = False; w.dma = False; w.raw = True
            w.pos = len(self.ops[e]); w.waits = []; w.prewait = None; w.cnt = None
            w.val = self.nbar
            self.ops[e].append(w)
        for e in ENGS:
            for f in ENGS:
                self.known[e][f] = len(self.ops[f]) - 1
            self.dma_waited[e] = set()

    def finish(self):
        o = Op()
        o.eng = "sp"; o.emit = "nop"; o.sig = False; o.dma = False; o.raw = True
        o.pos = len(self.ops["sp"]); o.waits = list(self.out_dmas) + [y for y in self.pending_dma]
        o.prewait = None; o.cnt = None; o.val = 0
        self.ops["sp"].append(o)

    def emit_all(self, es):
        nc = self.nc
        esem = {e: es.enter_context(nc.semaphore("es_" + e)) for e in ENGS}
        dsem = {}
        for e in ENGS:
            if self.dma_count[e] > 0:
                for i in range(min(self.NDSEM, self.dma_count[e])):
                    dsem[(e, i)] = es.enter_context(nc.semaphore("ds_%s_%d" % (e, i)))
        bsem = es.enter_context(nc.semaphore("barrier"))
        for e in ENGS:
            c = 0
            for o in self.ops[e]:
                if o.sig:
                    c += 1
                    o.cnt = c
        block = es.enter_context(nc.Block())
        decos = {"pe": block.tensor, "act": block.scalar, "dve": block.vector,
                 "pool": block.gpsimd, "sp": block.sync}
        stats = {}
        for e in ENGS:
            ops = self.ops[e]
            stats[e] = len(ops)
            if not ops:
                continue

            def body(eng, ops=ops):
                for o in ops:
                    for y in o.waits:
                        if y.dma:
                            eng.wait_ge(dsem[y.sem], y.val)
                        else:
                            eng.wait_ge(esem[y.eng], y.cnt)
                    if o.prewait is not None:
                        eng.wait_ge(dsem[o.prewait[0]], o.prewait[1])
                    if o.raw:
                        if o.emit == "bar_sig":
                            eng.nop().then_inc(bsem, 1)
                        elif o.emit == "bar_wait":
                            eng.wait_ge(bsem, o.val)
                        continue
                    ins = o.emit(eng)
                    if o.dma:
                        ins.then_inc(dsem[o.sem], 16)
                    elif o.sig:
                        ins.then_inc(esem[o.eng], 1)

            decos[e](body)
        return stats


def in_perm():
    o_u, o_q, o_kv, o_g, o_lx, o_lg = 0, 256, 768, 1536, 1560, 1816
    p = list(range(o_u, o_u + 256))
    for j in range(4):
        p += list(range(o_q + 64 * j, o_q + 64 * j + 64))
        p += list(range(o_q + 64 * (4 + j), o_q + 64 * (4 + j) + 64))
    for i in (0, 1, 2, 4):
        p += list(range(o_kv + 128 * i, o_kv + 128 * i + 128))
    p += list(range(o_lx, o_lx + 256))
    p += list(range(o_lg, o_lg + 256))
    for i in (3, 5):
        p += list(range(o_kv + 128 * i, o_kv + 128 * i + 128))
    p += list(range(o_g, o_g + 24))
    assert len(p) == INW
    return np.array(p)


class Ctx:
    pass


class _NC:
    def __init__(self, nc):
        self._nc = nc
        self._n = 0

    def __getattr__(self, k):
        return getattr(self._nc, k)

    def sbuf_tensor(self, name, *a, **kw):
        self._n += 1
        return self._nc.sbuf_tensor("%s_%d" % (name, self._n), *a, **kw)

    def psum_tensor(self, name, *a, **kw):
        self._n += 1
        return self._nc.psum_tensor("%s_%d" % (name, self._n), *a, **kw)


def bview(t, bufs):
    return t, bufs


def build_program(dbg=None):
    dbg = dbg or {}
    nc = _NC(bass.Bass("TRN2", target_bir_lowering=False))
    S = Sched(nc)
    C = Ctx()
    C.nc = nc
    C.S = S
    C.dbg = dbg
    C.bcast = lambda ap, n: ap.to_broadcast([128, n])

    def dump(name, ap, shape, dt, reads):
        if name not in dbg.get("dump", ()):
            return
        d = nc.dram_tensor("dump_" + name, list(shape), dt, kind="ExternalOutput").ap()
        S.op("sp", lambda q: q.dma_start(out=d, in_=ap), reads=reads, writes=[Buf()], dma=True, is_out=True)
    C.dump = dump

    def din(name, shape, dt=F32):
        return nc.dram_tensor(name, list(shape), dt, kind="ExternalInput").ap()

    def dscr(name, shape, dt=F32):
        kind = "ExternalOutput" if name in dbg.get("expose", ()) else "Internal"
        return nc.dram_tensor(name, list(shape), dt, kind=kind).ap()

    C.x = din("x", [NSEQ, SEQ, D])
    C.w_in = din("w_in", [DEPTH, D, INW])
    C.w_out = din("w_out", [DEPTH, D, D])
    C.w_gate = din("w_gate", [DEPTH, D, DFF])
    C.w_up = din("w_up", [DEPTH, D, DFF])
    C.w_down = din("w_down", [DEPTH, DFF, D])
    C.gains = din("gains", [128, 5, 8])
    C.gfin = din("gfin", [128, D])
    C.grep_in = din("grep", [128, 4, D])
    C.ident = din("ident", [128, 128])
    C.s5p = din("s5p", [DEPTH, 128, 8, 3])
    C.s5_bre = din("s5_bre", [DEPTH, 128, 8, 128])
    C.s5_bim = din("s5_bim", [DEPTH, 128, 8, 128])
    C.s5_cre = din("s5_cre", [DEPTH, 128, 8, 128])
    C.s5_cim = din("s5_cim", [DEPTH, 128, 8, 128])
    C.s5_dsk = din("s5_dsk", [DEPTH, 128, 2])
    C.s5_wglu = din("s5_wglu", [DEPTH, 256, 512])
    C.btab_raw = din("btab_raw", [128, 8, NBT])
    C.cvec = din("cvec", [128, 8])
    C.expand = din("expand", [32, NT, 128], BF16)
    C.vca_const = din("vca_const", [128, 33], BF16)
    C.cmask = din("cmask", [128, NT, 32])
    C.nsa_w1 = din("nsa_w1", [DEPTH, 2, 2, 128, 32, 64])
    C.nsa_peT = din("nsa_peT", [DEPTH, 128, 2, 32])
    C.nsa_w2pad = din("nsa_w2pad", [DEPTH, 64, 2, 128])
    C.nsa_w2v = din("nsa_w2v", [DEPTH, 64, 64])
    C.btab = dscr("btab", [128, 8, NBT], BF16)
    C.b_btab = Buf()
    C.lruv = din("lruv", [DEPTH, 128, 2, 8])
    C.lru_wa = din("lru_wa", [DEPTH, 128, 2, 128])
    C.lru_wx = din("lru_wx", [DEPTH, 128, 2, 128])
    C.out = nc.dram_tensor("out", [NSEQ, SEQ, D], F32, kind="ExternalOutput").ap()

    C.hres = dscr("hres", [NSEQ, SEQ, D])
    C.zu = dscr("zu", [256, SEQ])
    C.zlru = dscr("zlru", [512, SEQ])
    C.zqk = dscr("zqk", [8 * 128, SEQ], BF16)
    C.zv = dscr("zv", [SEQ, 2, 2, 64], BF16)
    C.zg = dscr("zg", [128, NT * 24])
    if "mixT_in" in dbg:
        C.mixT = din("mixT", [D, SEQ], BF16)
    else:
        C.mixT = dscr("mixT", [D, SEQ], BF16)
    C.hn2T = dscr("hn2T", [D, SEQ], BF16)
    C.b_hn2T = [Buf() for _ in range(NT)]
    C.b_hres = [[Buf("hres%d_%d" % (s, t)) for t in range(NT)] for s in range(NSEQ)]
    C.b_mixT = [Buf("mixT%d" % i) for i in range(8)]
    C.b_zu = [Buf() for _ in range(2)]
    C.b_zlru = [Buf() for _ in range(4)]
    C.b_zqk = [Buf() for _ in range(8)]
    C.b_zv = Buf()
    C.b_zg = Buf()
    C.b_out = Buf()
    C.s5c = []
    C.b_s5c = []
    for l_ in range(DEPTH):
        C.s5c.append({"W1T": dscr("c_W1T%d" % l_, [128, 8 * 2 * 8 * 128], BF16), "CaR": dscr("c_CaR%d" % l_, [128, 9 * 8 * 128], BF16),
                      "nCaI": dscr("c_nCaI%d" % l_, [128, 9 * 8 * 128], BF16), "Kt": dscr("c_Kt%d" % l_, [128, 2 * 8 * 128], BF16),
                      "Ec": dscr("c_Ec%d" % l_, [128, 8 * 256]), "Es": dscr("c_Es%d" % l_, [128, 8 * 256])})
        C.b_s5c.append({k_: Buf() for k_ in ("W1T", "CaR", "nCaI", "Kt", "Ec", "Es")})

    with ExitStack() as es:
        C.identb = es.enter_context(nc.sbuf_tensor("identb", [128, 128], BF16))
        C.identf = es.enter_context(nc.sbuf_tensor("identf", [128, 128], F32))
        C.gn = es.enter_context(nc.sbuf_tensor("gn", [128, 5, 8], F32))
        C.gfin_sb = es.enter_context(nc.sbuf_tensor("gfin_sb", [128, D], F32))
        C.epsc = es.enter_context(nc.sbuf_tensor("epsc", [128, 1], F32))
        C.b_const = Buf("const")
        S.op("sp", lambda q: q.dma_start(out=C.identf[:], in_=C.ident[:, :]), writes=[C.b_const], dma=True)
        S.op("sp", lambda q: q.dma_start(out=C.gn[:], in_=C.gains[:, :, :]), writes=[C.b_const], dma=True)
        S.op("sp", lambda q: q.dma_start(out=C.gfin_sb[:], in_=C.gfin[:, :]), writes=[C.b_const], dma=True)
        S.op("dve", lambda v: v.tensor_copy(out=C.identb[:], in_=C.identf[:]), reads=[C.b_const], writes=[C.b_const])
        S.op("dve", lambda v: v.memset(C.epsc[:], EPS), writes=[C.b_const])
        S.barrier()
        if "NSA" in dbg.get("phases", ("NSA",)):
            phase_btab(C)
            S.barrier()

        phases = dbg.get("phases", ("A", "S5", "LRU", "NSA", "O1", "O2"))
        fns = {"A": lambda s, l, src: phase_A(C, s, l, src),
               "S5": lambda s, l, src: (phase_S5B(C, s, l) if C.dbg.get("s5b", 1) else phase_S5(C, s, l)),
               "LRU": lambda s, l, src: phase_LRU(C, s, l),
               "NSA": lambda s, l, src: phase_NSA(C, s, l),
               "O1": lambda s, l, src: phase_O1(C, s, l, src),
               "O2": lambda s, l, src: phase_O2(C, s, l, last=(l == DEPTH - 1))}
        for s in range(dbg.get("nseq", NSEQ)):
            for l in range(dbg.get("nlayer", DEPTH)):
                src = C.x if l == 0 else C.hres
                for ph in phases:
                    fns[ph](s, l, src)
                    S.barrier()
        S.finish()
        stats = S.emit_all(es)
    C.stats = stats
    return nc._nc, C


def rms_tile(C, S, grep, htile, b_h, sqjunk, ss, rstd, xn, b_tmp, b_xn):
    S.op("act", lambda a: a.activation(out=sqjunk[:], in_=htile, func=AF.Square, accum_out=ss[:]),
         reads=[b_h], writes=[b_tmp])
    S.op("act", lambda a: a.activation(out=rstd[:], in_=ss[:], func=AF.Sqrt, scale=1.0 / D, bias=C.epsc[:]),
         reads=[b_tmp, C.b_const], writes=[b_tmp])
    S.op("dve", lambda v: v.reciprocal(out=rstd[:], in_=rstd[:]), reads=[b_tmp], writes=[b_tmp])
    S.op("dve", lambda v: v.scalar_tensor_tensor(out=xn[:], in0=htile, scalar=rstd[:], in1=grep, op0=ALU.mult, op1=ALU.mult),
         reads=[b_h, b_tmp, C.b_gain], writes=[b_xn])


def norm_transpose_phase(C, S, es, nc, src_ap, s, b_src, hnT, b_hnT, ps_tr, b_ps_tr, grep, after_tile=None):
    NBF = 4
    ht = [es.enter_context(nc.sbuf_tensor("nt_h%d" % i, [128, D], F32)) for i in range(NBF)]
    b_ht = [Buf() for _ in range(NBF)]
    sq = es.enter_context(nc.sbuf_tensor("nt_sq", [128, D], BF16))
    ss = [es.enter_context(nc.sbuf_tensor("nt_ss%d" % i, [128, 1], F32)) for i in range(NBF)]
    rs = [es.enter_context(nc.sbuf_tensor("nt_rs%d" % i, [128, 1], F32)) for i in range(NBF)]
    xn = [es.enter_context(nc.sbuf_tensor("nt_xn%d" % i, [128, D], BF16)) for i in range(NBF)]
    b_tmp = [Buf() for _ in range(NBF)]
    b_xn = [Buf() for _ in range(NBF)]
    for tt in range(NT):
        i = tt % NBF
        j2 = tt % 2
        S.op("sp", lambda q, tt=tt, i=i: q.dma_start(out=ht[i][:], in_=src_ap[s, tt * 128:(tt + 1) * 128, :]),
             reads=[b_src[tt]], writes=[b_ht[i]], dma=True)
        rms_tile(C, S, grep, ht[i][:], b_ht[i], sq, ss[i], rs[i], xn[i], b_tmp[i], b_xn[i])
        for ct in range(8):
            S.op("pe", lambda pe, ct=ct, i=i, j2=j2: pe.transpose(ps_tr[j2][:, ct * 128:(ct + 1) * 128],
                                                             xn[i][:, ct * 128:(ct + 1) * 128], C.identb[:]),
                 reads=[b_xn[i], C.b_const], writes=[b_ps_tr[j2]])
        eng = "act" if tt % 2 == 0 else "dve"
        if eng == "act":
            S.op("act", lambda a, tt=tt, j2=j2: a.copy(out=hnT[:, :, tt * 128:(tt + 1) * 128],
                                                     in_=ps_tr[j2][:].rearrange("p (c t) -> p c t", c=8)),
                 reads=[b_ps_tr[j2]], writes=[b_hnT[tt]])
        else:
            S.op("dve", lambda v, tt=tt, j2=j2: v.tensor_copy(out=hnT[:, :, tt * 128:(tt + 1) * 128],
                                                            in_=ps_tr[j2][:].rearrange("p (c t) -> p c t", c=8)),
                 reads=[b_ps_tr[j2]], writes=[b_hnT[tt]])
        if after_tile is not None:
            after_tile(tt)


def load_cast_weight(C, S, nc, stage, b_stage, dst_ap_fn, src_ap_fn, n, gain_fn, b_dst, eng="act", ks=None):
    for k in (range(n) if ks is None else ks):
        i = k % len(stage)
        S.op("sp", lambda q, k=k, i=i: q.dma_start(out=stage[i][:], in_=src_ap_fn(k)),
             writes=[b_stage[i]], dma=True)
        g = gain_fn(k) if gain_fn is not None else None
        e = eng if isinstance(eng, str) else eng[k % len(eng)]
        if e == "act":
            if g is not None:
                S.op("act", lambda a, k=k, i=i, g=g: a.activation(out=dst_ap_fn(k), in_=stage[i][:], func=AF.Copy, scale=g),
                     reads=[b_stage[i], C.b_const], writes=[b_dst[k] if isinstance(b_dst, list) else b_dst])
            else:
                S.op("act", lambda a, k=k, i=i: a.copy(out=dst_ap_fn(k), in_=stage[i][:]),
                     reads=[b_stage[i]], writes=[b_dst[k] if isinstance(b_dst, list) else b_dst])
        else:
            if g is not None:
                S.op(e, lambda v, k=k, i=i, g=g: v.tensor_scalar(out=dst_ap_fn(k), in0=stage[i][:], scalar1=g, scalar2=None, op0=ALU.mult),
                     reads=[b_stage[i], C.b_const], writes=[b_dst[k] if isinstance(b_dst, list) else b_dst])
            else:
                S.op(e, lambda v, k=k, i=i: v.tensor_copy(out=dst_ap_fn(k), in_=stage[i][:]),
                     reads=[b_stage[i]], writes=[b_dst[k] if isinstance(b_dst, list) else b_dst])


def phase_A(C, s, l, src):
    nc, S = C.nc, C.S
    with ExitStack() as es:
        winb = es.enter_context(nc.sbuf_tensor("A_winb", [128, 8, INW], BF16))
        b_winb = [Buf() for _ in range(8)]
        stage = [es.enter_context(nc.sbuf_tensor("A_wst%d" % i, [128, INW], F32)) for i in range(2)]
        b_stage = [Buf() for _ in range(2)]
        hnT = es.enter_context(nc.sbuf_tensor("A_hnT", [128, 8, SEQ], BF16))
        b_hnT = [Buf() for _ in range(NT)]
        ps_tr = [es.enter_context(nc.psum_tensor("A_pstr%d" % i, [128, D], BF16)) for i in range(2)]
        b_ps_tr = [Buf() for _ in range(2)]
        ps_mm = [es.enter_context(nc.psum_tensor("A_psmm%d" % i, [128, 512], F32)) for i in range(4)]
        b_ps_mm = [Buf() for _ in range(4)]
        ev32 = [es.enter_context(nc.sbuf_tensor("A_ev32_%d" % i, [128, 512], F32)) for i in range(2)]
        ev16 = [es.enter_context(nc.sbuf_tensor("A_ev16_%d" % i, [128, 512], BF16)) for i in range(2)]
        b_ev32 = [Buf() for _ in range(2)]
        b_ev16 = [Buf() for _ in range(2)]
        vt = [es.enter_context(nc.sbuf_tensor("A_vt%d" % i, [128, 256], BF16)) for i in range(2)]
        gt_all = es.enter_context(nc.sbuf_tensor("A_gtall", [128, NT, 24], F32))
        b_vt = [Buf() for _ in range(2)]
        b_gt = [Buf() for _ in range(2)]

        w_src = C.w_in[l].rearrange("(c p) f -> p c f", p=128)
        def w_chunk(tt):
            if tt % 2 == 1:
                load_cast_weight(C, S, nc, stage, b_stage, lambda k: winb[:, k, :], lambda k: w_src[:, k, :], 8,
                                 None, b_winb, eng=("act", "dve"), ks=[tt // 2])
        if C.dbg.get("A_steps", 9) < 2:
            return
        b_src = C.b_hres[s]
        gt_ = es.enter_context(nc.sbuf_tensor("A_gain", [128, D], F32))
        C.b_gain = Buf()
        S.op("sp", lambda q: q.dma_start(out=gt_[:], in_=C.grep_in[:, l, :]), writes=[C.b_gain], dma=True)
        norm_transpose_phase(C, S, es, nc, src, s, b_src, hnT, b_hnT, ps_tr, b_ps_tr, gt_[:], after_tile=w_chunk)
        if C.dbg.get("A_steps", 9) < 3:
            return

        k32 = 0
        k16 = 0
        mmi = 0
        for ft in range(NFM):
            for nb in range(4):
                pi = mmi % 4
                mmi += 1
                for ct in range(8):
                    S.op("pe", lambda pe, ft=ft, nb=nb, ct=ct, pi=pi: pe.matmul(
                        ps_mm[pi][:], lhsT=winb[:, ct, ft * 128:(ft + 1) * 128],
                        rhs=hnT[:, ct, nb * 512:(nb + 1) * 512], start=(ct == 0), stop=(ct == 7)),
                        reads=[b_winb[ct]] + b_hnT[nb * 4:(nb + 1) * 4], writes=[b_ps_mm[pi]])
                cs = slice(nb * 512, (nb + 1) * 512)
                if ft < 2 or ft >= 10:
                    j = k32 % 2
                    k32 += 1
                    if ft < 2:
                        dst, bd = C.zu[ft * 128:(ft + 1) * 128, cs], C.b_zu[ft]
                    else:
                        dst, bd = C.zlru[(ft - 10) * 128:(ft - 9) * 128, cs], C.b_zlru[ft - 10]
                    S.op("dve", lambda v, j=j, pi=pi: v.tensor_copy(out=ev32[j][:], in_=ps_mm[pi][:]),
                         reads=[b_ps_mm[pi]], writes=[b_ev32[j]])
                    S.op("sp", lambda q, j=j, dst=dst: q.dma_start(out=dst, in_=ev32[j][:]),
                         reads=[b_ev32[j]], writes=[bd], dma=True)
                else:
                    j = k16 % 2
                    k16 += 1
                    sc = 0.125 if ft < 6 else 1.0
                    dst, bd = C.zqk[(ft - 2) * 128:(ft - 1) * 128, cs], C.b_zqk[ft - 2]
                    S.op("act", lambda a, j=j, pi=pi, sc=sc: a.mul(out=ev16[j][:], in_=ps_mm[pi][:], mul=sc),
                         reads=[b_ps_mm[pi]], writes=[b_ev16[j]])
                    S.op("sp", lambda q, j=j, dst=dst: q.dma_start(out=dst, in_=ev16[j][:]),
                         reads=[b_ev16[j]], writes=[bd], dma=True)
        if C.dbg.get("A_steps", 9) < 4:
            return
        for tt in range(NT):
            pi = mmi % 4
            mmi += 1
            j = tt % 2
            for ct in range(8):
                S.op("pe", lambda pe, tt=tt, ct=ct, pi=pi: pe.matmul(
                    ps_mm[pi][:, 0:NTM], lhsT=hnT[:, ct, tt * 128:(tt + 1) * 128],
                    rhs=winb[:, ct, NFM * 128:INW], start=(ct == 0), stop=(ct == 7)),
                    reads=[b_winb[ct], b_hnT[tt]], writes=[b_ps_mm[pi]])
            a4 = C.dbg.get("A4", "vg")
            if "v" in a4:
                S.op("dve", lambda v, j=j, pi=pi: v.tensor_copy(out=vt[j][:], in_=ps_mm[pi][:, 0:256]),
                     reads=[b_ps_mm[pi]], writes=[b_vt[j]])
                S.op("sp", lambda q, j=j, tt=tt: q.dma_start(
                    out=C.zv[tt * 128:(tt + 1) * 128].rearrange("t a h d -> t (a h d)"), in_=vt[j][:]),
                    reads=[b_vt[j]], writes=[C.b_zv], dma=True)
            if "g" in a4:
                S.op("act", lambda a, tt=tt, pi=pi: a.activation(out=gt_all[:, tt, :], in_=ps_mm[pi][:, 256:280], func=AF.Sigmoid),
                     reads=[b_ps_mm[pi]], writes=[b_gt[0]])
        S.op("sp", lambda q: q.dma_start(out=C.zg[:, :], in_=gt_all[:].rearrange("p t g -> p (t g)")),
             reads=[b_gt[0]], writes=[C.b_zg], dma=True)


GELU_C = 1.5957691216057308


def gelu_tanh(S, x, b_x, t, b_t, out, b_out, n=None):
    S.op("act", lambda a: a.activation(out=t, in_=x, func=AF.Square), reads=[b_x], writes=[b_t])
    S.op("dve", lambda v: v.tensor_scalar(out=t, in0=t, scalar1=0.044715, scalar2=1.0, op0=ALU.mult, op1=ALU.add),
         reads=[b_t], writes=[b_t])
    S.op("dve", lambda v: v.tensor_tensor(out=t, in0=t, in1=x, op=ALU.mult), reads=[b_t, b_x], writes=[b_t])
    S.op("act", lambda a: a.activation(out=t, in_=t, func=AF.Sigmoid, scale=GELU_C), reads=[b_t], writes=[b_t])
    S.op("dve", lambda v: v.tensor_tensor(out=out, in0=t, in1=x, op=ALU.mult), reads=[b_t, b_x], writes=[b_out])


def phase_LRU(C, s, l):
    nc, S = C.nc, C.S
    N = SEQ
    with ExitStack() as es:
        def T(name, shape, dt=F32):
            return es.enter_context(nc.sbuf_tensor("L_" + name, shape, dt))
        lv = T("lv", [128, 2, 8])
        wa = T("wa", [128, 2, 128]); wx = T("wx", [128, 2, 128])
        b_par = Buf()
        xpad = T("xpad", [128, N + 3]); xc = T("xc", [128, N]); r = T("r", [128, N]); gi = T("gi", [128, N])
        a = T("a", [128, N]); a2 = T("a2", [128, N]); bt = T("bt", [128, N]); h = T("h", [128, N])
        g = T("g", [128, N]); tg = T("tg", [128, N]); y = T("y", [128, N], BF16)
        sp = T("sp", [128, 4]); ones = T("ones", [128, 1])
        b = {k: Buf(k) for k in ("xpad", "xc", "r", "gi", "a", "a2", "bt", "h", "g", "tg", "y", "sp")}
        ps = [es.enter_context(nc.psum_tensor("L_ps%d" % i, [128, 512], F32)) for i in range(4)]
        b_ps = [Buf() for _ in range(4)]
        S.op("sp", lambda q: q.dma_start(out=lv[:], in_=C.lruv[l]), writes=[b_par], dma=True)
        S.op("sp", lambda q: q.dma_start(out=wa[:], in_=C.lru_wa[l]), writes=[b_par], dma=True)
        S.op("sp", lambda q: q.dma_start(out=wx[:], in_=C.lru_wx[l]), writes=[b_par], dma=True)
        S.op("dve", lambda v: v.memset(ones[:], 1.0), writes=[b_par])
        S.op("dve", lambda v: v.memset(xpad[:, 0:3], 0.0), writes=[b["xpad"]])
        mmi = 0
        for ct in range(2):
            S.op("sp", lambda q, ct=ct: q.dma_start(out=xpad[:, 3:3 + N], in_=C.zlru[ct * 128:(ct + 1) * 128, :]),
                 reads=[C.b_zlru[ct]], writes=[b["xpad"]], dma=True)
            S.op("sp", lambda q, ct=ct: q.dma_start(out=g[:], in_=C.zlru[256 + ct * 128:256 + (ct + 1) * 128, :]),
                 reads=[C.b_zlru[2 + ct]], writes=[b["g"]], dma=True)
            S.op("act", lambda e, ct=ct: e.activation(out=sp[:, 0:1], in_=lv[:, ct, 7:8], func=AF.Exp, scale=-1.0),
                 reads=[b_par], writes=[b["sp"]])
            S.op("act", lambda e: e.activation(out=sp[:, 1:2], in_=sp[:, 0:1], func=AF.Ln, scale=1.0, bias=ones[:]),
                 reads=[b["sp"], b_par], writes=[b["sp"]])
            S.op("dve", lambda v: v.tensor_scalar(out=sp[:, 2:3], in0=sp[:, 1:2], scalar1=-8.0, scalar2=None, op0=ALU.mult),
                 reads=[b["sp"]], writes=[b["sp"]])
            S.op("dve", lambda v: v.tensor_scalar(out=sp[:, 3:4], in0=sp[:, 1:2], scalar1=-16.0, scalar2=None, op0=ALU.mult),
                 reads=[b["sp"]], writes=[b["sp"]])
            S.op("dve", lambda v, ct=ct: v.tensor_scalar(out=xc[:], in0=xpad[:, 3:3 + N], scalar1=lv[:, ct, 3:4],
                                                         scalar2=lv[:, ct, 4:5], op0=ALU.mult, op1=ALU.add),
                 reads=[b["xpad"], b_par], writes=[b["xc"]])
            for i in range(3):
                S.op("dve", lambda v, ct=ct, i=i: v.scalar_tensor_tensor(out=xc[:], in0=xpad[:, i:i + N], scalar=lv[:, ct, i:i + 1],
                                                                         in1=xc[:], op0=ALU.mult, op1=ALU.add),
                     reads=[b["xpad"], b["xc"], b_par], writes=[b["xc"]])
            for (w, dst, bi, nm) in ((wa, r, 5, "r"), (wx, gi, 6, "gi")):
                for nb in range(4):
                    pi = mmi % 4
                    mmi += 1
                    ns = slice(nb * 512, (nb + 1) * 512)
                    S.op("pe", lambda pe, w=w, ct=ct, pi=pi, ns=ns: pe.matmul(ps[pi][:], lhsT=w[:, ct, :], rhs=xc[:, ns],
                                                                             start=True, stop=True),
                         reads=[b_par, b["xc"]], writes=[b_ps[pi]])
                    S.op("act", lambda e, dst=dst, pi=pi, ns=ns, ct=ct, bi=bi: e.activation(
                        out=dst[:, ns], in_=ps[pi][:], func=AF.Sigmoid, bias=lv[:, ct, bi:bi + 1]),
                        reads=[b_ps[pi], b_par], writes=[b[nm]])
            S.op("act", lambda e: e.activation(out=a[:], in_=r[:], func=AF.Exp, scale=sp[:, 2:3]),
                 reads=[b["r"], b["sp"]], writes=[b["a"]])
            S.op("act", lambda e: e.activation(out=a2[:], in_=r[:], func=AF.Exp, scale=sp[:, 3:4]),
                 reads=[b["r"], b["sp"]], writes=[b["a2"]])
            S.op("act", lambda e: e.activation(out=a2[:], in_=a2[:], func=AF.Sqrt, scale=-1.0, bias=ones[:]),
                 reads=[b["a2"], b_par], writes=[b["a2"]])
            S.op("dve", lambda v: v.tensor_tensor(out=bt[:], in0=gi[:], in1=xc[:], op=ALU.mult),
                 reads=[b["gi"], b["xc"]], writes=[b["bt"]])
            S.op("dve", lambda v: v.tensor_tensor(out=bt[:], in0=bt[:], in1=a2[:], op=ALU.mult),
                 reads=[b["bt"], b["a2"]], writes=[b["bt"]])
            S.op("dve", lambda v: v.tensor_tensor_scan(out=h[:], data0=a[:], data1=bt[:], initial=0.0,
                                                       op0=ALU.mult, op1=ALU.add),
                 reads=[b["a"], b["bt"]], writes=[b["h"]])
            gelu_tanh(S, g[:], b["g"], tg[:], b["tg"], tg[:], b["tg"])
            S.op("dve", lambda v: v.tensor_tensor(out=y[:], in0=h[:], in1=tg[:], op=ALU.mult),
                 reads=[b["h"], b["tg"]], writes=[b["y"]])
            S.op("sp", lambda q, ct=ct: q.dma_start(out=C.mixT[768 + ct * 128:768 + (ct + 1) * 128, :], in_=y[:]),
                 reads=[b["y"]], writes=[C.b_mixT[6 + ct]], dma=True)


TC = 512
NLEV = 9
MAGIC = 12582912.0
TWO_PI = 2.0 * math.pi
CW1 = 6.28125
CW2 = TWO_PI - 6.28125


def phase_S5(C, s, l):
    nc, S = C.nc, C.S
    with ExitStack() as es:
        def T(name, shape, dt=F32):
            return es.enter_context(nc.sbuf_tensor("S_" + name, shape, dt))
        par = T("par", [128, 8, 3]); b_par = Buf()
        Bre = T("Bre", [128, 8, 128]); Bim = T("Bim", [128, 8, 128]); b_B = Buf()
        Cre = T("Cre", [128, 8, 128]); Cim = T("Cim", [128, 8, 128]); b_Cf = Buf()
        Creb = T("Creb", [128, 8, 128], BF16); nCimb = T("nCimb", [128, 8, 128], BF16); b_Cb = Buf()
        dsk = T("dsk", [128, 2]); wgl = T("wgl", [128, 2, 512]); wglb = T("wglb", [128, 2, 512], BF16); b_w = Buf()
        sm = T("sm", [128, 24, 8]); b_sm = Buf()
        wre = T("wre", [128, NLEV + 1, 8]); wim = T("wim", [128, NLEV + 1, 8]); b_wp = Buf()
        hpi = T("hpi", [128, 1])
        Ec = T("Ec", [128, 8, TC]); Es = T("Es", [128, 8, TC]); b_E = [Buf() for _ in range(8)]
        Mre = T("Mre", [128, 8, 128], BF16); Mim = T("Mim", [128, 8, 128], BF16); Mt = T("Mt", [128, 128]); b_M = Buf(); b_Mt = Buf()
        WB = T("WB", [128, 2, 8, 128], BF16); b_WB = Buf()
        u32 = T("u32", [128, 2, SEQ]); ub = T("ub", [128, 2, SEQ], BF16); b_u = [Buf() for _ in range(2)]; b_ub = [Buf() for _ in range(2)]
        gr = T("gr", [128, 8, TC]); gim = T("gim", [128, 8, TC]); b_g = [Buf() for _ in range(8)]
        init = T("init", [128, 2, 8]); b_init = Buf()
        t1 = T("t1", [128, TC]); t2 = T("t2", [128, TC]); xr = T("xr", [128, TC]); xi = T("xi", [128, TC])
        b_t1 = Buf(); b_t2 = Buf(); b_xr = Buf(); b_xi = Buf()
        p1 = T("p1", [128, TC]); p2 = T("p2", [128, TC]); b_p1 = Buf(); b_p2 = Buf()
        p3 = T("p3", [128, TC]); b_p3 = Buf()
        hre = T("hre", [128, 8, TC], BF16); him = T("him", [128, 8, TC], BF16); b_h = [Buf() for _ in range(8)]
        yv = T("yv", [128, TC]); yt = T("yt", [128, TC]); b_yv = Buf(); b_yt = Buf()
        yg = T("yg", [128, 2, TC], BF16); b_yg = [Buf() for _ in range(2)]
        sgl = T("sgl", [128, TC]); b_sgl = Buf()
        og = [T("og%d" % i, [128, TC], BF16) for i in range(2)]; b_og = [Buf() for _ in range(2)]
        psX = [es.enter_context(nc.psum_tensor("S_psX%d" % i, [128, 512], F32)) for i in range(4)]
        b_psX = [Buf() for _ in range(4)]
        psY = [es.enter_context(nc.psum_tensor("S_psY%d" % i, [128, 512], F32)) for i in range(2)]
        b_psY = [Buf() for _ in range(2)]
        psT = es.enter_context(nc.psum_tensor("S_psT", [128, 8, 128], BF16)); b_psT = Buf()

        def dve(fn, reads, writes):
            S.op("dve", fn, reads=reads, writes=writes)

        def act(fn, reads, writes):
            S.op("act", fn, reads=reads, writes=writes)

        S.op("sp", lambda q: q.dma_start(out=par[:], in_=C.s5p[l]), writes=[b_par], dma=True)
        S.op("sp", lambda q: q.dma_start(out=Bre[:], in_=C.s5_bre[l]), writes=[b_B], dma=True)
        S.op("sp", lambda q: q.dma_start(out=Bim[:], in_=C.s5_bim[l]), writes=[b_B], dma=True)
        S.op("sp", lambda q: q.dma_start(out=Cre[:], in_=C.s5_cre[l]), writes=[b_Cf], dma=True)
        S.op("sp", lambda q: q.dma_start(out=Cim[:], in_=C.s5_cim[l]), writes=[b_Cf], dma=True)
        S.op("sp", lambda q: q.dma_start(out=dsk[:], in_=C.s5_dsk[l]), writes=[b_w], dma=True)
        S.op("sp", lambda q: q.dma_start(out=wgl[:], in_=C.s5_wglu[l].rearrange("(c p) f -> p c f", p=128)), writes=[b_w], dma=True)
        for ct in range(2):
            S.op("sp", lambda q, ct=ct: q.dma_start(out=u32[:, ct, :], in_=C.zu[ct * 128:(ct + 1) * 128, :]),
                 reads=[C.b_zu[ct]], writes=[b_u[ct]], dma=True)
            S.op("act", lambda v, ct=ct: v.copy(out=ub[:, ct, :], in_=u32[:, ct, :]), reads=[b_u[ct]], writes=[b_ub[ct]])
        act(lambda a: a.copy(out=wglb[:], in_=wgl[:]), [b_w], [b_w])
        act(lambda a: a.copy(out=Creb[:], in_=Cre[:]), [b_Cf], [b_Cb])
        act(lambda a: a.mul(out=nCimb[:], in_=Cim[:], mul=-1.0), [b_Cf], [b_Cb])
        dve(lambda v: v.memset(hpi[:], math.pi / 2.0), [], [b_sm])

        LR, LI, LDT = par[:, :, 0], par[:, :, 1], par[:, :, 2]
        sl = lambda i: sm[:, i, :]
        DT, MAG, ANG, K_, R_, AR, SN, CS, ABR, ABI, DEN, T1_, T2_, FRE, FIM = range(15)
        act(lambda a: a.activation(out=sl(DT), in_=LDT, func=AF.Exp), [b_par], [b_sm])
        dve(lambda v: v.tensor_tensor(out=sl(MAG), in0=LR, in1=sl(DT), op=ALU.mult), [b_par, b_sm], [b_sm])
        act(lambda a: a.activation(out=sl(MAG), in_=sl(MAG), func=AF.Exp), [b_sm], [b_sm])
        dve(lambda v: v.tensor_tensor(out=sl(ANG), in0=LI, in1=sl(DT), op=ALU.mult), [b_par, b_sm], [b_sm])
        dve(lambda v: v.tensor_scalar(out=sl(K_), in0=sl(ANG), scalar1=1.0 / TWO_PI, scalar2=MAGIC, op0=ALU.mult, op1=ALU.add), [b_sm], [b_sm])
        dve(lambda v: v.tensor_scalar(out=sl(K_), in0=sl(K_), scalar1=-MAGIC, scalar2=None, op0=ALU.add), [b_sm], [b_sm])
        dve(lambda v: v.scalar_tensor_tensor(out=sl(R_), in0=sl(K_), scalar=-CW1, in1=sl(ANG), op0=ALU.mult, op1=ALU.add), [b_sm], [b_sm])
        dve(lambda v: v.scalar_tensor_tensor(out=sl(R_), in0=sl(K_), scalar=-CW2, in1=sl(R_), op0=ALU.mult, op1=ALU.add), [b_sm], [b_sm])
        dve(lambda v: v.tensor_scalar(out=sl(R_), in0=sl(R_), scalar1=math.pi, scalar2=-math.pi, op0=ALU.min, op1=ALU.max), [b_sm], [b_sm])
        act(lambda a: a.activation(out=sl(AR), in_=sl(R_), func=AF.Abs), [b_sm], [b_sm])
        act(lambda a: a.activation(out=sl(SN), in_=sl(R_), func=AF.Sin), [b_sm], [b_sm])
        act(lambda a: a.activation(out=sl(CS), in_=sl(AR), func=AF.Sin, scale=-1.0, bias=hpi[:]), [b_sm], [b_sm])
        dve(lambda v: v.tensor_tensor(out=sl(ABR), in0=sl(MAG), in1=sl(CS), op=ALU.mult), [b_sm], [b_sm])
        dve(lambda v: v.tensor_tensor(out=sl(ABI), in0=sl(MAG), in1=sl(SN), op=ALU.mult), [b_sm], [b_sm])
        dve(lambda v: v.tensor_tensor(out=sl(DEN), in0=LR, in1=LR, op=ALU.mult), [b_par, b_sm], [b_sm])
        dve(lambda v: v.tensor_tensor(out=sl(T1_), in0=LI, in1=LI, op=ALU.mult), [b_par, b_sm], [b_sm])
        dve(lambda v: v.tensor_tensor(out=sl(DEN), in0=sl(DEN), in1=sl(T1_), op=ALU.add), [b_sm], [b_sm])
        dve(lambda v: v.reciprocal(out=sl(DEN), in_=sl(DEN)), [b_sm], [b_sm])
        dve(lambda v: v.tensor_scalar(out=sl(T1_), in0=sl(ABR), scalar1=-1.0, scalar2=None, op0=ALU.add), [b_sm], [b_sm])
        dve(lambda v: v.tensor_tensor(out=sl(FRE), in0=sl(T1_), in1=LR, op=ALU.mult), [b_par, b_sm], [b_sm])
        dve(lambda v: v.tensor_tensor(out=sl(T2_), in0=sl(ABI), in1=LI, op=ALU.mult), [b_par, b_sm], [b_sm])
        dve(lambda v: v.tensor_tensor(out=sl(FRE), in0=sl(FRE), in1=sl(T2_), op=ALU.add), [b_sm], [b_sm])
        dve(lambda v: v.tensor_tensor(out=sl(FRE), in0=sl(FRE), in1=sl(DEN), op=ALU.mult), [b_sm], [b_sm])
        dve(lambda v: v.tensor_tensor(out=sl(FIM), in0=sl(ABI), in1=LR, op=ALU.mult), [b_par, b_sm], [b_sm])
        dve(lambda v: v.tensor_tensor(out=sl(T2_), in0=sl(T1_), in1=LI, op=ALU.mult), [b_par, b_sm], [b_sm])
        dve(lambda v: v.tensor_tensor(out=sl(FIM), in0=sl(FIM), in1=sl(T2_), op=ALU.subtract), [b_sm], [b_sm])
        dve(lambda v: v.tensor_tensor(out=sl(FIM), in0=sl(FIM), in1=sl(DEN), op=ALU.mult), [b_sm], [b_sm])
        dve(lambda v: v.tensor_copy(out=wre[:, 0, :], in_=sl(CS)), [b_sm], [b_wp])
        dve(lambda v: v.tensor_copy(out=wim[:, 0, :], in_=sl(SN)), [b_sm], [b_wp])
        for k in range(NLEV):
            dve(lambda v, k=k: v.tensor_tensor(out=sl(T1_), in0=wre[:, k, :], in1=wre[:, k, :], op=ALU.mult), [b_wp, b_sm], [b_sm])
            dve(lambda v, k=k: v.tensor_tensor(out=sl(T2_), in0=wim[:, k, :], in1=wim[:, k, :], op=ALU.mult), [b_wp, b_sm], [b_sm])
            dve(lambda v, k=k: v.tensor_tensor(out=wre[:, k + 1, :], in0=sl(T1_), in1=sl(T2_), op=ALU.subtract), [b_sm, b_wp], [b_wp])
            dve(lambda v, k=k: v.tensor_tensor(out=sl(T1_), in0=wre[:, k, :], in1=wim[:, k, :], op=ALU.mult), [b_wp, b_sm], [b_sm])
            dve(lambda v, k=k: v.tensor_scalar(out=wim[:, k + 1, :], in0=sl(T1_), scalar1=2.0, scalar2=None, op0=ALU.mult), [b_sm, b_wp], [b_wp])
        tA = T("tA", [128, 8, TC // 2]); tB = T("tB", [128, 8, TC // 2]); b_tA = Buf(); b_tB = Buf()
        b_Eall = Buf()
        dve(lambda v: v.memset(Ec[:, :, 0:1], 1.0), [], [b_Eall])
        dve(lambda v: v.memset(Es[:, :, 0:1], 0.0), [], [b_Eall])
        for k in range(NLEV):
            n = 1 << k
            wrb = wre[:, k, :].unsqueeze(2).to_broadcast([128, 8, n])
            wib = wim[:, k, :].unsqueeze(2).to_broadcast([128, 8, n])
            dve(lambda v, n=n, wib=wib: v.tensor_tensor(out=tA[:, :, 0:n], in0=Es[:, :, 0:n], in1=wib, op=ALU.mult), [b_Eall, b_wp], [b_tA])
            dve(lambda v, n=n, wrb=wrb: v.tensor_tensor(out=tB[:, :, 0:n], in0=Ec[:, :, 0:n], in1=wrb, op=ALU.mult), [b_Eall, b_wp], [b_tB])
            dve(lambda v, n=n: v.tensor_tensor(out=Ec[:, :, n:2 * n], in0=tB[:, :, 0:n], in1=tA[:, :, 0:n], op=ALU.subtract), [b_tA, b_tB], [b_Eall])
            dve(lambda v, n=n, wib=wib: v.tensor_tensor(out=tA[:, :, 0:n], in0=Ec[:, :, 0:n], in1=wib, op=ALU.mult), [b_Eall, b_wp], [b_tA])
            dve(lambda v, n=n, wrb=wrb: v.tensor_tensor(out=tB[:, :, 0:n], in0=Es[:, :, 0:n], in1=wrb, op=ALU.mult), [b_Eall, b_wp], [b_tB])
            dve(lambda v, n=n: v.tensor_tensor(out=Es[:, :, n:2 * n], in0=tB[:, :, 0:n], in1=tA[:, :, 0:n], op=ALU.add), [b_tA, b_tB], [b_Eall])
        for st in range(8):
            dve(lambda v: v.engine_nop() if hasattr(v, "engine_nop") else v.memset(hpi[:], math.pi / 2.0), [b_Eall], [b_E[st]]) if False else None
            b_E[st].last_w = b_Eall.last_w
        frb = sm[:, FRE, :].unsqueeze(2).to_broadcast([128, 8, 128])
        fib = sm[:, FIM, :].unsqueeze(2).to_broadcast([128, 8, 128])
        MtA = T("MtA", [128, 8, 128]); MtB = T("MtB", [128, 8, 128])
        for (dst, A_, B_, sgn) in ((Mre, Bre, Bim, ALU.subtract), (Mim, Bim, Bre, ALU.add)):
            dve(lambda v, B_=B_: v.tensor_tensor(out=MtA[:], in0=B_[:], in1=fib, op=ALU.mult), [b_B, b_sm], [b_Mt])
            dve(lambda v, A_=A_: v.tensor_tensor(out=MtB[:], in0=A_[:], in1=frb, op=ALU.mult), [b_B, b_sm], [b_Mt])
            dve(lambda v, dst=dst, sgn=sgn: v.tensor_tensor(out=dst[:], in0=MtB[:], in1=MtA[:], op=sgn), [b_Mt], [b_M])
        for ri, Msrc in enumerate((Mre, Mim)):
            for st in range(8):
                S.op("pe", lambda pe, st=st, Msrc=Msrc: pe.transpose(psT[:, st, :], Msrc[:, st, :], C.identb[:]),
                     reads=[b_M, C.b_const], writes=[b_psT])
            act(lambda a, ri=ri: a.copy(out=WB[:, ri, :, :], in_=psT[:]), [b_psT], [b_WB])
        dve(lambda v: v.memset(init[:], 0.0), [], [b_init])

        xi_ = 0
        for k in range(SEQ // TC):
            ts = slice(k * TC, (k + 1) * TC)
            for st in range(8):
                ct = st // 4
                pr, pim = xi_ % 4, (xi_ + 1) % 4
                xi_ += 2
                S.op("pe", lambda pe, st=st, ct=ct, pr=pr, ts=ts: pe.matmul(psX[pr][:], lhsT=WB[:, 0, st, :], rhs=ub[:, ct, ts], start=True, stop=True),
                     reads=[b_WB, b_ub[ct]], writes=[b_psX[pr]])
                S.op("pe", lambda pe, st=st, ct=ct, pim=pim, ts=ts: pe.matmul(psX[pim][:], lhsT=WB[:, 1, st, :], rhs=ub[:, ct, ts], start=True, stop=True),
                     reads=[b_WB, b_ub[ct]], writes=[b_psX[pim]])
                dve(lambda v, st=st, pr=pr: v.tensor_tensor(out=t1[:], in0=psX[pr][:], in1=Ec[:, st, :], op=ALU.mult), [b_psX[pr], b_E[st]], [b_t1])
                dve(lambda v, st=st, pim=pim: v.tensor_tensor(out=t2[:], in0=psX[pim][:], in1=Es[:, st, :], op=ALU.mult), [b_psX[pim], b_E[st]], [b_t2])
                dve(lambda v: v.tensor_tensor(out=xr[:], in0=t1[:], in1=t2[:], op=ALU.add), [b_t1, b_t2], [b_xr])
                dve(lambda v, st=st, pim=pim: v.tensor_tensor(out=t1[:], in0=psX[pim][:], in1=Ec[:, st, :], op=ALU.mult), [b_psX[pim], b_E[st]], [b_t1])
                dve(lambda v, st=st, pr=pr: v.tensor_tensor(out=t2[:], in0=psX[pr][:], in1=Es[:, st, :], op=ALU.mult), [b_psX[pr], b_E[st]], [b_t2])
                dve(lambda v: v.tensor_tensor(out=xi[:], in0=t1[:], in1=t2[:], op=ALU.subtract), [b_t1, b_t2], [b_xi])
                dve(lambda v, st=st: v.tensor_tensor_scan(out=gr[:, st, :], data0=C.bcast(sm[:, MAG, st:st + 1], TC), data1=xr[:],
                                                          initial=init[:, 0, st:st + 1], op0=ALU.mult, op1=ALU.add),
                    [b_sm, b_xr, b_init], [b_g[st]])
                dve(lambda v, st=st: v.tensor_tensor_scan(out=gim[:, st, :], data0=C.bcast(sm[:, MAG, st:st + 1], TC), data1=xi[:],
                                                          initial=init[:, 1, st:st + 1], op0=ALU.mult, op1=ALU.add),
                    [b_sm, b_xi, b_init], [b_g[st]])
                pl = lambda fn, reads, writes: S.op("pool", fn, reads=reads, writes=writes)
                dve(lambda v, st=st: v.tensor_tensor(out=p3[:], in0=gr[:, st, :], in1=Ec[:, st, :], op=ALU.mult), [b_g[st], b_E[st]], [b_p3])
                pl(lambda v, st=st: v.tensor_tensor(out=p2[:], in0=gim[:, st, :], in1=Es[:, st, :], op=ALU.mult), [b_g[st], b_E[st]], [b_p2])
                pl(lambda v, st=st: v.tensor_tensor(out=hre[:, st, :], in0=p3[:], in1=p2[:], op=ALU.subtract), [b_p3, b_p2], [b_h[st]])
                pl(lambda v, st=st: v.tensor_tensor(out=p1[:], in0=gim[:, st, :], in1=Ec[:, st, :], op=ALU.mult), [b_g[st], b_E[st]], [b_p1])
                pl(lambda v, st=st: v.tensor_tensor(out=p2[:], in0=gr[:, st, :], in1=Es[:, st, :], op=ALU.mult), [b_g[st], b_E[st]], [b_p2])
                pl(lambda v, st=st: v.tensor_tensor(out=him[:, st, :], in0=p1[:], in1=p2[:], op=ALU.add), [b_p1, b_p2], [b_h[st]])
            if k + 1 < SEQ // TC:
                glr, gli = gr[:, :, TC - 1], gim[:, :, TC - 1]
                wr_, wi_ = wre[:, NLEV, :], wim[:, NLEV, :]
                dve(lambda v: v.tensor_tensor(out=sl(T1_), in0=glr, in1=wr_, op=ALU.mult), b_g + [b_wp, b_sm], [b_sm])
                dve(lambda v: v.tensor_tensor(out=sl(T2_), in0=gli, in1=wi_, op=ALU.mult), b_g + [b_wp, b_sm], [b_sm])
                dve(lambda v: v.tensor_tensor(out=init[:, 0, :], in0=sl(T1_), in1=sl(T2_), op=ALU.subtract), [b_sm, b_init], [b_init])
                dve(lambda v: v.tensor_tensor(out=sl(T1_), in0=gli, in1=wr_, op=ALU.mult), b_g + [b_wp, b_sm], [b_sm])
                dve(lambda v: v.tensor_tensor(out=sl(T2_), in0=glr, in1=wi_, op=ALU.mult), b_g + [b_wp, b_sm], [b_sm])
                dve(lambda v: v.tensor_tensor(out=init[:, 1, :], in0=sl(T1_), in1=sl(T2_), op=ALU.add), [b_sm, b_init], [b_init])
            for ct in range(2):
                py = ct
                for j in range(4):
                    st = ct * 4 + j
                    S.op("pe", lambda pe, st=st, py=py, j=j: pe.matmul(psY[py][:], lhsT=Creb[:, st, :], rhs=hre[:, st, :], start=(j == 0), stop=False),
                         reads=[b_Cb, b_h[st]], writes=[b_psY[py]])
                    S.op("pe", lambda pe, st=st, py=py, j=j: pe.matmul(psY[py][:], lhsT=nCimb[:, st, :], rhs=him[:, st, :], start=False, stop=(j == 3)),
                         reads=[b_Cb, b_h[st]], writes=[b_psY[py]])
                dve(lambda v, ct=ct, py=py, ts=ts: v.scalar_tensor_tensor(out=yv[:], in0=u32[:, ct, ts], scalar=dsk[:, ct:ct + 1], in1=psY[py][:],
                                                                         op0=ALU.mult, op1=ALU.add), [b_u[ct], b_w, b_psY[py]], [b_yv])
                gelu_tanh(S, yv[:], b_yv, yt[:], b_yt, yg[:, ct, :], b_yg[ct])
            for j in range(2):
                S.op("pe", lambda pe, j=j: pe.matmul(psY[0][:], lhsT=wglb[:, 0, j * 128:(j + 1) * 128], rhs=yg[:, 0, :], start=True, stop=False),
                     reads=[b_w] + b_yg, writes=[b_psY[0]])
                S.op("pe", lambda pe, j=j: pe.matmul(psY[0][:], lhsT=wglb[:, 1, j * 128:(j + 1) * 128], rhs=yg[:, 1, :], start=False, stop=True),
                     reads=[b_w] + b_yg, writes=[b_psY[0]])
                S.op("pe", lambda pe, j=j: pe.matmul(psY[1][:], lhsT=wglb[:, 0, 256 + j * 128:256 + (j + 1) * 128], rhs=yg[:, 0, :], start=True, stop=False),
                     reads=[b_w] + b_yg, writes=[b_psY[1]])
                S.op("pe", lambda pe, j=j: pe.matmul(psY[1][:], lhsT=wglb[:, 1, 256 + j * 128:256 + (j + 1) * 128], rhs=yg[:, 1, :], start=False, stop=True),
                     reads=[b_w] + b_yg, writes=[b_psY[1]])
                act(lambda a: a.activation(out=sgl[:], in_=psY[1][:], func=AF.Sigmoid), [b_psY[1]], [b_sgl])
                dve(lambda v, j=j: v.tensor_tensor(out=og[j][:], in0=psY[0][:], in1=sgl[:], op=ALU.mult), [b_psY[0], b_sgl], [b_og[j]])
                S.op("sp", lambda q, j=j, ts=ts: q.dma_start(out=C.mixT[j * 128:(j + 1) * 128, ts], in_=og[j][:]),
                     reads=[b_og[j]], writes=[C.b_mixT[j]], dma=True)


LB = 8
NBLK = SEQ // LB
NW = 11


def phase_S5B(C, s, l):
    nc, S = C.nc, C.S
    with ExitStack() as es:
        def T(name, shape, dt=F32):
            return es.enter_context(nc.sbuf_tensor("B_" + name, shape, dt))
        par = T("par", [128, 8, 3]); b_par = Buf()
        Bre = T("Bre", [128, 8, 128]); Bim = T("Bim", [128, 8, 128]); b_B = Buf()
        Cre = T("Cre", [128, 8, 128]); Cim = T("Cim", [128, 8, 128]); b_Cf = Buf()
        dsk = T("dsk", [128, 2]); wgl = T("wgl", [128, 2, 512]); wglb = T("wglb", [128, 2, 512], BF16); b_w = Buf()
        sm = T("sm", [128, 24, 8]); b_sm = Buf()
        wre = T("wre", [128, NW + 1, 8]); wim = T("wim", [128, NW + 1, 8]); b_wp = Buf()
        hpi = T("hpi", [128, 1])
        pwr = T("pwr", [128, LB + 1, 8]); pwi = T("pwi", [128, LB + 1, 8]); b_pw = Buf()
        r8 = T("r8", [128, 8]); b_r8 = Buf()
        Ec = T("Ec", [128, 8, NBLK]); Es = T("Es", [128, 8, NBLK]); b_Eall = Buf()
        tA = T("tA", [128, 8, NBLK // 2]); tB = T("tB", [128, 8, NBLK // 2]); b_tA = Buf(); b_tB = Buf()
        MbR = T("MbR", [128, 8, 128]); MbI = T("MbI", [128, 8, 128]); b_Mb = Buf()
        MtA = T("MtA", [128, 8, 128]); MtB = T("MtB", [128, 8, 128]); b_Mt = Buf()
        PtA = T("PtA", [128, 8, 128]); PtB = T("PtB", [128, 8, 128]); b_Pt = Buf()
        vst = [T("vst%d" % i, [128, 8, 128], BF16) for i in range(2)]; b_vst = [Buf() for _ in range(2)]
        W1T = T("W1T", [128, LB, 2, 8, 128], BF16); b_W1T = Buf()
        CaR = T("CaR", [128, LB + 1, 8, 128], BF16); nCaI = T("nCaI", [128, LB + 1, 8, 128], BF16); b_Ca = Buf()
        BbR = T("BbR", [128, 8, 128], BF16); BbI = T("BbI", [128, 8, 128], BF16); b_Bb = Buf()
        Kt = T("Kt", [128, 2, LB, 128], BF16); b_Kt = Buf()
        u32 = T("u32", [128, 2, SEQ]); b_u = [Buf() for _ in range(2)]
        uS = T("uS", [128, 2, LB, NBLK], BF16); b_uS = [Buf() for _ in range(2)]
        Hs = T("Hs", [128, 8, 2, NBLK], BF16); b_Hs = [Buf() for _ in range(8)]
        t1 = T("t1", [128, NBLK]); t2 = T("t2", [128, NBLK]); xr = T("xr", [128, NBLK]); xi = T("xi", [128, NBLK])
        gr = T("gr", [128, NBLK]); gi = T("gi", [128, NBLK]); p1 = T("p1", [128, NBLK]); p2 = T("p2", [128, NBLK])
        b_t1 = Buf(); b_t2 = Buf(); b_xr = Buf(); b_xi = Buf(); b_gr = Buf(); b_gi = Buf(); b_p1 = Buf(); b_p2 = Buf()
        yv = T("yv", [128, 512]); yt = T("yt", [128, 512]); b_yv = Buf(); b_yt = Buf()
        yg = T("yg", [128, 2, 512], BF16); b_yg = [Buf() for _ in range(2)]
        sgl = T("sgl", [128, 512]); b_sgl = Buf()
        og = [T("og%d" % i, [128, 512], BF16) for i in range(2)]; b_og = [Buf() for _ in range(2)]
        psT = es.enter_context(nc.psum_tensor("B_psT", [128, 8, 128], BF16)); b_psT = Buf()
        psSb = es.enter_context(nc.psum_tensor("B_psS", [128, 2, NBLK], F32)); b_psS = Buf()
        psK = es.enter_context(nc.psum_tensor("B_psK", [128, 4, 128], F32)); b_psK = Buf()
        psY = [es.enter_context(nc.psum_tensor("B_psY%d" % i, [128, 512], F32)) for i in range(2)]; b_psY = [Buf() for _ in range(2)]
        psG = [es.enter_context(nc.psum_tensor("B_psG%d" % i, [128, 512], F32)) for i in range(2)]; b_psG = [Buf() for _ in range(2)]

        def dve(fn, reads, writes):
            S.op("dve", fn, reads=reads, writes=writes)

        def act(fn, reads, writes):
            S.op("act", fn, reads=reads, writes=writes)

        def pl(fn, reads, writes):
            S.op("pool", fn, reads=reads, writes=writes)

        def pe(fn, reads, writes):
            S.op("pe", fn, reads=reads, writes=writes)

        S.op("sp", lambda q: q.dma_start(out=par[:], in_=C.s5p[l]), writes=[b_par], dma=True)
        S.op("sp", lambda q: q.dma_start(out=Bre[:], in_=C.s5_bre[l]), writes=[b_B], dma=True)
        S.op("sp", lambda q: q.dma_start(out=Bim[:], in_=C.s5_bim[l]), writes=[b_B], dma=True)
        S.op("sp", lambda q: q.dma_start(out=Cre[:], in_=C.s5_cre[l]), writes=[b_Cf], dma=True)
        S.op("sp", lambda q: q.dma_start(out=Cim[:], in_=C.s5_cim[l]), writes=[b_Cf], dma=True)
        S.op("sp", lambda q: q.dma_start(out=dsk[:], in_=C.s5_dsk[l]), writes=[b_w], dma=True)
        S.op("sp", lambda q: q.dma_start(out=wgl[:], in_=C.s5_wglu[l].rearrange("(c p) f -> p c f", p=128)), writes=[b_w], dma=True)
        for ct in range(2):
            S.op("sp", lambda q, ct=ct: q.dma_start(out=u32[:, ct, :], in_=C.zu[ct * 128:(ct + 1) * 128, :]),
                 reads=[C.b_zu[ct]], writes=[b_u[ct]], dma=True)
            pl(lambda v, ct=ct: v.tensor_copy(out=uS[:, ct, :, :], in_=u32[:, ct, :].rearrange("p (k j) -> p j k", j=LB)), [b_u[ct]], [b_uS[ct]])
        act(lambda a: a.copy(out=wglb[:], in_=wgl[:]), [b_w], [b_w])
        dve(lambda v: v.memset(hpi[:], math.pi / 2.0), [], [b_sm])
        LR, LI, LDT = par[:, :, 0], par[:, :, 1], par[:, :, 2]
        sl = lambda i: sm[:, i, :]
        DT, MAG, ANG, K_, R_, AR, SN, CS, ABR, ABI, DEN, T1_, T2_, FRE, FIM = range(15)
        act(lambda a: a.activation(out=sl(DT), in_=LDT, func=AF.Exp), [b_par], [b_sm])
        dve(lambda v: v.tensor_tensor(out=sl(MAG), in0=LR, in1=sl(DT), op=ALU.mult), [b_par, b_sm], [b_sm])
        act(lambda a: a.activation(out=sl(MAG), in_=sl(MAG), func=AF.Exp), [b_sm], [b_sm])
        dve(lambda v: v.tensor_tensor(out=sl(ANG), in0=LI, in1=sl(DT), op=ALU.mult), [b_par, b_sm], [b_sm])
        dve(lambda v: v.tensor_scalar(out=sl(K_), in0=sl(ANG), scalar1=1.0 / TWO_PI, scalar2=MAGIC, op0=ALU.mult, op1=ALU.add), [b_sm], [b_sm])
        dve(lambda v: v.tensor_scalar(out=sl(K_), in0=sl(K_), scalar1=-MAGIC, scalar2=None, op0=ALU.add), [b_sm], [b_sm])
        dve(lambda v: v.scalar_tensor_tensor(out=sl(R_), in0=sl(K_), scalar=-CW1, in1=sl(ANG), op0=ALU.mult, op1=ALU.add), [b_sm], [b_sm])
        dve(lambda v: v.scalar_tensor_tensor(out=sl(R_), in0=sl(K_), scalar=-CW2, in1=sl(R_), op0=ALU.mult, op1=ALU.add), [b_sm], [b_sm])
        dve(lambda v: v.tensor_scalar(out=sl(R_), in0=sl(R_), scalar1=math.pi, scalar2=-math.pi, op0=ALU.min, op1=ALU.max), [b_sm], [b_sm])
        act(lambda a: a.activation(out=sl(AR), in_=sl(R_), func=AF.Abs), [b_sm], [b_sm])
        act(lambda a: a.activation(out=sl(SN), in_=sl(R_), func=AF.Sin), [b_sm], [b_sm])
        act(lambda a: a.activation(out=sl(CS), in_=sl(AR), func=AF.Sin, scale=-1.0, bias=hpi[:]), [b_sm], [b_sm])
        dve(lambda v: v.tensor_tensor(out=sl(ABR), in0=sl(MAG), in1=sl(CS), op=ALU.mult), [b_sm], [b_sm])
        dve(lambda v: v.tensor_tensor(out=sl(ABI), in0=sl(MAG), in1=sl(SN), op=ALU.mult), [b_sm], [b_sm])
        dve(lambda v: v.tensor_tensor(out=sl(DEN), in0=LR, in1=LR, op=ALU.mult), [b_par, b_sm], [b_sm])
        dve(lambda v: v.tensor_tensor(out=sl(T1_), in0=LI, in1=LI, op=ALU.mult), [b_par, b_sm], [b_sm])
        dve(lambda v: v.tensor_tensor(out=sl(DEN), in0=sl(DEN), in1=sl(T1_), op=ALU.add), [b_sm], [b_sm])
        dve(lambda v: v.reciprocal(out=sl(DEN), in_=sl(DEN)), [b_sm], [b_sm])
        dve(lambda v: v.tensor_scalar(out=sl(T1_), in0=sl(ABR), scalar1=-1.0, scalar2=None, op0=ALU.add), [b_sm], [b_sm])
        dve(lambda v: v.tensor_tensor(out=sl(FRE), in0=sl(T1_), in1=LR, op=ALU.mult), [b_par, b_sm], [b_sm])
        dve(lambda v: v.tensor_tensor(out=sl(T2_), in0=sl(ABI), in1=LI, op=ALU.mult), [b_par, b_sm], [b_sm])
        dve(lambda v: v.tensor_tensor(out=sl(FRE), in0=sl(FRE), in1=sl(T2_), op=ALU.add), [b_sm], [b_sm])
        dve(lambda v: v.tensor_tensor(out=sl(FRE), in0=sl(FRE), in1=sl(DEN), op=ALU.mult), [b_sm], [b_sm])
        dve(lambda v: v.tensor_tensor(out=sl(FIM), in0=sl(ABI), in1=LR, op=ALU.mult), [b_par, b_sm], [b_sm])
        dve(lambda v: v.tensor_tensor(out=sl(T2_), in0=sl(T1_), in1=LI, op=ALU.mult), [b_par, b_sm], [b_sm])
        dve(lambda v: v.tensor_tensor(out=sl(FIM), in0=sl(FIM), in1=sl(T2_), op=ALU.subtract), [b_sm], [b_sm])
        dve(lambda v: v.tensor_tensor(out=sl(FIM), in0=sl(FIM), in1=sl(DEN), op=ALU.mult), [b_sm], [b_sm])
        dve(lambda v: v.tensor_copy(out=wre[:, 0, :], in_=sl(CS)), [b_sm], [b_wp])
        dve(lambda v: v.tensor_copy(out=wim[:, 0, :], in_=sl(SN)), [b_sm], [b_wp])
        for k in range(NW):
            dve(lambda v, k=k: v.tensor_tensor(out=sl(T1_), in0=wre[:, k, :], in1=wre[:, k, :], op=ALU.mult), [b_wp, b_sm], [b_sm])
            dve(lambda v, k=k: v.tensor_tensor(out=sl(T2_), in0=wim[:, k, :], in1=wim[:, k, :], op=ALU.mult), [b_wp, b_sm], [b_sm])
            dve(lambda v, k=k: v.tensor_tensor(out=wre[:, k + 1, :], in0=sl(T1_), in1=sl(T2_), op=ALU.subtract), [b_sm, b_wp], [b_wp])
            dve(lambda v, k=k: v.tensor_tensor(out=sl(T1_), in0=wre[:, k, :], in1=wim[:, k, :], op=ALU.mult), [b_wp, b_sm], [b_sm])
            dve(lambda v, k=k: v.tensor_scalar(out=wim[:, k + 1, :], in0=sl(T1_), scalar1=2.0, scalar2=None, op0=ALU.mult), [b_sm, b_wp], [b_wp])

        dve(lambda v: v.tensor_tensor(out=r8[:], in0=sl(MAG), in1=sl(MAG), op=ALU.mult), [b_sm], [b_r8])
        dve(lambda v: v.tensor_tensor(out=r8[:], in0=r8[:], in1=r8[:], op=ALU.mult), [b_r8], [b_r8])
        dve(lambda v: v.tensor_tensor(out=r8[:], in0=r8[:], in1=r8[:], op=ALU.mult), [b_r8], [b_r8])
        if s == 0:
            dve(lambda v: v.memset(pwr[:, 0, :], 1.0), [], [b_pw])
            dve(lambda v: v.memset(pwi[:, 0, :], 0.0), [], [b_pw])
            for q_ in range(LB):
                dve(lambda v, q_=q_: v.tensor_tensor(out=sl(T1_), in0=pwr[:, q_, :], in1=sl(ABR), op=ALU.mult), [b_pw, b_sm], [b_sm])
                dve(lambda v, q_=q_: v.tensor_tensor(out=sl(T2_), in0=pwi[:, q_, :], in1=sl(ABI), op=ALU.mult), [b_pw, b_sm], [b_sm])
                dve(lambda v, q_=q_: v.tensor_tensor(out=pwr[:, q_ + 1, :], in0=sl(T1_), in1=sl(T2_), op=ALU.subtract), [b_sm, b_pw], [b_pw])
                dve(lambda v, q_=q_: v.tensor_tensor(out=sl(T1_), in0=pwr[:, q_, :], in1=sl(ABI), op=ALU.mult), [b_pw, b_sm], [b_sm])
                dve(lambda v, q_=q_: v.tensor_tensor(out=sl(T2_), in0=pwi[:, q_, :], in1=sl(ABR), op=ALU.mult), [b_pw, b_sm], [b_sm])
                dve(lambda v, q_=q_: v.tensor_tensor(out=pwi[:, q_ + 1, :], in0=sl(T1_), in1=sl(T2_), op=ALU.add), [b_sm, b_pw], [b_pw])
            dve(lambda v: v.memset(Ec[:, :, 0:1], 1.0), [], [b_Eall])
            dve(lambda v: v.memset(Es[:, :, 0:1], 0.0), [], [b_Eall])
            for k in range(8):
                n = 1 << k
                wrb = wre[:, k + 3, :].unsqueeze(2).to_broadcast([128, 8, n])
                wib = wim[:, k + 3, :].unsqueeze(2).to_broadcast([128, 8, n])
                dve(lambda v, n=n, wib=wib: v.tensor_tensor(out=tA[:, :, 0:n], in0=Es[:, :, 0:n], in1=wib, op=ALU.mult), [b_Eall, b_wp], [b_tA])
                dve(lambda v, n=n, wrb=wrb: v.tensor_tensor(out=tB[:, :, 0:n], in0=Ec[:, :, 0:n], in1=wrb, op=ALU.mult), [b_Eall, b_wp], [b_tB])
                dve(lambda v, n=n: v.tensor_tensor(out=Ec[:, :, n:2 * n], in0=tB[:, :, 0:n], in1=tA[:, :, 0:n], op=ALU.subtract), [b_tA, b_tB], [b_Eall])
                dve(lambda v, n=n, wib=wib: v.tensor_tensor(out=tA[:, :, 0:n], in0=Ec[:, :, 0:n], in1=wib, op=ALU.mult), [b_Eall, b_wp], [b_tA])
                dve(lambda v, n=n, wrb=wrb: v.tensor_tensor(out=tB[:, :, 0:n], in0=Es[:, :, 0:n], in1=wrb, op=ALU.mult), [b_Eall, b_wp], [b_tB])
                dve(lambda v, n=n: v.tensor_tensor(out=Es[:, :, n:2 * n], in0=tB[:, :, 0:n], in1=tA[:, :, 0:n], op=ALU.add), [b_tA, b_tB], [b_Eall])
            frb = sm[:, FRE, :].unsqueeze(2).to_broadcast([128, 8, 128])
            fib = sm[:, FIM, :].unsqueeze(2).to_broadcast([128, 8, 128])
            for (dst, A_, B_, sgn) in ((MbR, Bre, Bim, ALU.subtract), (MbI, Bim, Bre, ALU.add)):
                dve(lambda v, B_=B_: v.tensor_tensor(out=MtA[:], in0=B_[:], in1=fib, op=ALU.mult), [b_B, b_sm], [b_Mt])
                dve(lambda v, A_=A_: v.tensor_tensor(out=MtB[:], in0=A_[:], in1=frb, op=ALU.mult), [b_B, b_sm], [b_Mt])
                dve(lambda v, dst=dst, sgn=sgn: v.tensor_tensor(out=dst[:], in0=MtB[:], in1=MtA[:], op=sgn), [b_Mt], [b_Mb])
            act(lambda a: a.copy(out=BbR[:], in_=MbR[:]), [b_Mb], [b_Bb])
            act(lambda a: a.copy(out=BbI[:], in_=MbI[:]), [b_Mb], [b_Bb])

            def cmul(eng, q_, Xr, Xi, outR, outI, negI, bx, bo, tmpa, tmpb, b_tmp):
                prb = pwr[:, q_, :].unsqueeze(2).to_broadcast([128, 8, 128])
                pib = pwi[:, q_, :].unsqueeze(2).to_broadcast([128, 8, 128])
                op = (lambda fn, r, w: S.op(eng, fn, reads=r, writes=w))
                op(lambda v: v.tensor_tensor(out=tmpa[:], in0=Xr[:], in1=prb, op=ALU.mult), [bx, b_pw], [b_tmp])
                op(lambda v: v.tensor_tensor(out=tmpb[:], in0=Xi[:], in1=pib, op=ALU.mult), [bx, b_pw], [b_tmp])
                op(lambda v: v.tensor_tensor(out=outR, in0=tmpa[:], in1=tmpb[:], op=ALU.subtract), [b_tmp], [bo])
                op(lambda v: v.tensor_tensor(out=tmpa[:], in0=Xi[:], in1=prb, op=ALU.mult), [bx, b_pw], [b_tmp])
                op(lambda v: v.tensor_tensor(out=tmpb[:], in0=Xr[:], in1=pib, op=ALU.mult), [bx, b_pw], [b_tmp])
                if negI:
                    op(lambda v: v.scalar_tensor_tensor(out=outI, in0=tmpa[:], scalar=-1.0, in1=tmpb[:], op0=ALU.mult, op1=ALU.subtract),
                       [b_tmp], [bo])
                else:
                    op(lambda v: v.tensor_tensor(out=outI, in0=tmpa[:], in1=tmpb[:], op=ALU.add), [b_tmp], [bo])

            for q_ in range(LB + 1):
                if q_ % 3 == 2:
                    cmul("pool", q_, Cre, Cim, CaR[:, q_, :, :], nCaI[:, q_, :, :], False, b_Cf, b_Ca, PtA, PtB, b_Pt)
                    pl(lambda v, q_=q_: v.tensor_scalar(out=nCaI[:, q_, :, :], in0=nCaI[:, q_, :, :], scalar1=-1.0, scalar2=0.0, op0=ALU.mult, op1=ALU.add),
                       [b_Ca], [b_Ca])
                else:
                    cmul("dve", q_, Cre, Cim, CaR[:, q_, :, :], nCaI[:, q_, :, :], True, b_Cf, b_Ca, MtA, MtB, b_Mt)
            for j in range(LB):
                q_ = LB - 1 - j
                vi = j % 2
                for ri in range(2):
                    pass
                cmul("dve", q_, MbR, MbI, vst[0][:], vst[1][:], False, b_Mb, b_vst[0], MtA, MtB, b_Mt)
                b_vst[1].last_w = b_vst[0].last_w
                for ri in range(2):
                    for st in range(8):
                        pe(lambda p, st=st, ri=ri: p.transpose(psT[:, st, :], vst[ri][:, st, :], C.identb[:]), [b_vst[0], C.b_const], [b_psT])
                    act(lambda a, j=j, ri=ri: a.copy(out=W1T[:, j, ri, :, :], in_=psT[:]), [b_psT], [b_W1T])
            for ct in range(2):
                for d0 in (0, 4):
                    for dd in range(4):
                        d = d0 + dd
                        for jj in range(4):
                            st = ct * 4 + jj
                            pe(lambda p, st=st, d=d, dd=dd, jj=jj: p.matmul(psK[:, dd, :], lhsT=BbR[:, st, :], rhs=CaR[:, d, st, :],
                                                                         start=(jj == 0), stop=False), [b_Bb, b_Ca], [b_psK])
                            pe(lambda p, st=st, d=d, dd=dd, jj=jj: p.matmul(psK[:, dd, :], lhsT=BbI[:, st, :], rhs=nCaI[:, d, st, :],
                                                                         start=False, stop=(jj == 3)), [b_Bb, b_Ca], [b_psK])
                    act(lambda a, ct=ct, d0=d0: a.copy(out=Kt[:, ct, d0:d0 + 4, :], in_=psK[:]), [b_psK], [b_Kt])

            cc = C.s5c[l]
            for nm_, t_, bb_ in (("W1T", W1T[:].rearrange("p a b c d -> p (a b c d)"), b_W1T), ("CaR", CaR[:].rearrange("p a b c -> p (a b c)"), b_Ca),
                                 ("nCaI", nCaI[:].rearrange("p a b c -> p (a b c)"), b_Ca), ("Kt", Kt[:].rearrange("p a b c -> p (a b c)"), b_Kt),
                                 ("Ec", Ec[:].rearrange("p a b -> p (a b)"), b_Eall), ("Es", Es[:].rearrange("p a b -> p (a b)"), b_Eall)):
                S.op("sp", lambda q, nm_=nm_, t_=t_: q.dma_start(out=cc[nm_][:, :], in_=t_), reads=[bb_], writes=[C.b_s5c[l][nm_]], dma=True)
        else:
            cc = C.s5c[l]
            for nm_, t_, bb_ in (("W1T", W1T[:].rearrange("p a b c d -> p (a b c d)"), b_W1T), ("CaR", CaR[:].rearrange("p a b c -> p (a b c)"), b_Ca),
                                 ("nCaI", nCaI[:].rearrange("p a b c -> p (a b c)"), b_Ca), ("Kt", Kt[:].rearrange("p a b c -> p (a b c)"), b_Kt),
                                 ("Ec", Ec[:].rearrange("p a b -> p (a b)"), b_Eall), ("Es", Es[:].rearrange("p a b -> p (a b)"), b_Eall)):
                S.op("sp", lambda q, nm_=nm_, t_=t_: q.dma_start(out=t_, in_=cc[nm_][:, :]), reads=[C.b_s5c[l][nm_]], writes=[bb_], dma=True)
        dve(lambda v: v.memset(Hs[:, :, :, 0:1], 0.0), [], b_Hs)
        for st in range(8):
            ct = st // 4
            for ri in range(2):
                for j in range(LB):
                    pe(lambda p, st=st, ri=ri, j=j, ct=ct: p.matmul(psSb[:, ri, :], lhsT=W1T[:, j, ri, st, :], rhs=uS[:, ct, j, :],
                                                                 start=(j == 0), stop=(j == LB - 1)), [b_W1T, b_uS[ct]], [b_psS])
            Xr, Xi = psSb[:, 0, :], psSb[:, 1, :]
            dve(lambda v, st=st, Xr=Xr: v.tensor_tensor(out=t1[:], in0=Xr, in1=Ec[:, st, :], op=ALU.mult), [b_psS, b_Eall], [b_t1])
            dve(lambda v, st=st, Xi=Xi: v.tensor_tensor(out=t2[:], in0=Xi, in1=Es[:, st, :], op=ALU.mult), [b_psS, b_Eall], [b_t2])
            dve(lambda v: v.tensor_tensor(out=xr[:], in0=t1[:], in1=t2[:], op=ALU.add), [b_t1, b_t2], [b_xr])
            dve(lambda v, st=st, Xi=Xi: v.tensor_tensor(out=t1[:], in0=Xi, in1=Ec[:, st, :], op=ALU.mult), [b_psS, b_Eall], [b_t1])
            dve(lambda v, st=st, Xr=Xr: v.tensor_tensor(out=t2[:], in0=Xr, in1=Es[:, st, :], op=ALU.mult), [b_psS, b_Eall], [b_t2])
            dve(lambda v: v.tensor_tensor(out=xi[:], in0=t1[:], in1=t2[:], op=ALU.subtract), [b_t1, b_t2], [b_xi])
            dve(lambda v, st=st: v.tensor_tensor_scan(out=gr[:], data0=C.bcast(r8[:, st:st + 1], NBLK), data1=xr[:], initial=0.0,
                                                      op0=ALU.mult, op1=ALU.add), [b_r8, b_xr], [b_gr])
            dve(lambda v, st=st: v.tensor_tensor_scan(out=gi[:], data0=C.bcast(r8[:, st:st + 1], NBLK), data1=xi[:], initial=0.0,
                                                      op0=ALU.mult, op1=ALU.add), [b_r8, b_xi], [b_gi])
            pl(lambda v, st=st: v.tensor_tensor(out=p1[:], in0=gr[:], in1=Ec[:, st, :], op=ALU.mult), [b_gr, b_Eall], [b_p1])
            pl(lambda v, st=st: v.tensor_tensor(out=p2[:], in0=gi[:], in1=Es[:, st, :], op=ALU.mult), [b_gi, b_Eall], [b_p2])
            pl(lambda v, st=st: v.tensor_tensor(out=Hs[:, st, 0, 1:NBLK], in0=p1[:, 0:NBLK - 1], in1=p2[:, 0:NBLK - 1], op=ALU.subtract),
               [b_p1, b_p2], [b_Hs[st]])
            pl(lambda v, st=st: v.tensor_tensor(out=p1[:], in0=gi[:], in1=Ec[:, st, :], op=ALU.mult), [b_gi, b_Eall], [b_p1])
            pl(lambda v, st=st: v.tensor_tensor(out=p2[:], in0=gr[:], in1=Es[:, st, :], op=ALU.mult), [b_gr, b_Eall], [b_p2])
            pl(lambda v, st=st: v.tensor_tensor(out=Hs[:, st, 1, 1:NBLK], in0=p1[:, 0:NBLK - 1], in1=p2[:, 0:NBLK - 1], op=ALU.add),
               [b_p1, b_p2], [b_Hs[st]])
        yi = 0
        for qr in range(4):
            ks = slice(qr * 64, (qr + 1) * 64)
            ts = slice(qr * 512, (qr + 1) * 512)
            for ct in range(2):
                py = yi % 2; yi += 1
                first = True
                for tau in range(LB):
                    outap = psY[py][:, tau * 64:(tau + 1) * 64]
                    for jj in range(4):
                        st = ct * 4 + jj
                        pe(lambda p, outap=outap, tau=tau, st=st, ks=ks, fm=first: p.matmul(
                            outap, lhsT=CaR[:, tau + 1, st, :], rhs=Hs[:, st, 0, ks], start=fm, stop=False, skip_group_check=True),
                            [b_Ca, b_Hs[st]], [b_psY[py]])
                        first = False
                        pe(lambda p, outap=outap, tau=tau, st=st, ks=ks: p.matmul(
                            outap, lhsT=nCaI[:, tau + 1, st, :], rhs=Hs[:, st, 1, ks], start=False, stop=False, skip_group_check=True),
                            [b_Ca, b_Hs[st]], [b_psY[py]])
                    for j in range(tau + 1):
                        pe(lambda p, outap=outap, tau=tau, j=j, ct=ct, ks=ks: p.matmul(
                            outap, lhsT=Kt[:, ct, tau - j, :], rhs=uS[:, ct, j, ks], start=False, stop=(j == tau), skip_group_check=True),
                            [b_Kt, b_uS[ct]], [b_psY[py]])
                dve(lambda v, ct=ct, py=py, ts=ts: v.scalar_tensor_tensor(
                    out=yv[:].rearrange("p (k t) -> p k t", t=LB), in0=u32[:, ct, ts].rearrange("p (k t) -> p k t", t=LB), scalar=dsk[:, ct:ct + 1],
                    in1=psY[py][:].rearrange("p (t k) -> p k t", t=LB), op0=ALU.mult, op1=ALU.add), [b_u[ct], b_w, b_psY[py]], [b_yv])
                gelu_tanh(S, yv[:], b_yv, yt[:], b_yt, yg[:, ct, :], b_yg[ct])
            for j in range(2):
                pe(lambda p, j=j: p.matmul(psG[0][:], lhsT=wglb[:, 0, j * 128:(j + 1) * 128], rhs=yg[:, 0, :], start=True, stop=False), [b_w] + b_yg, [b_psG[0]])
                pe(lambda p, j=j: p.matmul(psG[0][:], lhsT=wglb[:, 1, j * 128:(j + 1) * 128], rhs=yg[:, 1, :], start=False, stop=True), [b_w] + b_yg, [b_psG[0]])
                pe(lambda p, j=j: p.matmul(psG[1][:], lhsT=wglb[:, 0, 256 + j * 128:256 + (j + 1) * 128], rhs=yg[:, 0, :], start=True, stop=False), [b_w] + b_yg, [b_psG[1]])
                pe(lambda p, j=j: p.matmul(psG[1][:], lhsT=wglb[:, 1, 256 + j * 128:256 + (j + 1) * 128], rhs=yg[:, 1, :], start=False, stop=True), [b_w] + b_yg, [b_psG[1]])
                act(lambda a: a.activation(out=sgl[:], in_=psG[1][:], func=AF.Sigmoid), [b_psG[1]], [b_sgl])
                dve(lambda v, j=j: v.tensor_tensor(out=og[j][:], in0=psG[0][:], in1=sgl[:], op=ALU.mult), [b_psG[0], b_sgl], [b_og[j]])
                S.op("sp", lambda q, j=j, ts=ts: q.dma_start(out=C.mixT[j * 128:(j + 1) * 128, ts], in_=og[j][:]),
                     reads=[b_og[j]], writes=[C.b_mixT[j]], dma=True)


NBT = 2688
OFF_BS, OFF_BW, OFF_BC = 0, 256, 640


def phase_btab(C):
    nc, S = C.nc, C.S
    with ExitStack() as es:
        raw = [es.enter_context(nc.sbuf_tensor("BT_raw%d" % i, [128, NBT], F32)) for i in range(2)]
        cv = es.enter_context(nc.sbuf_tensor("BT_cv", [128, 8], F32))
        ob = [es.enter_context(nc.sbuf_tensor("BT_ob%d" % i, [128, NBT], BF16)) for i in range(2)]
        b_raw = [Buf() for _ in range(2)]; b_ob = [Buf() for _ in range(2)]; b_cv = Buf()
        S.op("sp", lambda q: q.dma_start(out=cv[:], in_=C.cvec[:, :]), writes=[b_cv], dma=True)
        for hq in range(8):
            i = hq % 2
            S.op("sp", lambda q, hq=hq, i=i: q.dma_start(out=raw[i][:], in_=C.btab_raw[:, hq, :]), writes=[b_raw[i]], dma=True)
            S.op("dve", lambda v, hq=hq, i=i: v.tensor_scalar(out=ob[i][:], in0=raw[i][:], scalar1=cv[:, hq:hq + 1], scalar2=None, op0=ALU.subtract),
                 reads=[b_raw[i], b_cv], writes=[b_ob[i]])
            S.op("sp", lambda q, hq=hq, i=i: q.dma_start(out=C.btab[:, hq, :], in_=ob[i][:]), reads=[b_ob[i]], writes=[C.b_btab], dma=True)


def phase_NSA(C, s, l):
    nc, S = C.nc, C.S
    with ExitStack() as es:
        def T(name, shape, dt=F32):
            return es.enter_context(nc.sbuf_tensor("N_" + name, shape, dt))
        qT = T("qT", [128, 4, SEQ], BF16); b_q = [Buf() for _ in range(4)]
        kcT = T("kcT", [128, SEQ + 32], BF16); vcT = T("vcT", [128, SEQ + 32], BF16); b_kc = Buf(); b_vc = Buf()
        ksT = [T("ksT%d" % i, [128, SEQ], BF16) for i in range(2)]; kwT = [T("kwT%d" % i, [128, SEQ], BF16) for i in range(2)]; b_ks = Buf(); b_kw = Buf()
        VS = T("VS", [128, NT, 2, 65], BF16); VW = T("VW", [128, NT, 2, 65], BF16); b_VS = Buf(); b_VW = Buf()
        btab = T("btab", [128, 8, NBT], BF16); b_bt = Buf()
        zg = T("zg", [128, NT, 24]); cmask = T("cmask", [128, NT, 32]); b_zg = Buf(); b_cm = Buf()
        expb = T("expb", [128, NT, 128], BF16); vcc = T("vcc", [128, 33], BF16); b_cst = Buf()
        acc = T("acc", [128, NT, 8, 64]); b_acc = [[Buf() for _ in range(8)] for _ in range(4)]
        imp = T("imp", [128, NT, 2, 32]); b_imp = [Buf() for _ in range(4)]
        negT = T("negT", [128, 2, SEQ], BF16); b_neg = [[Buf() for _ in range(2)] for _ in range(2)]
        PT = [T("PT%d" % i, [128, 512], BF16) for i in range(4)]; b_PT = [Buf() for _ in range(4)]
        w1f = T("w1f", [128, 32, 64]); w1b = [[T("w1b%d_%d" % (i, j), [128, 32, 64], BF16) for j in range(2)] for i in range(2)]; b_w1f = Buf(); b_w1 = Buf()
        pef = T("pef", [128, 2, 32]); peb = T("peb", [128, 2, 32], BF16)
        w2pf = T("w2pf", [64, 2, 128]); w2pb = T("w2pb", [64, 2, 128], BF16); w2vf = T("w2vf", [64, 64]); w2vb = T("w2vb", [64, 64], BF16)
        cst = T("cst", [64, 2]); xm = T("xm", [64, 256]); xt = T("xt", [64, 256]); hmid = [T("hmid%d" % i, [64, 256], BF16) for i in range(2)]
        b_cstv = Buf(); b_xm = Buf(); b_xt = Buf(); b_hm = [Buf() for _ in range(2)]
        kc2 = T("kc2", [128, 2, 128], BF16); VC = T("VC", [128, 2, 97], BF16); b_kc2 = Buf(); b_VC = Buf()
        rd = T("rd", [128, 4]); wg = T("wg", [128, 4]); tmpo = T("tmpo", [128, 4, 64]); tmpi = T("tmpi", [128, 4, 32])
        b_rd = Buf(); b_wg = Buf(); b_tmpo = Buf(); b_tmpi = Buf()
        sc = T("sc", [128, 32]); top8 = T("top8", [128, 8]); nsb = T("nsb", [128, 32], BF16); b_sc = Buf(); b_top = Buf(); b_nsb = Buf()
        accb = T("accb", [128, 512], BF16); b_accb = Buf()
        ost = T("ost", [128, 4, SEQ], BF16); b_ost = [Buf() for _ in range(NT)]
        psS = [es.enter_context(nc.psum_tensor("N_psS%d" % i, [128, 512], F32)) for i in range(4)]; b_psS = [Buf() for _ in range(4)]
        psO = [es.enter_context(nc.psum_tensor("N_psO%d" % i, [128, 512], F32)) for i in range(2)]; b_psO = [Buf() for _ in range(2)]
        psM = es.enter_context(nc.psum_tensor("N_psM", [128, 512], F32)); b_psM = Buf()
        psT = es.enter_context(nc.psum_tensor("N_psT", [128, 1024], BF16)); b_psT = Buf()

        def dve(fn, reads, writes):
            S.op("dve", fn, reads=reads, writes=writes)

        def act(fn, reads, writes):
            S.op("act", fn, reads=reads, writes=writes)

        def pe(fn, reads, writes):
            S.op("pe", fn, reads=reads, writes=writes)

        def dma(fn, reads, writes):
            S.op("sp", fn, reads=reads, writes=writes, dma=True)

        S.op("pool", lambda v: v.memset(negT[:, :, :], 0.0), reads=[], writes=[b_neg[hh_][kk_] for hh_ in range(2) for kk_ in range(2)])
        zq = C.zqk.rearrange("(c p) t -> p c t", p=128)
        dve(lambda v: v.memset(kcT[:, SEQ:SEQ + 32], 0.0), [], [b_kc])
        dve(lambda v: v.memset(vcT[:, SEQ:SEQ + 32], 0.0), [], [b_vc])
        dma(lambda q: q.dma_start(out=kcT[:, 0:SEQ], in_=zq[:, 4, :]), [C.b_zqk[4]], [b_kc])
        dma(lambda q: q.dma_start(out=vcT[:, 0:SEQ], in_=zq[:, 5, :]), [C.b_zqk[5]], [b_vc])
        for wi in range(2):
            for hh in range(2):
                dma(lambda q, wi=wi, hh=hh: q.dma_start(out=w1f[:], in_=C.nsa_w1[l, wi, hh]), [], [b_w1f])
                act(lambda a, wi=wi, hh=hh: a.copy(out=w1b[wi][hh][:], in_=w1f[:]), [b_w1f], [b_w1])
        dma(lambda q: q.dma_start(out=pef[:], in_=C.nsa_peT[l]), [], [b_w1f])
        dma(lambda q: q.dma_start(out=w2pf[:], in_=C.nsa_w2pad[l]), [], [b_w1f])
        dma(lambda q: q.dma_start(out=w2vf[:], in_=C.nsa_w2v[l]), [], [b_w1f])
        dve(lambda v: v.tensor_copy(out=peb[:], in_=pef[:]), [b_w1f], [b_w1])
        dve(lambda v: v.tensor_copy(out=w2pb[:], in_=w2pf[:]), [b_w1f], [b_w1])
        dve(lambda v: v.tensor_copy(out=w2vb[:], in_=w2vf[:]), [b_w1f], [b_w1])
        dma(lambda q: q.dma_start(out=vcc[:], in_=C.vca_const[:, :]), [], [b_cst])
        S.op("pool", lambda v: v.memset(expb[:, :, :], 0.0), reads=[], writes=[b_cst])
        dma(lambda q: q.dma_start(out=expb[0:32, :, :], in_=C.expand[:, :, :]), [], [b_cst])
        dma(lambda q: q.dma_start(out=cmask[:], in_=C.cmask[:, :, :]), [], [b_cm])
        dma(lambda q: q.dma_start(out=zg[:].rearrange("p t g -> p (t g)"), in_=C.zg[:, :]), [C.b_zg], [b_zg])
        for j in range(4):
            dma(lambda q, j=j: q.dma_start(out=qT[:, j, :], in_=zq[:, j, :]), [C.b_zqk[j]], [b_q[j]])
        for hh in range(2):
            lo, hi = 64 * hh, 64 * hh + 64
            olo, ohi = 64 * (1 - hh), 64 * (1 - hh) + 64
            S.op("pool", lambda v, hh=hh, olo=olo, ohi=ohi: v.memset(ksT[hh][olo:ohi, :], 0.0), reads=[], writes=[b_ks])
            S.op("pool", lambda v, hh=hh, olo=olo, ohi=ohi: v.memset(kwT[hh][olo:ohi, :], 0.0), reads=[], writes=[b_kw])
            dma(lambda q, hh=hh, lo=lo, hi=hi: q.dma_start(out=ksT[hh][lo:hi, :], in_=zq[lo:hi, 6, :]), [C.b_zqk[6]], [b_ks])
            dma(lambda q, hh=hh, lo=lo, hi=hi: q.dma_start(out=kwT[hh][lo:hi, :], in_=zq[lo:hi, 7, :]), [C.b_zqk[7]], [b_kw])
        zv = C.zv.rearrange("(t p) a h d -> p t a h d", p=128)
        for hh in range(2):
            dma(lambda q, hh=hh: q.dma_start(out=VS[:, :, hh, 0:64], in_=zv[:, :, 0, hh, :]), [C.b_zv], [b_VS])
            dma(lambda q, hh=hh: q.dma_start(out=VW[:, :, hh, 0:64], in_=zv[:, :, 1, hh, :]), [C.b_zv], [b_VW])
        dve(lambda v: v.memset(VS[:, :, :, 64:65], 1.0), [], [b_VS])
        dve(lambda v: v.memset(VW[:, :, :, 64:65], 1.0), [], [b_VW])
        for hq in range(8):
            dma(lambda q, hq=hq: q.dma_start(out=btab[:, hq, :], in_=C.btab[:, hq, :]), [C.b_btab], [b_bt])

        if C.dbg.get("nsa_stop", 99) <= 1:
            return
        kcS = T("kcS", [128, 32, 128], BF16); b_kcS = Buf()
        for wi, src, b_src in ((0, kcT, b_kc), (1, vcT, b_vc)):
            S.op("pool", lambda v, src=src: v.tensor_copy(out=kcS[:], in_=src[:, 0:SEQ].rearrange("p (n l) -> p l n", l=16)[:, :, :].unsqueeze(1)) if False else
                 v.tensor_copy(out=kcS[:, 0:16, :], in_=src[:, 0:SEQ].rearrange("p (n l) -> p l n", l=16)), reads=[b_src], writes=[b_kcS])
            S.op("pool", lambda v, src=src: v.tensor_copy(out=kcS[:, 16:32, :], in_=src[:, 16:SEQ + 16].rearrange("p (n l) -> p l n", l=16)),
                 reads=[b_src], writes=[b_kcS])
            for ll in range(32):
                pe(lambda p, wi=wi, ll=ll: p.matmul(psM[0:64, 256:257], lhsT=w1b[wi][0][:, ll, :], rhs=peb[:, wi, ll:ll + 1],
                                                   start=(ll == 0), stop=(ll == 31)), [b_w1], [b_psM])
            act(lambda a, wi=wi: a.copy(out=cst[:, wi:wi + 1], in_=psM[0:64, 256:257]), [b_psM], [b_cstv])
            if C.dbg.get("nsa_stop", 99) <= 1.2:
                return
            for hh in range(2):
                for ll in range(32):
                    pe(lambda p, wi=wi, hh=hh, ll=ll, src=src: p.matmul(
                        psM[0:64, hh * 128:(hh + 1) * 128], lhsT=w1b[wi][hh][:, ll, :],
                        rhs=kcS[:, ll, :], start=(ll == 0), stop=(ll == 31)),
                        [b_w1, b_kcS], [b_psM])
            if C.dbg.get("nsa_stop", 99) <= 1.5:
                return
            act(lambda a, wi=wi: a.activation(out=xm[:], in_=psM[0:64, 0:256], func=AF.Identity, bias=cst[:, wi:wi + 1]),
                [b_psM, b_cstv], [b_xm])
            if C.dbg.get("nsa_stop", 99) <= 1.6:
                return
            gelu_tanh(S, xm[:], b_xm, xt[:], b_xt, hmid[wi][:], b_hm[wi])
            if C.dbg.get("nsa_stop", 99) <= 1.7:
                return
            if wi == 0:
                pe(lambda p: p.matmul(psM[:, 384:512], lhsT=w2pb[:, 0, :], rhs=hmid[0][:, 0:128], start=True, stop=False), [b_w1, b_hm[0]], [b_psM])
                pe(lambda p: p.matmul(psM[:, 384:512], lhsT=w2pb[:, 1, :], rhs=hmid[0][:, 128:256], start=False, stop=True), [b_w1, b_hm[0]], [b_psM])
                dve(lambda v: v.memset(kc2[:], 0.0), [], [b_kc2])
                act(lambda a: a.copy(out=kc2[0:64, 0, :], in_=psM[0:64, 384:512]), [b_psM], [b_kc2])
                act(lambda a: a.copy(out=kc2[64:128, 1, :], in_=psM[64:128, 384:512]), [b_psM], [b_kc2])
            else:
                for hh in range(2):
                    pe(lambda p, hh=hh: p.matmul(psM[:, 384 + hh * 64:384 + (hh + 1) * 64], lhsT=hmid[1][:, hh * 128:(hh + 1) * 128], rhs=w2vb[:],
                                                start=True, stop=True), [b_w1, b_hm[1]], [b_psM])
                act(lambda a: a.copy(out=VC[:, :, 0:64], in_=psM[:, 384:512].rearrange("p (h d) -> p h d", h=2)), [b_psM], [b_VC])
        for hh in range(2):
            dve(lambda v, hh=hh: v.tensor_copy(out=VC[:, hh, 64:97], in_=vcc[:]), [b_cst], [b_VC])

        if C.dbg.get("nsa_stop", 99) <= 2:
            return

        C.dump("VC", VC[:].rearrange("p h c -> p (h c)"), [128, 194], BF16, [b_VC])
        si = [0]
        oi = [0]
        brs = C.dbg.get("nsa_branches", (0, 1, 2))
        if 0 not in brs:
            dve(lambda v: v.memset(acc[:], 0.0), [], [b_acc[nb_][hq_] for nb_ in range(4) for hq_ in range(8)])

        def evac(po, W, h, g, nb, branch, first):
            hq = 4 * h + g
            tts = slice(4 * nb, 4 * nb + 4)
            ps3 = psO[po][:, 0:4 * W].rearrange("p (t c) -> p t c", c=W)
            col = h * 12 + g * 3 + branch
            dve(lambda v: v.tensor_scalar(out=rd[:], in0=ps3[:, :, 64], scalar1=1e-30, scalar2=None, op0=ALU.max), [b_psO[po]], [b_rd])
            dve(lambda v: v.reciprocal(out=rd[:], in_=rd[:]), [b_rd], [b_rd])
            dve(lambda v: v.tensor_tensor(out=wg[:], in0=rd[:], in1=zg[:, tts, col], op=ALU.mult), [b_rd, b_zg], [b_wg])
            wgb = wg[:].unsqueeze(2).to_broadcast([128, 4, 64])
            if branch not in brs:
                pass
            elif first:
                dve(lambda v: v.tensor_tensor(out=acc[:, tts, hq, :], in0=ps3[:, :, 0:64], in1=wgb, op=ALU.mult),
                    [b_psO[po], b_wg], [b_acc[nb][hq]])
            else:
                dve(lambda v: v.tensor_tensor(out=tmpo[:], in0=ps3[:, :, 0:64], in1=wgb, op=ALU.mult), [b_psO[po], b_wg], [b_tmpo])
                dve(lambda v: v.tensor_tensor(out=acc[:, tts, hq, :], in0=acc[:, tts, hq, :], in1=tmpo[:], op=ALU.add),
                    [b_tmpo, b_acc[nb][hq]], [b_acc[nb][hq]])
            if branch == 0:
                rdb = rd[:].unsqueeze(2).to_broadcast([128, 4, 32])
                if g == 0:
                    dve(lambda v: v.tensor_tensor(out=imp[:, tts, h, :], in0=ps3[:, :, 65:97], in1=rdb, op=ALU.mult), [b_psO[po], b_rd], [b_imp[nb]])
                else:
                    dve(lambda v: v.tensor_tensor(out=tmpi[:], in0=ps3[:, :, 65:97], in1=rdb, op=ALU.mult), [b_psO[po], b_rd], [b_tmpi])
                    dve(lambda v: v.tensor_tensor(out=imp[:, tts, h, :], in0=imp[:, tts, h, :], in1=tmpi[:], op=ALU.add), [b_tmpi, b_imp[nb]], [b_imp[nb]])

        NB_ = len(psS)

        def run_pipeline(items, look=2):
            n = len(items)
            for j in range(min(look, n)):
                items[j][0](j % NB_)
            for j in range(n):
                if j + look < n:
                    items[j + look][0]((j + look) % NB_)
                items[j][1](j % NB_)

        items = []
        for h in range(2):
            hs = slice(64 * h, 64 * h + 64)
            for g in range(4):
                hq = 4 * h + g
                for nb in range(4):
                    ns = slice(nb * 512, (nb + 1) * 512)

                    def s1(i, g=g, ns=ns, hs=hs, hq=hq, nb=nb, h=h):
                        pe(lambda p: p.matmul(psS[i][:], lhsT=kc2[:, h, :], rhs=qT[:, g, ns], start=True, stop=False),
                           [b_kc2, b_q[g]], [b_psS[i]])
                        pe(lambda p: p.matmul(psS[i][:], lhsT=C.identb[:], rhs=btab[:, hq, OFF_BC + nb * 512:OFF_BC + (nb + 1) * 512],
                                              start=False, stop=True), [C.b_const, b_bt], [b_psS[i]])
                        act(lambda a: a.activation(out=PT[i][:], in_=psS[i][:], func=AF.Exp), [b_psS[i]], [b_PT[i]])

                    def s2(i, h=h, g=g, nb=nb):
                        po = oi[0] % 2; oi[0] += 1
                        for ql in range(4):
                            pe(lambda p, ql=ql: p.matmul(psO[po][:, ql * 97:(ql + 1) * 97], lhsT=PT[i][:, ql * 128:(ql + 1) * 128],
                                                         rhs=VC[:, h, :], start=(ql == 0), stop=(ql == 3), skip_group_check=True),
                               [b_PT[i], b_VC], [b_psO[po]])
                        evac(po, 97, h, g, nb, 0, True)
                    items.append((s1, s2))
        run_pipeline(items)
        for h in range(2):
            for tt in range(NT):
                dve(lambda v, tt=tt, h=h: v.tensor_tensor(out=sc[:], in0=imp[:, tt, h, :], in1=cmask[:, tt, :], op=ALU.add), [b_imp[tt // 4], b_cm], [b_sc])
                dve(lambda v: v.max(out=top8[:], in_=sc[:]), [b_sc], [b_top])
                dve(lambda v: v.tensor_scalar(out=nsb[:], in0=sc[:], scalar1=top8[:, 7:8], scalar2=NEG, op0=ALU.is_lt, op1=ALU.mult),
                    [b_sc, b_top], [b_nsb])
                pe(lambda p, tt=tt: p.transpose(psT[0:32, (tt % 8) * 128:(tt % 8 + 1) * 128], nsb[:], C.identb[:]), [b_nsb, C.b_const], [b_psT])
                if tt % 8 == 7:
                    act(lambda a, tt=tt, h=h: a.copy(out=negT[0:32, h, (tt // 8) * 1024:(tt // 8 + 1) * 1024], in_=psT[0:32, :]),
                        [b_psT], [b_neg[h][tt // 8]])
        items = []
        for h in range(2):
            hs = slice(64 * h, 64 * h + 64)
            for g in range(4):
                hq = 4 * h + g
                for nb in range(4):
                    for branch in (1, 2):
                        if branch not in brs:
                            continue
                        if branch == 1:
                            kts = list(range(0, 4 * nb + 4)); K_, b_K, V_, b_V = ksT[h], b_ks, VS, b_VS
                        else:
                            kts = list(range(max(0, 4 * nb - 2), 4 * nb + 4)); K_, b_K, V_, b_V = kwT[h], b_kw, VW, b_VW
                        blk = {"po": None}
                        for kt in kts:
                            t_lo = max(512 * nb, 128 * kt)
                            t_hi = 512 * (nb + 1)
                            if branch == 2:
                                t_hi = min(t_hi, 128 * kt + 384)
                            N = t_hi - t_lo

                            def s1(i, kt=kt, g=g, h=h, hq=hq, hs=hs, t_lo=t_lo, t_hi=t_hi, N=N, K_=K_, b_K=b_K, branch=branch, nb=nb):
                                pe(lambda p: p.matmul(psS[i][:, 0:N], lhsT=K_[:, kt * 128:(kt + 1) * 128], rhs=qT[:, g, t_lo:t_hi],
                                                      start=True, stop=False), [b_K, b_q[g]], [b_psS[i]])
                                if branch == 1:
                                    b_lo, b_hi = max(t_lo, 128 * kt), min(t_hi, 128 * kt + 256)
                                    if b_hi > b_lo:
                                        pe(lambda p: p.matmul(psS[i][:, b_lo - t_lo:b_hi - t_lo], lhsT=C.identb[:],
                                                              rhs=btab[:, hq, OFF_BS + b_lo - 128 * kt:OFF_BS + b_hi - 128 * kt],
                                                              start=False, stop=False), [C.b_const, b_bt], [b_psS[i]])
                                    pe(lambda p: p.matmul(psS[i][:, 0:N], lhsT=expb[:, kt, :], rhs=negT[:, h, t_lo:t_hi], start=False, stop=True),
                                       [b_cst, b_neg[h][nb // 2]], [b_psS[i]])
                                else:
                                    pe(lambda p: p.matmul(psS[i][:, 0:N], lhsT=C.identb[:],
                                                          rhs=btab[:, hq, OFF_BW + t_lo - 128 * kt:OFF_BW + t_hi - 128 * kt],
                                                          start=False, stop=True), [C.b_const, b_bt], [b_psS[i]])
                                act(lambda a: a.activation(out=PT[i][:, 0:N], in_=psS[i][:, 0:N], func=AF.Exp), [b_psS[i]], [b_PT[i]])

                            def s2(i, kt=kt, g=g, h=h, t_lo=t_lo, t_hi=t_hi, V_=V_, b_V=b_V, branch=branch, nb=nb, blk=blk,
                                   first=(kt == kts[0]), last=(kt == kts[-1])):
                                if first:
                                    blk["po"] = oi[0] % 2; oi[0] += 1
                                po = blk["po"]
                                fm = first
                                for qt in range(t_lo // 128, t_hi // 128):
                                    ql = qt - 4 * nb
                                    pe(lambda p, ql=ql, qt=qt, fm=fm: p.matmul(
                                        psO[po][:, ql * 65:(ql + 1) * 65], lhsT=PT[i][:, qt * 128 - t_lo:qt * 128 - t_lo + 128],
                                        rhs=V_[:, kt, h, :], start=fm, stop=(kt == qt), skip_group_check=True),
                                        [b_PT[i], b_V], [b_psO[po]])
                                    fm = False
                                if last:
                                    evac(po, 65, h, g, nb, branch, False)
                            items.append((s1, s2))
        run_pipeline(items)
        if C.dbg.get("nsa_stop", 99) <= 5:
            return
        for tt in range(NT):
            act(lambda a, tt=tt: a.copy(out=accb[:], in_=acc[:, tt, :, :].rearrange("p h d -> p (h d)")),
                [b_acc[tt // 4][hq] for hq in range(8)], [b_accb])
            for ft in range(4):
                pe(lambda p, ft=ft: p.transpose(psT[:, ft * 128:(ft + 1) * 128], accb[:, ft * 128:(ft + 1) * 128], C.identb[:]),
                   [b_accb, C.b_const], [b_psT])
            dve(lambda v, tt=tt: v.tensor_copy(out=ost[:, :, tt * 128:(tt + 1) * 128], in_=psT[:, 0:512].rearrange("p (f t) -> p f t", f=4)),
                [b_psT], [b_ost[tt]])
        for ft in range(4):
            dma(lambda q, ft=ft: q.dma_start(out=C.mixT[256 + ft * 128:256 + (ft + 1) * 128, :], in_=ost[:, ft, :]),
                b_ost, [C.b_mixT[2 + ft]])


def phase_O1(C, s, l, src):
    nc, S = C.nc, C.S
    with ExitStack() as es:
        wob = es.enter_context(nc.sbuf_tensor("O1_wob", [128, 8, D], BF16))
        b_wob = [Buf() for _ in range(8)]
        stage = [es.enter_context(nc.sbuf_tensor("O1_wst%d" % i, [128, D], F32)) for i in range(2)]
        b_stage = [Buf() for _ in range(2)]
        mixT = es.enter_context(nc.sbuf_tensor("O1_mixT", [128, 8, SEQ], BF16))
        b_mix = [Buf() for _ in range(8)]
        ps = [es.enter_context(nc.psum_tensor("O1_ps%d" % i, [128, 512], F32)) for i in range(4)]
        b_ps = [Buf() for _ in range(4)]
        ps_tr = [es.enter_context(nc.psum_tensor("O1_pstr%d" % i, [128, D], BF16)) for i in range(2)]
        b_ps_tr = [Buf() for _ in range(2)]
        NB3 = 3
        ht = [es.enter_context(nc.sbuf_tensor("O1_h%d" % i, [128, D], F32)) for i in range(NB3)]
        b_ht = [Buf() for _ in range(NB3)]
        h1 = [es.enter_context(nc.sbuf_tensor("O1_h1%d" % i, [128, D], F32)) for i in range(NB3)]
        b_h1 = [Buf() for _ in range(NB3)]
        sq = es.enter_context(nc.sbuf_tensor("O1_sq", [128, D], BF16))
        ss = [es.enter_context(nc.sbuf_tensor("O1_ss%d" % i, [128, 1], F32)) for i in range(NB3)]
        rs = [es.enter_context(nc.sbuf_tensor("O1_rs%d" % i, [128, 1], F32)) for i in range(NB3)]
        xn = [es.enter_context(nc.sbuf_tensor("O1_xn%d" % i, [128, D], BF16)) for i in range(NB3)]
        b_tmp = [Buf() for _ in range(NB3)]
        b_xn = [Buf() for _ in range(NB3)]
        hn2 = [es.enter_context(nc.sbuf_tensor("O1_hn2_%d" % i, [128, 8, 128], BF16)) for i in range(2)]
        b_hn2 = [Buf() for _ in range(2)]

        gt_ = es.enter_context(nc.sbuf_tensor("O1_gain", [128, D], F32))
        C.b_gain = Buf()
        S.op("sp", lambda q: q.dma_start(out=gt_[:], in_=C.grep_in[:, 2 + l, :]), writes=[C.b_gain], dma=True)
        w_src = C.w_out[l].rearrange("(c p) f -> p c f", p=128)
        load_cast_weight(C, S, nc, stage, b_stage, lambda k: wob[:, k, :], lambda k: w_src[:, k, :], 8,
                         None, b_wob, eng=("act", "dve"))
        mix_src = C.mixT.rearrange("(c p) t -> p c t", p=128)
        for kt in range(8):
            S.op("sp", lambda q, kt=kt: q.dma_start(out=mixT[:, kt, :], in_=mix_src[:, kt, :]),
                 reads=[C.b_mixT[kt]], writes=[b_mix[kt]], dma=True)
        mmi = 0

        def mm_part(tt):
            nonlocal mmi
            i = tt % NB3
            S.op("sp", lambda q, tt=tt, i=i: q.dma_start(out=ht[i][:], in_=src[s, tt * 128:(tt + 1) * 128, :]),
                 reads=[C.b_hres[s][tt]], writes=[b_ht[i]], dma=True)
            for half in range(2):
                pi = mmi % 4
                mmi += 1
                for kt in range(8):
                    S.op("pe", lambda pe, tt=tt, kt=kt, pi=pi, half=half: pe.matmul(
                        ps[pi][:], lhsT=mixT[:, kt, tt * 128:(tt + 1) * 128],
                        rhs=wob[:, kt, half * 512:(half + 1) * 512], start=(kt == 0), stop=(kt == 7)),
                        reads=[b_mix[kt], b_wob[kt]], writes=[b_ps[pi]])
                S.op("dve", lambda v, i=i, pi=pi, half=half: v.tensor_tensor(
                    out=h1[i][:, half * 512:(half + 1) * 512], in0=ps[pi][:], in1=ht[i][:, half * 512:(half + 1) * 512],
                    op=ALU.add), reads=[b_ps[pi], b_ht[i]], writes=[b_h1[i]])
            S.op("sp", lambda q, tt=tt, i=i: q.dma_start(out=C.hres[s, tt * 128:(tt + 1) * 128, :], in_=h1[i][:]),
                 reads=[b_h1[i]], writes=[C.b_hres[s][tt]], dma=True)
            rms_tile(C, S, gt_[:], h1[i][:], b_h1[i], sq, ss[i], rs[i], xn[i], b_tmp[i], b_xn[i])

        def tr_part(tt):
            i = tt % NB3
            j = tt % 2
            for ct in range(8):
                S.op("pe", lambda pe, ct=ct, i=i, j=j: pe.transpose(ps_tr[j][:, ct * 128:(ct + 1) * 128],
                                                                      xn[i][:, ct * 128:(ct + 1) * 128], C.identb[:]),
                     reads=[b_xn[i], C.b_const], writes=[b_ps_tr[j]])
            S.op("act", lambda a, j=j: a.copy(out=hn2[j][:], in_=ps_tr[j][:].rearrange("p (c t) -> p c t", c=8)),
                 reads=[b_ps_tr[j]], writes=[b_hn2[j]])
            S.op("sp", lambda q, tt=tt, j=j: q.dma_start(
                out=C.hn2T.rearrange("(c p) t -> p c t", p=128)[:, :, tt * 128:(tt + 1) * 128], in_=hn2[j][:]),
                reads=[b_hn2[j]], writes=[C.b_hn2T[tt]], dma=True)

        for tt in range(NT):
            mm_part(tt)
            if tt >= 1:
                tr_part(tt - 1)
        tr_part(NT - 1)


def phase_O2(C, s, l, last):
    nc, S = C.nc, C.S
    HT = 1024
    with ExitStack() as es:
        hn2T = es.enter_context(nc.sbuf_tensor("O2_hn2T", [128, 8, HT], BF16))
        b_hn = [Buf() for _ in range(8)]
        actT = es.enter_context(nc.sbuf_tensor("O2_actT", [128, NFT, HT], BF16))
        b_act = [[Buf() for _ in range(2)] for _ in range(NFT)]
        wdb = es.enter_context(nc.sbuf_tensor("O2_wdb", [128, NFT, D], BF16))
        b_wdb = [Buf() for _ in range(NFT)]
        stage_d = [es.enter_context(nc.sbuf_tensor("O2_wdst%d" % i, [128, D], F32)) for i in range(2)]
        b_stage_d = [Buf() for _ in range(2)]
        stage_g = [es.enter_context(nc.sbuf_tensor("O2_wgst%d" % i, [128, 8, 128], F32)) for i in range(2)]
        stage_u = [es.enter_context(nc.sbuf_tensor("O2_wust%d" % i, [128, 8, 128], F32)) for i in range(2)]
        b_stage_g = [Buf() for _ in range(2)]
        b_stage_u = [Buf() for _ in range(2)]
        wgb = [es.enter_context(nc.sbuf_tensor("O2_wgb%d" % i, [128, 8, 128], BF16)) for i in range(2)]
        wub = [es.enter_context(nc.sbuf_tensor("O2_wub%d" % i, [128, 8, 128], BF16)) for i in range(2)]
        b_wgb = [Buf() for _ in range(2)]
        b_wub = [Buf() for _ in range(2)]
        psg = [es.enter_context(nc.psum_tensor("O2_psg%d" % i, [128, 512], F32)) for i in range(2)]
        psu = [es.enter_context(nc.psum_tensor("O2_psu%d" % i, [128, 512], F32)) for i in range(2)]
        psd = [es.enter_context(nc.psum_tensor("O2_psd%d" % i, [128, 512], F32)) for i in range(2)]
        b_psg = [Buf() for _ in range(2)]
        b_psu = [Buf() for _ in range(2)]
        b_psd = [Buf() for _ in range(2)]
        sg = [es.enter_context(nc.sbuf_tensor("O2_sg%d" % i, [128, 512], F32)) for i in range(2)]
        b_sg = [Buf() for _ in range(2)]
        h1 = [es.enter_context(nc.sbuf_tensor("O2_h1_%d" % i, [128, D], F32)) for i in range(2)]
        b_h1 = [Buf() for _ in range(2)]
        h2 = [es.enter_context(nc.sbuf_tensor("O2_h2_%d" % i, [128, D], F32)) for i in range(2)]
        b_h2 = [Buf() for _ in range(2)]
        if last:
            sq = es.enter_context(nc.sbuf_tensor("O2_sq", [128, D], BF16))
            ss = [es.enter_context(nc.sbuf_tensor("O2_ss%d" % i, [128, 1], F32)) for i in range(2)]
            rs = [es.enter_context(nc.sbuf_tensor("O2_rs%d" % i, [128, 1], F32)) for i in range(2)]
            yo = [es.enter_context(nc.sbuf_tensor("O2_yo%d" % i, [128, D], F32)) for i in range(2)]
            b_tmp = [Buf() for _ in range(2)]
            b_yo = [Buf() for _ in range(2)]

        wd_src = C.w_down[l].rearrange("(c p) f -> p c f", p=128)
        def wd_chunk(k):
            load_cast_weight(C, S, nc, stage_d, b_stage_d, lambda k: wdb[:, k, :], lambda k: wd_src[:, k, :], NFT,
                             None, b_wdb, eng=("dve", "pool"), ks=[k])
        wg_src = C.w_gate[l].rearrange("(c p) f -> p c f", p=128)
        wu_src = C.w_up[l].rearrange("(c p) f -> p c f", p=128)
        hn_src = C.hn2T.rearrange("(c p) t -> p c t", p=128)
        gi = 0
        for hf in range(2):
            t0 = hf * HT
            for ct in range(8):
                S.op("sp", lambda q, ct=ct, t0=t0: q.dma_start(out=hn2T[:, ct, :], in_=hn_src[:, ct, t0:t0 + HT]),
                     reads=C.b_hn2T[hf * 8:(hf + 1) * 8], writes=[b_hn[ct]], dma=True)
            for ft in range(NFT):
                j = gi % 2
                gi += 1
                fs = slice(ft * 128, (ft + 1) * 128)
                if not (C.dbg.get("ffn_skip2") and hf == 1):
                    S.op("sp", lambda q, j=j, fs=fs: q.dma_start(out=stage_g[j][:], in_=wg_src[:, :, fs]),
                         writes=[b_stage_g[j]], dma=True)
                    S.op("sp", lambda q, j=j, fs=fs: q.dma_start(out=stage_u[j][:], in_=wu_src[:, :, fs]),
                         writes=[b_stage_u[j]], dma=True)
                    S.op("act", lambda a, j=j: a.copy(out=wgb[j][:], in_=stage_g[j][:]), reads=[b_stage_g[j]], writes=[b_wgb[j]])
                    S.op("pool", lambda v, j=j: v.tensor_copy(out=wub[j][:], in_=stage_u[j][:]), reads=[b_stage_u[j]], writes=[b_wub[j]])
                if hf == 0:
                    wd_chunk(ft)
                for nb in range(2):
                    pi = (ft * 2 + nb) % 2
                    ns = slice(nb * 512, (nb + 1) * 512)
                    for ct in range(8):
                        S.op("pe", lambda pe, j=j, ct=ct, pi=pi, ns=ns: pe.matmul(
                            psg[pi][:], lhsT=wgb[j][:, ct, :], rhs=hn2T[:, ct, ns], start=(ct == 0), stop=(ct == 7)),
                            reads=[b_wgb[j], b_hn[ct]], writes=[b_psg[pi]])
                    for ct in range(8):
                        S.op("pe", lambda pe, j=j, ct=ct, pi=pi, ns=ns: pe.matmul(
                            psu[pi][:], lhsT=wub[j][:, ct, :], rhs=hn2T[:, ct, ns], start=(ct == 0), stop=(ct == 7)),
                            reads=[b_wub[j], b_hn[ct]], writes=[b_psu[pi]])
                    S.op("act", lambda a, pi=pi: a.activation(out=sg[pi][:], in_=psg[pi][:], func=AF.Silu),
                         reads=[b_psg[pi]], writes=[b_sg[pi]])
                    S.op("dve", lambda v, pi=pi, ft=ft, ns=ns: v.tensor_tensor(
                        out=actT[:, ft, ns], in0=psu[pi][:], in1=sg[pi][:], op=ALU.mult),
                        reads=[b_psu[pi], b_sg[pi]], writes=[b_act[ft][nb]])
            for tl in range(8):
                tt = hf * 8 + tl
                i = tt % 2
                S.op("sp", lambda q, tt=tt, i=i: q.dma_start(out=h1[i][:], in_=C.hres[s, tt * 128:(tt + 1) * 128, :]),
                     reads=[C.b_hres[s][tt]], writes=[b_h1[i]], dma=True)
                for half in range(2):
                    pi = (tt * 2 + half) % 2
                    for ft in range(NFT):
                        S.op("pe", lambda pe, ft=ft, tl=tl, pi=pi, half=half: pe.matmul(
                            psd[pi][:], lhsT=actT[:, ft, tl * 128:(tl + 1) * 128],
                            rhs=wdb[:, ft, half * 512:(half + 1) * 512], start=(ft == 0), stop=(ft == NFT - 1)),
                            reads=[b_act[ft][tl // 4], b_wdb[ft]], writes=[b_psd[pi]])
                    S.op("dve", lambda v, i=i, pi=pi, half=half: v.tensor_tensor(
                        out=h2[i][:, half * 512:(half + 1) * 512], in0=psd[pi][:],
                        in1=h1[i][:, half * 512:(half + 1) * 512], op=ALU.add),
                        reads=[b_psd[pi], b_h1[i]], writes=[b_h2[i]])
                if not last:
                    S.op("sp", lambda q, tt=tt, i=i: q.dma_start(out=C.hres[s, tt * 128:(tt + 1) * 128, :], in_=h2[i][:]),
                         reads=[b_h2[i]], writes=[C.b_hres[s][tt]], dma=True)
                else:
                    S.op("act", lambda a, i=i: a.activation(out=sq[:], in_=h2[i][:], func=AF.Square, accum_out=ss[i][:]),
                         reads=[b_h2[i]], writes=[b_tmp[i]])
                    S.op("act", lambda a, i=i: a.activation(out=rs[i][:], in_=ss[i][:], func=AF.Sqrt, scale=1.0 / D, bias=C.epsc[:]),
                         reads=[b_tmp[i], C.b_const], writes=[b_tmp[i]])
                    S.op("dve", lambda v, i=i: v.reciprocal(out=rs[i][:], in_=rs[i][:]), reads=[b_tmp[i]], writes=[b_tmp[i]])
                    S.op("dve", lambda v, i=i: v.scalar_tensor_tensor(
                        out=yo[i][:], in0=h2[i][:], scalar=rs[i][:], in1=C.gfin_sb[:], op0=ALU.mult, op1=ALU.mult),
                        reads=[b_h2[i], b_tmp[i], C.b_const], writes=[b_yo[i]])
                    S.op("sp", lambda q, tt=tt, i=i: q.dma_start(out=C.out[s, tt * 128:(tt + 1) * 128, :], in_=yo[i][:]),
                         reads=[b_yo[i]], writes=[C.b_out], dma=True, is_out=True)


def host_layout(inputs):
    f = lambda k: np.ascontiguousarray(np.asarray(inputs[k], dtype=np.float32))
    perm = in_perm()
    com = {}
    com["w_in"] = np.ascontiguousarray(f("w_in")[:, :, perm])
    com["w_out"] = f("w_out")
    com["w_gate"] = f("w_gate")
    com["w_up"] = f("w_up")
    com["w_down"] = f("w_down")
    vecs = [f("norm_mix")[0], f("norm_mix")[1], f("norm_ffn")[0], f("norm_ffn")[1], f("norm_final")]
    com["gains"] = np.ascontiguousarray(np.stack([v.reshape(8, 128).T for v in vecs], axis=1))
    com["gfin"] = np.ascontiguousarray(np.broadcast_to(f("norm_final")[None, :], (128, D)))
    com["grep"] = np.ascontiguousarray(np.broadcast_to(np.stack(vecs[:4], axis=0)[None, :, :], (128, 4, D)))
    com["ident"] = np.eye(128, dtype=np.float32)
    L = DEPTH
    com.update(nsa_host(inputs))
    s5p = np.zeros((L, 128, 8, 3), np.float32)
    pads = {k: np.zeros((L, 128, 8, 128), np.float32) for k in ("s5_bre", "s5_bim", "s5_cre", "s5_cim")}
    lre, lim, ldt = f("s5_lam_re"), f("s5_lam_im"), f("s5_log_dt")
    bre, bim, cre, cim = f("s5_b_re"), f("s5_b_im"), f("s5_c_re"), f("s5_c_im")
    for l in range(L):
        for g in range(16):
            st, po, co = g // 2, (g % 2) * 64, (g % 8) * 16
            s5p[l, po:po + 64, st, 0] = lre[l, g]
            s5p[l, po:po + 64, st, 1] = lim[l, g]
            s5p[l, po:po + 64, st, 2] = ldt[l, g]
            pads["s5_bre"][l, po:po + 64, st, co:co + 16] = bre[l, g]
            pads["s5_bim"][l, po:po + 64, st, co:co + 16] = bim[l, g]
            pads["s5_cre"][l, po:po + 64, st, co:co + 16] = cre[l, g].T
            pads["s5_cim"][l, po:po + 64, st, co:co + 16] = cim[l, g].T
    com["s5p"] = s5p
    com.update(pads)
    com["s5_dsk"] = np.ascontiguousarray(f("s5_d").reshape(L, 2, 128).transpose(0, 2, 1))
    com["s5_wglu"] = f("s5_w_glu")
    lruv = np.zeros((L, 128, 2, 8), np.float32)
    cw = f("lru_conv_w")
    for l in range(L):
        for ct in range(2):
            cs = slice(ct * 128, (ct + 1) * 128)
            for i in range(4):
                lruv[l, :, ct, i] = cw[l, i, cs]
            lruv[l, :, ct, 4] = f("lru_conv_b")[l, cs]
            lruv[l, :, ct, 5] = f("lru_b_a")[l, cs]
            lruv[l, :, ct, 6] = f("lru_b_x")[l, cs]
            lruv[l, :, ct, 7] = f("lru_lam")[l, cs]
    com["lruv"] = lruv
    for nm, key in (("lru_wa", "lru_w_a"), ("lru_wx", "lru_w_x")):
        w = f(key)
        bd = np.zeros((L, 128, 2, 128), np.float32)
        for l in range(L):
            for hh in range(4):
                ct, o = hh // 2, (hh % 2) * 64
                bd[l, o:o + 64, ct, o:o + 64] = w[l, hh]
        com[nm] = bd
    return com


def t5_bucket_np(dist):
    n = np.maximum(dist, 0)
    nf = np.maximum(n, 1).astype(np.float32)
    large = 16 + (np.log(nf / np.float32(16)) / np.float32(math.log(128 / 16)) * np.float32(16)).astype(np.int32)
    large = np.minimum(large, 31)
    return np.where(n < 16, n, large)


def nsa_host(inputs):
    import ml_dtypes
    f = lambda k: np.ascontiguousarray(np.asarray(inputs[k], dtype=np.float32))
    tbl = f("rel_bias_table")
    out = {}
    k = np.arange(128)[:, None]
    c = np.arange(256)[None, :]
    d_s = c - k
    c = np.arange(384)[None, :]
    d_w = c - k
    t = np.arange(SEQ)[None, :]
    d_c = t - (16 * k + 31)
    raw = np.zeros((128, 8, NBT), np.float32)
    for hq in range(8):
        col = tbl[:, hq]
        raw[:, hq, OFF_BS:OFF_BS + 256] = np.where(d_s >= 0, col[t5_bucket_np(d_s)], np.float32(NEG))
        raw[:, hq, OFF_BW:OFF_BW + 384] = np.where((d_w >= 0) & (d_w < 256), col[t5_bucket_np(d_w)], np.float32(NEG))
        raw[:, hq, OFF_BC:OFF_BC + SEQ] = np.where((d_c >= 0) & (k < 127), col[t5_bucket_np(d_c)], np.float32(NEG))
    out["btab_raw"] = raw
    out["cvec"] = np.ascontiguousarray(np.broadcast_to(tbl[31][None, :], (128, 8)))
    ex = np.zeros((32, NT, 128), np.float32)
    for kt in range(NT):
        for kk in range(128):
            ex[(kt * 128 + kk) // 64, kt, kk] = 1.0
    out["expand"] = ex.astype(ml_dtypes.bfloat16)
    cs = np.arange(128) * 16
    ss = np.arange(32) * 64
    ov = ((cs[:, None] < ss[None, :] + 64) & (ss[None, :] < cs[:, None] + 32)).astype(np.float32)
    ov[127] = 0.0
    out["vca_const"] = np.concatenate([np.ones((128, 1), np.float32), ov], axis=1).astype(ml_dtypes.bfloat16)
    tt = np.arange(SEQ)
    cur = (tt // 64)[:, None]
    ids = np.arange(32)[None, :]
    forced = (ids == 0) | (ids == cur) | (ids == cur - 1)
    avail = ids * 64 <= tt[:, None]
    cm = np.where(avail, np.where(forced, 1e4, 0.0), -1e9).astype(np.float32)
    out["cmask"] = np.ascontiguousarray(cm.reshape(NT, 128, 32).transpose(1, 0, 2))
    L = DEPTH
    w1 = np.stack([f("nsa_w1_k"), f("nsa_w1_v")], axis=1)
    w1r = w1.reshape(L, 2, 32, 64, 64).transpose(0, 1, 3, 2, 4)
    w1z = np.zeros((L, 2, 2, 128, 32, 64), np.float32)
    w1z[:, :, 0, 0:64] = w1r
    w1z[:, :, 1, 64:128] = w1r
    out["nsa_w1"] = w1z
    pe = np.stack([f("nsa_pe_k"), f("nsa_pe_v")], axis=1)
    peT = pe.transpose(0, 3, 1, 2)
    out["nsa_peT"] = np.ascontiguousarray(np.concatenate([peT, np.zeros_like(peT)], axis=1))
    w2k = f("nsa_w2_k")
    w2p = np.zeros((L, 64, 2, 128), np.float32)
    w2p[:, :, 0, 0:64] = w2k
    w2p[:, :, 1, 64:128] = w2k
    out["nsa_w2pad"] = w2p
    out["nsa_w2v"] = f("nsa_w2_v")
    return out


_CACHE = {}


def kernel(**inputs):
    x = np.ascontiguousarray(np.asarray(inputs["x"], dtype=np.float32))
    com = host_layout(inputs)
    if "nc" not in _CACHE:
        _CACHE["nc"] = build_program()[0]
    nc = _CACHE["nc"]
    in_maps = []
    for c in range(8):
        m = dict(com)
        m["x"] = np.ascontiguousarray(x[c * NSEQ:(c + 1) * NSEQ])
        in_maps.append(m)
    res = run_bass_kernel_spmd(nc, in_maps, core_ids=list(range(8)))
    return np.concatenate([r["out"] for r in res.results], axis=0).astype(np.float32)
```
